# Optimizing a Trainium2 kernel written in Bass

```python
import math
import jax, jax.numpy as jnp
from jax import lax
import numpy as np

D_MODEL = 1024
BATCH = 8
SEQ = 2048
DEPTH = 2

CHUNK = 64
N_A = DEPTH // 2
N_B = DEPTH - N_A
GMLP_BLOCK = 128
GATE_DIM = 2 * D_MODEL
A_GROUPS = 8
A_GROUP_DIM = GATE_DIM // A_GROUPS
B_HEADS = 8
QK_NOPE = 128
QK_ROPE = 64
V_HEAD = 128
Q_LORA = 384
KV_LORA = 256
ROPE_THETA = 10000.0
Q_BLOCK = 128
D_FF = 4 * D_MODEL
EPS = 1e-6

kernel_name = "yoco_gmlp_mla_sqrelu_trunk"


def rmsnorm(x, g):
    xf = x.astype(jnp.float32)
    y = xf * lax.rsqrt(jnp.mean(xf * xf, axis=-1, keepdims=True) + EPS)
    return (y * g.astype(jnp.float32)).astype(x.dtype)


def layernorm(x, g, b):
    xf = x.astype(jnp.float32)
    mu = jnp.mean(xf, axis=-1, keepdims=True)
    var = jnp.mean(jnp.square(xf - mu), axis=-1, keepdims=True)
    y = (xf - mu) * lax.rsqrt(var + EPS)
    return (y * g.astype(jnp.float32) + b.astype(jnp.float32)).astype(x.dtype)


def rope_angles(positions):
    inv_freq = ROPE_THETA ** (-jnp.arange(0, QK_ROPE, 2, dtype=jnp.float32) / QK_ROPE)
    return positions.astype(jnp.float32)[..., None] * inv_freq


def apply_rope(x, ang):
    xf = x.astype(jnp.float32)
    x1, x2 = jnp.split(xf, 2, axis=-1)
    c, s = jnp.cos(ang), jnp.sin(ang)
    return jnp.concatenate([x1 * c - x2 * s, x2 * c + x1 * s], axis=-1).astype(x.dtype)


def chunk_mask(q_idx, k_idx):
    return (k_idx[None, :] // CHUNK) <= (q_idx[:, None] // CHUNK)


def gmlp_mixer(hn, w_in, ln_g, ln_b, w_s, b_s, w_out):
    B, S, _ = hn.shape
    z = jax.nn.gelu(hn @ w_in, approximate=False)
    u, v = jnp.split(z, 2, axis=-1)
    v = layernorm(v, ln_g, ln_b)
    nb = S // GMLP_BLOCK
    vb = v.reshape(B, nb, GMLP_BLOCK, A_GROUPS, A_GROUP_DIM)
    idx = jnp.arange(GMLP_BLOCK)
    ws = jnp.where(chunk_mask(idx, idx)[None], w_s, jnp.zeros_like(w_s))
    sv = jnp.einsum('gij,bnjgc->bnigc', ws, vb) + b_s.T[None, None, :, :, None]
    return (u * sv.reshape(B, S, GATE_DIM)) @ w_out


def shared_latent_kv(h, positions, src_g, w_kv_a, kv_a_g, w_kv_b):
    B, S, _ = h.shape
    hn = rmsnorm(h, src_g)
    ckv = hn @ w_kv_a
    c_kv, k_pe = ckv[..., :KV_LORA], ckv[..., KV_LORA:]
    c_kv = rmsnorm(c_kv, kv_a_g)
    kv = (c_kv @ w_kv_b).reshape(B, S, B_HEADS, QK_NOPE + V_HEAD)
    k_nope, v = kv[..., :QK_NOPE], kv[..., QK_NOPE:]
    k_pe = apply_rope(k_pe, rope_angles(positions))
    return k_nope, k_pe, v


def chunk_causal_mla_attention(q_nope, q_pe, k_nope, k_pe, v):
    B, S, H, _ = q_nope.shape
    nb = S // Q_BLOCK
    qn = q_nope.reshape(B, nb, Q_BLOCK, H, QK_NOPE).transpose(1, 0, 2, 3, 4)
    qp = q_pe.reshape(B, nb, Q_BLOCK, H, QK_ROPE).transpose(1, 0, 2, 3, 4)
    k_idx = jnp.arange(S)
    scale = (QK_NOPE + QK_ROPE) ** -0.5

    def one_block(args):
        qn_b, qp_b, i = args
        s = (jnp.einsum('bqhd,bkhd->bhqk', qn_b, k_nope).astype(jnp.float32)
             + jnp.einsum('bqhd,bkd->bhqk', qp_b, k_pe).astype(jnp.float32)) * scale
        q_idx = i * Q_BLOCK + jnp.arange(Q_BLOCK)
        s = jnp.where(chunk_mask(q_idx, k_idx)[None, None], s, jnp.finfo(jnp.float32).min)
        p = jax.nn.softmax(s, axis=-1).astype(v.dtype)
        return jnp.einsum('bhqk,bkhd->bqhd', p, v)

    out = lax.map(one_block, (qn, qp, jnp.arange(nb)))
    return out.transpose(1, 0, 2, 3, 4).reshape(B, S, H, V_HEAD)


def mla_mixer(hn, kv, positions, w_q_a, q_g, w_q_b, w_o):
    B, S, _ = hn.shape
    k_nope, k_pe, v = kv
    cq = rmsnorm(hn @ w_q_a, q_g)
    q = (cq @ w_q_b).reshape(B, S, B_HEADS, QK_NOPE + QK_ROPE)
    q_nope = q[..., :QK_NOPE]
    q_pe = apply_rope(q[..., QK_NOPE:], rope_angles(positions)[:, :, None, :])
    o = chunk_causal_mla_attention(q_nope, q_pe, k_nope, k_pe, v)
    return o.reshape(B, S, B_HEADS * V_HEAD) @ w_o


def sq_relu_mlp(hn, w1, w2):
    return jnp.square(jax.nn.relu(hn @ w1)) @ w2


def setup_inputs(seed: int = 0) -> dict:
    key = jax.random.key(seed)
    ks = jax.random.split(key, 24)

    def nrm(k, shape, fan_in, mult=1.0):
        return jax.random.normal(k, shape, jnp.float32) * (mult * fan_in ** -0.5)

    def gain(k, shape):
        return 1.0 + 0.05 * jax.random.normal(k, shape, jnp.float32)

    x = jax.random.normal(ks[0], (BATCH, SEQ, D_MODEL), jnp.float32)
    offset = jax.random.randint(ks[1], (BATCH, 1), 0, 4096, dtype=jnp.int32)
    positions = offset + jnp.arange(SEQ, dtype=jnp.int32)[None, :]
    return {
        "x": x,
        "positions": positions,
        "norm_mix_g": gain(ks[2], (DEPTH, D_MODEL)),
        "norm_mlp_g": gain(ks[3], (DEPTH, D_MODEL)),
        "a_w_in": nrm(ks[4], (N_A, D_MODEL, 2 * GATE_DIM), D_MODEL),
        "a_ln_v_g": gain(ks[5], (N_A, GATE_DIM)),
        "a_ln_v_b": 0.02 * jax.random.normal(ks[6], (N_A, GATE_DIM), jnp.float32),
        "a_w_s": nrm(ks[7], (N_A, A_GROUPS, GMLP_BLOCK, GMLP_BLOCK), GMLP_BLOCK, 0.5),
        "a_b_s": 1.0 + 0.1 * jax.random.normal(ks[8], (N_A, A_GROUPS, GMLP_BLOCK), jnp.float32),
        "a_w_out": nrm(ks[9], (N_A, GATE_DIM, D_MODEL), GATE_DIM),
        "b_w_q_a": nrm(ks[10], (N_B, D_MODEL, Q_LORA), D_MODEL),
        "b_q_norm_g": gain(ks[11], (N_B, Q_LORA)),
        "b_w_q_b": nrm(ks[12], (N_B, Q_LORA, B_HEADS * (QK_NOPE + QK_ROPE)), Q_LORA),
        "b_w_o": nrm(ks[13], (N_B, B_HEADS * V_HEAD, D_MODEL), B_HEADS * V_HEAD),
        "kv_src_norm_g": gain(ks[14], (D_MODEL,)),
        "kv_w_a": nrm(ks[15], (D_MODEL, KV_LORA + QK_ROPE), D_MODEL),
        "kv_a_norm_g": gain(ks[16], (KV_LORA,)),
        "kv_w_b": nrm(ks[17], (KV_LORA, B_HEADS * (QK_NOPE + V_HEAD)), KV_LORA),
        "mlp_w1": nrm(ks[18], (DEPTH, D_MODEL, D_FF), D_MODEL),
        "mlp_w2": nrm(ks[19], (DEPTH, D_FF, D_MODEL), D_FF, 0.5),
        "final_norm_g": gain(ks[20], (D_MODEL,)),
    }


def reference(x, positions, norm_mix_g, norm_mlp_g, a_w_in, a_ln_v_g, a_ln_v_b, a_w_s,
              a_b_s, a_w_out, b_w_q_a, b_q_norm_g, b_w_q_b, b_w_o, kv_src_norm_g, kv_w_a,
              kv_a_norm_g, kv_w_b, mlp_w1, mlp_w2, final_norm_g):
    h = x
    kv = None
    for l in range(DEPTH):
        hn = rmsnorm(h, norm_mix_g[l])
        if l < N_A:
            h = h + gmlp_mixer(hn, a_w_in[l], a_ln_v_g[l], a_ln_v_b[l], a_w_s[l],
                               a_b_s[l], a_w_out[l])
        else:
            j = l - N_A
            h = h + mla_mixer(hn, kv, positions, b_w_q_a[j], b_q_norm_g[j], b_w_q_b[j], b_w_o[j])
        h = h + sq_relu_mlp(rmsnorm(h, norm_mlp_g[l]), mlp_w1[l], mlp_w2[l])
        if l == N_A - 1:
            kv = shared_latent_kv(h, positions, kv_src_norm_g, kv_w_a, kv_a_norm_g, kv_w_b)
    return rmsnorm(h, final_norm_g)
```

```python
import contextlib
import numpy as np
import concourse.bass as bass
import concourse.mybir as mybir
from concourse.bass_utils import run_bass_kernel_spmd

F32 = mybir.dt.float32
BF16 = mybir.dt.bfloat16
I32 = mybir.dt.int32
AF = mybir.ActivationFunctionType
ALU = mybir.AluOpType

D = 1024
T = 2048
KC = 8
NTG = 4
EPS = 1e-6
GATE = 2048
DFF = 4096
QL = 384
KVL = 256
NH = 8
SCALE = 192.0 ** -0.5
PI = float(np.pi)
TWO_PI = float(2 * np.pi)

ENGS = ("pe", "act", "dve", "pool", "sp")

C_MIX0, C_MLP0, C_KVSRC, C_MIX1, C_MLP1, C_FIN = 0, 8, 16, 24, 32, 40
C_QG, C_KVAG, C_LNG, C_FREQ, C_LNB, NCOL = 48, 51, 53, 69, 70, 86


class Op:
    __slots__ = ("eng", "fn", "deps", "dma_sem", "token", "needed")

    def __init__(self, eng, fn, deps, dma_sem):
        self.eng = eng
        self.fn = fn
        self.deps = deps
        self.dma_sem = dma_sem
        self.token = None
        self.needed = False


class Prog:
    def __init__(self, nc):
        self.nc = nc
        self.ops = {e: [] for e in ENGS}
        self.last_writer = {}
        self.readers = {}
        self.dma_counts = {}
        self.pending_barrier = {}

    def add(self, eng, fn, reads=(), writes=(), dma=None, dma_val=None):
        deps = []
        for r in reads:
            w = self.last_writer.get(r)
            if w is not None:
                deps.append(w)
        for w_ in writes:
            rd = self.readers.get(w_)
            if rd:
                deps.extend(rd.values())
            w = self.last_writer.get(w_)
            if w is not None:
                deps.append(w)
        pb = self.pending_barrier.pop(eng, None)
        if pb:
            deps.extend(pb)
        op = Op(eng, fn, deps, dma)
        if dma is not None:
            c = self.dma_counts.get(dma, 0) + 16
            self.dma_counts[dma] = c
            op.token = (("dma", dma), c if dma_val is None else dma_val)
        self.ops[eng].append(op)
        for r in reads:
            d = self.readers.setdefault(r, {})
            d[eng if dma is None else ("dma", dma)] = op
        for w_ in writes:
            self.last_writer[w_] = op
            self.readers[w_] = {}
        return op

    def barrier(self):
        tails = []
        for e in ENGS:
            last_c = None
            last_d = {}
            for op in self.ops[e]:
                if op.dma_sem is None:
                    last_c = op
                else:
                    last_d[op.dma_sem] = op
            if last_c is not None:
                tails.append(last_c)
            tails.extend(last_d.values())
        for e in ENGS:
            self.pending_barrier[e] = list(tails)
        self.last_writer = {}
        self.readers = {}

    def emit(self):
        nc = self.nc
        for e in ENGS:
            for op in self.ops[e]:
                for d in op.deps:
                    if d.dma_sem is None:
                        if d.eng == "pe" and op.eng == "pe" and op.dma_sem is None:
                            continue
                        d.needed = True
        for e in ENGS:
            c = 0
            for op in self.ops[e]:
                if op.dma_sem is None and op.needed:
                    c += 1
                    op.token = (("eng", e), c)
        with contextlib.ExitStack() as st:
            sems = {}
            for e in ENGS:
                sems[("eng", e)] = st.enter_context(nc.semaphore("s_" + e))
            for name in self.dma_counts:
                sems[("dma", name)] = st.enter_context(nc.semaphore("d_" + str(name)))
            block = st.enter_context(nc.Block())
            engobj = {"pe": block.tensor, "act": block.scalar, "dve": block.vector,
                      "pool": block.gpsimd, "sp": block.sync}

            def make(e):
                def body(eng):
                    seen = {}
                    for op in self.ops[e]:
                        need = {}
                        for d in op.deps:
                            if d.token is None:
                                continue
                            if (d.dma_sem is None and d.eng == "pe" and e == "pe"
                                    and op.dma_sem is None):
                                continue
                            k, v = d.token
                            if v > need.get(k, 0):
                                need[k] = v
                        for k, v in need.items():
                            if seen.get(k, 0) >= v:
                                continue
                            seen[k] = v
                            eng.wait_ge(sems[k], v)
                        ins = op.fn(eng)
                        if op.dma_sem is not None:
                            ins.then_inc(sems[("dma", op.dma_sem)], 16)
                        elif op.needed:
                            ins.then_inc(sems[op.token[0]], 1)
                    fin = {}
                    for op in self.ops[e]:
                        if op.dma_sem is not None:
                            k, v = op.token
                            fin[k] = max(fin.get(k, 0), v)
                    for k, v in fin.items():
                        eng.wait_ge(sems[k], v)
                return body

            for e in ENGS:
                if self.ops[e]:
                    engobj[e](make(e))


class Arena:
    def __init__(self, nc, nbytes):
        self.t = nc.alloc_sbuf_tensor("arena", [128, nbytes // 4], F32).ap()
        self.nbytes = nbytes
        self.off = 0
        self.peak = 0

    def alloc(self, shape, dtype):
        esz = 2 if dtype == BF16 else 4
        n = 1
        for s in shape:
            n *= s
        nb = (n * esz + 31) // 32 * 32
        assert self.off + nb <= self.nbytes, ("SBUF arena overflow", self.off, nb, self.nbytes)
        a = self.t[:, self.off // 4:(self.off + nb) // 4]
        if dtype != F32:
            a = a.bitcast(dtype)
        a = a[:, 0:n]
        if len(shape) == 2:
            a = a.rearrange("p (a b) -> p a b", a=shape[0])
        elif len(shape) == 3:
            a = a.rearrange("p (a b c) -> p a b c", a=shape[0], b=shape[1])
        self.off += nb
        self.peak = max(self.peak, self.off)
        return a

    def mark(self):
        return self.off

    def reset(self, m):
        self.off = m


class WStream:
    def __init__(self, P, bufs, plan=None):
        self.P = P
        self.bufs = bufs
        self.nb = len(bufs)
        self.plan = plan
        self.reqs = []
        self.issued = 0

    def _issue(self, i):
        src, shape = self.plan[i]
        b = i % self.nb
        dst = self.view(b, shape)
        self.P.add("pool", lambda e, dst=dst, src=src: e.dma_start(out=dst, in_=src),
                   writes=[("wbuf", b)], dma=("w", b))

    def view(self, b, shape):
        n = 1
        for s in shape:
            n *= s
        a = self.bufs[b][:, 0:n]
        if len(shape) == 2:
            a = a.rearrange("p (a b) -> p a b", a=shape[0])
        return a

    def get(self, src, shape):
        i = len(self.reqs)
        self.reqs.append((src, shape))
        if self.plan is not None:
            while self.issued < min(len(self.plan), i + self.nb - 1):
                self._issue(self.issued)
                self.issued += 1
        b = i % self.nb
        return self.view(b, shape), ("wbuf", b)


class _Stop(Exception):
    pass


def build_program(debug=False, stop=None):
    nc = bass.Bass("TRN2", target_bir_lowering=False)
    dr = {}

    def din(name, shape, dt=F32):
        dr[name] = nc.dram_tensor(name, list(shape), dt, kind="ExternalInput").ap()
        return dr[name]

    xT = din("xT", [D, T])
    posb = din("posb", [64, T], I32)
    pcol_d = din("pcol", [128, NCOL])
    bsb_d = din("bsb", [128, 8, 128])
    wsT_d = din("wsT", [128, 8, 128])
    a_w_in = din("a_w_in", [D, 4096])
    a_w_out = din("a_w_out", [GATE, D])
    w_q_a = din("b_w_q_a", [D, QL])
    w_q_b = din("b_w_q_b", [QL, 1536])
    w_o = din("b_w_o", [D, D])
    kv_w_a = din("kv_w_a", [D, 320])
    kv_w_b = din("kv_w_b", [KVL, 2048])
    mlp_w1 = din("mlp_w1", [2, D, DFF])
    mlp_w2 = din("mlp_w2", [2, DFF, D])
    outT = nc.dram_tensor("outT", [D, T], F32, kind="ExternalOutput").ap()
    dbg = None
    if debug:
        dbg = nc.dram_tensor("dbg", [6, D, T], F32, kind="ExternalOutput").ap()

    ARENA_BYTES = 206 * 1024
    arena = Arena(nc, ARENA_BYTES)
    psum = [nc.alloc_psum_tensor("ps%d" % i, [128, 512], F32).ap() for i in range(8)]

    def emit_all(P, plan):
        holder = {}
        try:
            _emit_body(P, plan, holder)
        except _Stop:
            pass
        return holder["W"]

    def _emit_body(P, plan, holder):
        arena.off = 0
        hT = arena.alloc([KC, T], F32)
        pcol = arena.alloc([NCOL], F32)
        ones_bf = arena.alloc([128], BF16)
        wbufs = [arena.alloc([4096], BF16) for _ in range(4)]
        W = WStream(P, wbufs, plan)
        holder["W"] = W
        rstd = [arena.alloc([512], F32) for _ in range(2)]
        tmpf = [arena.alloc([512], F32) for _ in range(3)]
        pi_col = arena.alloc([8], F32)
        base_mark = arena.mark()
        cos2 = arena.alloc([T], F32)
        sin2 = arena.alloc([T], F32)
        rope_mark = arena.mark()
        cur = {}

        def col(c0, k):
            return pcol[:, c0 + k:c0 + k + 1]

        state = {"ps": 0, "rstd": 0}

        def ps_next(banks=range(8)):
            banks = list(banks)
            b = banks[state["ps"] % len(banks)]
            state["ps"] += 1
            return psum[b], ("ps", b)

        def mm_group(out_ap, pairs, reads, psreg):
            def fn(e, pairs=pairs, out_ap=out_ap):
                n = len(pairs)
                ins = None
                for i, (l, r) in enumerate(pairs):
                    ins = e.matmul(out_ap, lhsT=l, rhs=r, start=(i == 0), stop=(i == n - 1))
                return ins
            P.add("pe", fn, reads=reads, writes=[psreg])

        P.add("sp", lambda e: e.dma_start(out=pcol, in_=pcol_d), writes=["pcol"], dma="pcol")
        for kc in range(KC):
            for half in range(2):
                ts = slice(half * 1024, (half + 1) * 1024)
                P.add("sp", lambda e, kc=kc, ts=ts: e.dma_start(
                    out=hT[:, kc, ts], in_=xT[kc * 128:(kc + 1) * 128, ts]),
                    writes=[("hT", kc, half * 2), ("hT", kc, half * 2 + 1)], dma="x", dma_val=16 * 16)
        P.add("dve", lambda e: e.memset(ones_bf, 1.0), writes=["ones"])
        P.add("dve", lambda e: e.memset(pi_col, PI), writes=["pi_col"])

        def stop_here(name):
            if stop == name:
                for kc in range(KC):
                    P.add("sp", lambda e, kc=kc: e.dma_start(out=outT[kc * 128:(kc + 1) * 128, :],
                                                             in_=hT[:, kc, :]),
                          reads=[("hT", kc, tg) for tg in range(NTG)], dma="outstop", dma_val=16 * 8)
                raise _Stop()

        def dump(slot):
            if not debug:
                return
            for kc in range(KC):
                P.add("sp", lambda e, kc=kc: e.dma_start(out=dbg[slot, kc * 128:(kc + 1) * 128, :],
                                                         in_=hT[:, kc, :]),
                      reads=[("hT", kc, tg) for tg in range(NTG)], dma=("dbg", slot), dma_val=16 * 8)

        def rmsnorm_tg(src_fn, nchunks, dim, gcol0, dst_fn, tg, src_regs, dst_regs,
                       banks=range(8)):
            sq = cur["sq"]
            if nchunks == KC and src_fn is None:
                P.add("act", lambda e, tg=tg, sq=sq: e.activation(
                    out=sq, in_=hT[:, :, tg * 512:(tg + 1) * 512], func=AF.Square),
                    reads=src_regs, writes=[("sq", k) for k in range(KC)])
            else:
                for k in range(nchunks):
                    P.add("act", lambda e, k=k, sq=sq: e.activation(out=sq[:, k, :], in_=src_fn(k),
                                                                    func=AF.Square),
                          reads=src_regs, writes=[("sq", k)])
            ps, psreg = ps_next(banks)
            mm_group(ps, [(ones_bf, sq[:, k, :]) for k in range(nchunks)],
                     ["ones"] + [("sq", k) for k in range(nchunks)], psreg)
            ri = state["rstd"] % 2
            state["rstd"] += 1
            r = rstd[ri]
            rreg = ("rstd", ri)
            P.add("dve", lambda e, r=r, ps=ps: e.tensor_scalar(
                out=r, in0=ps, scalar1=1.0 / dim, scalar2=EPS, op0=ALU.mult, op1=ALU.add),
                reads=[psreg], writes=[rreg, psreg])
            P.add("act", lambda e, r=r: e.activation(out=r, in_=r, func=AF.Sqrt),
                  reads=[rreg], writes=[rreg])
            P.add("dve", lambda e, r=r: e.reciprocal(out=r, in_=r), reads=[rreg], writes=[rreg])
            for k in range(nchunks):
                s_ap = hT[:, k, tg * 512:(tg + 1) * 512] if src_fn is None else src_fn(k)
                P.add("dve", lambda e, k=k, s_ap=s_ap, r=r: e.scalar_tensor_tensor(
                    out=dst_fn(k), in0=s_ap, scalar=col(gcol0, k), in1=r,
                    op0=ALU.mult, op1=ALU.mult),
                    reads=src_regs + [rreg, "pcol"], writes=[dst_regs[k]])

        pos_i = arena.alloc([T], I32)
        ang = arena.alloc([T], F32)
        tmpA = arena.alloc([T], F32)
        P.add("sp", lambda e: e.dma_start(out=pos_i[0:64, :], in_=posb), writes=["pos_i"], dma="pos")
        P.add("dve", lambda e: e.tensor_copy(out=ang[0:64, :], in_=pos_i[0:64, :]),
              reads=["pos_i"], writes=["ang"])
        P.add("dve", lambda e: e.tensor_scalar(out=ang[0:64, :], in0=ang[0:64, :],
                                               scalar1=pcol[0:64, C_FREQ:C_FREQ + 1], scalar2=None,
                                               op0=ALU.mult),
              reads=["ang", "pcol"], writes=["ang"])
        C1 = 6.28125
        C2 = TWO_PI - 6.28125

        def sin_table(dst, shift, dname):
            P.add("dve", lambda e: e.tensor_scalar(out=tmpA[0:64, :], in0=ang[0:64, :], scalar1=shift,
                                                   scalar2=1.0 / TWO_PI, op0=ALU.add, op1=ALU.mult),
                  reads=["ang"], writes=["tmpA"])
            P.add("dve", lambda e: e.tensor_copy(out=pos_i[0:64, :], in_=tmpA[0:64, :]),
                  reads=["tmpA"], writes=["pos_i"])
            P.add("dve", lambda e: e.tensor_copy(out=tmpA[0:64, :], in_=pos_i[0:64, :]),
                  reads=["pos_i"], writes=["tmpA"])
            P.add("dve", lambda e: e.scalar_tensor_tensor(out=dst[0:64, :], in0=tmpA[0:64, :], scalar=-C1,
                                                          in1=ang[0:64, :], op0=ALU.mult, op1=ALU.add),
                  reads=["tmpA", "ang"], writes=[dname])
            P.add("dve", lambda e: e.scalar_tensor_tensor(out=dst[0:64, :], in0=tmpA[0:64, :], scalar=-C2,
                                                          in1=dst[0:64, :], op0=ALU.mult, op1=ALU.add),
                  reads=["tmpA", dname], writes=[dname])
            if shift:
                P.add("dve", lambda e: e.tensor_scalar(out=dst[0:64, :], in0=dst[0:64, :], scalar1=shift,
                                                       scalar2=None, op0=ALU.add),
                      reads=[dname], writes=[dname])
            P.add("dve", lambda e: e.tensor_scalar(out=tmpA[0:64, :], in0=dst[0:64, :], scalar1=PI,
                                                   scalar2=TWO_PI, op0=ALU.is_gt, op1=ALU.mult),
                  reads=[dname], writes=["tmpA"])
            P.add("dve", lambda e: e.tensor_tensor(out=dst[0:64, :], in0=dst[0:64, :], in1=tmpA[0:64, :],
                                                   op=ALU.subtract),
                  reads=[dname, "tmpA"], writes=[dname])
            P.add("act", lambda e: e.activation(out=dst[0:64, :], in_=dst[0:64, :], func=AF.Sin),
                  reads=[dname], writes=[dname])

        sin_table(sin2, 0.0, "sin2")
        sin_table(cos2, PI / 2, "cos2")
        P.add("dve", lambda e: e.tensor_scalar(out=sin2[0:32, :], in0=sin2[0:32, :], scalar1=-1.0,
                                               scalar2=None, op0=ALU.mult),
              reads=["sin2"], writes=["sin2"])
        if debug:
            P.add("sp", lambda e: e.dma_start(out=dbg[4, 0:64, :], in_=cos2[0:64, :]), reads=["cos2"], dma="dbgc")
            P.add("sp", lambda e: e.dma_start(out=dbg[5, 0:64, :], in_=sin2[0:64, :]), reads=["sin2"], dma="dbgs")
        stop_here("rope")
        P.barrier()
        arena.reset(rope_mark)

        cur["sq"] = arena.alloc([KC, 512], BF16)
        hn1 = arena.alloc([KC, 512], BF16)
        v_bf = arena.alloc([4, GATE], BF16)
        gateT = arena.alloc([16, 512], BF16)
        uT_sb = [arena.alloc([512], BF16) for _ in range(2)]
        B_T = arena.alloc([16, 128], F32)
        wsT_bf = arena.alloc([8, 128], BF16)
        stats = arena.alloc([4, 6], F32)
        mv = arena.alloc([4, 2], F32)
        nmr = arena.alloc([4, 2], F32)
        m1b = arena.mark()
        wsT_f = arena.alloc([8, 128], F32)
        rs_bc = arena.alloc([8, 128], F32)
        bs_bc = arena.alloc([8, 128], F32)

        P.add("sp", lambda e: e.dma_start(out=wsT_f, in_=wsT_d), writes=["wsT_f"], dma="wsT")
        P.add("sp", lambda e: e.dma_start(out=bs_bc, in_=bsb_d), writes=["bs_bc"], dma="bsb")
        P.add("dve", lambda e: e.memset(wsT_f[64:128, :, 0:64], 0.0), reads=["wsT_f"], writes=["wsT_f"])
        P.add("dve", lambda e: e.tensor_copy(out=wsT_bf, in_=wsT_f), reads=["wsT_f"], writes=["wsT_bf"])
        for half in range(2):
            ps, psreg = ps_next()
            mm_group(ps, [(ones_bf, wsT_bf[:, half * 4:(half + 1) * 4, :])], ["ones", "wsT_bf"], psreg)
            P.add("dve", lambda e, ps=ps, half=half: e.tensor_copy(
                out=rs_bc[:, half * 4:(half + 1) * 4, :], in_=ps.rearrange("p (a b) -> p a b", a=4)),
                reads=[psreg], writes=["rs_bc", psreg])
        for cc in range(16):
            g = cc // 2
            P.add("dve", lambda e, cc=cc, g=g: e.scalar_tensor_tensor(
                out=B_T[:, cc, :], in0=rs_bc[:, g, :], scalar=col(C_LNB, cc), in1=bs_bc[:, g, :],
                op0=ALU.mult, op1=ALU.add),
                reads=["rs_bc", "bs_bc", "pcol"], writes=["B_T"])
        stop_here("gsetup")

        w_in_v = a_w_in.rearrange("(k p) c -> p k c", p=128)
        w_out_v = a_w_out.rearrange("(k p) c -> p k c", p=128)
        for tp in range(4):
            hregs = [("hT", kc, tp) for kc in range(KC)]
            rmsnorm_tg(None, KC, D, C_MIX0, lambda k: hn1[:, k, :], tp, hregs,
                       [("hn1", k) for k in range(KC)])
            for n in range(4):
                wv, wreg = W.get(w_in_v[:, :, 2048 + n * 512:2048 + (n + 1) * 512], [KC, 512])
                for tt in range(4):
                    ps, psreg = ps_next()
                    mm_group(ps, [(hn1[:, k, tt * 128:(tt + 1) * 128], wv[:, k, :]) for k in range(KC)],
                             [("hn1", k) for k in range(KC)] + [wreg], psreg)
                    P.add("act", lambda e, ps=ps, tt=tt, n=n: e.activation(
                        out=v_bf[:, tt, n * 512:(n + 1) * 512], in_=ps, func=AF.Gelu),
                        reads=[psreg], writes=[("v", tt), psreg])
            if tp == 0:
                stop_here("gv")
            for tt in range(4):
                for n in range(4):
                    P.add("dve", lambda e, tt=tt, n=n: e.bn_stats(
                        out=stats[:, n, :], in_=v_bf[:, tt, n * 512:(n + 1) * 512]),
                        reads=[("v", tt)], writes=["stats"])
                P.add("dve", lambda e, tt=tt: e.bn_aggr(out=mv[:, tt, :], in_=stats.rearrange("p a b -> p (a b)")),
                      reads=["stats"], writes=[("mv", tt)])
                P.add("dve", lambda e, tt=tt: e.tensor_scalar(
                    out=nmr[:, tt, 1:2], in0=mv[:, tt, 1:2], scalar1=EPS, scalar2=None, op0=ALU.add),
                    reads=[("mv", tt)], writes=[("nmr", tt)])
                P.add("act", lambda e, tt=tt: e.activation(out=nmr[:, tt, 1:2], in_=nmr[:, tt, 1:2],
                                                           func=AF.Sqrt),
                      reads=[("nmr", tt)], writes=[("nmr", tt)])
                P.add("dve", lambda e, tt=tt: e.reciprocal(out=nmr[:, tt, 1:2], in_=nmr[:, tt, 1:2]),
                      reads=[("nmr", tt)], writes=[("nmr", tt)])
                P.add("dve", lambda e, tt=tt: e.tensor_scalar(
                    out=v_bf[:, tt, :], in0=v_bf[:, tt, :], scalar1=mv[:, tt, 0:1],
                    scalar2=nmr[:, tt, 1:2], op0=ALU.subtract, op1=ALU.mult),
                    reads=[("v", tt), ("mv", tt), ("nmr", tt)], writes=[("v", tt)])
            if tp == 0:
                stop_here("gln")
            for n in range(4):
                wu, wreg = W.get(w_in_v[:, :, n * 512:(n + 1) * 512], [KC, 512])
                for c4 in range(4):
                    cc = n * 4 + c4
                    g = cc // 2
                    ps_u, pr_u = ps_next()
                    mm_group(ps_u, [(wu[:, k, c4 * 128:(c4 + 1) * 128], hn1[:, k, :]) for k in range(KC)],
                             [("hn1", k) for k in range(KC)] + [wreg], pr_u)
                    ub = uT_sb[cc % 2]
                    ureg = ("uT", cc % 2)
                    P.add("act", lambda e, ps_u=ps_u, ub=ub: e.activation(out=ub, in_=ps_u, func=AF.Gelu),
                          reads=[pr_u], writes=[ureg, pr_u])
                    ps_s, pr_s = ps_next()

                    def fn(e, ps_s=ps_s, cc=cc, g=g):
                        ins = None
                        for tt in range(4):
                            ins = e.matmul(ps_s[:, tt * 128:(tt + 1) * 128],
                                           lhsT=v_bf[:, tt, cc * 128:(cc + 1) * 128],
                                           rhs=wsT_bf[:, g, :], start=True, stop=True)
                        return ins
                    P.add("pe", fn, reads=[("v", tt) for tt in range(4)] + ["wsT_bf"], writes=[pr_s])
                    tf = tmpf[cc % 2]
                    treg = ("tmpf", cc % 2)
                    P.add("dve", lambda e, ps_s=ps_s, tf=tf, cc=cc: e.scalar_tensor_tensor(
                        out=tf.rearrange("p (a b) -> p a b", a=4),
                        in0=ps_s.rearrange("p (a b) -> p a b", a=4),
                        scalar=col(C_LNG, cc),
                        in1=B_T[:, cc:cc + 1, :].broadcast_to([128, 4, 128]),
                        op0=ALU.mult, op1=ALU.add),
                        reads=[pr_s, "B_T", "pcol"], writes=[treg, pr_s])
                    P.add("dve", lambda e, tf=tf, ub=ub, cc=cc: e.tensor_tensor(
                        out=gateT[:, cc, :], in0=tf, in1=ub, op=ALU.mult),
                        reads=[treg, ureg], writes=[("gate", cc)])
            if tp == 0:
                stop_here("gu")
            for dq in range(4):
                wo, wreg = W.get(w_out_v[:, :, dq * 256:(dq + 1) * 256], [16, 256])
                for d2 in range(2):
                    dc = dq * 2 + d2
                    ps, psreg = ps_next()
                    mm_group(ps, [(wo[:, cc, d2 * 128:(d2 + 1) * 128], gateT[:, cc, :]) for cc in range(16)],
                             [("gate", cc) for cc in range(16)] + [wreg], psreg)
                    P.add("dve", lambda e, ps=ps, dc=dc, tp=tp: e.tensor_tensor(
                        out=hT[:, dc, tp * 512:(tp + 1) * 512], in0=ps,
                        in1=hT[:, dc, tp * 512:(tp + 1) * 512], op=ALU.add),
                        reads=[psreg, ("hT", dc, tp)], writes=[("hT", dc, tp), psreg])
        dump(0)
        stop_here("gmlp")
        P.barrier()
        arena.reset(rope_mark)

        def mlp(layer, gcol0):
            m = arena.mark()
            cur["sq"] = arena.alloc([KC, 512], BF16)
            hnT = arena.alloc([KC, T], BF16)
            hidT = arena.alloc([KC, T], BF16)
            w1v = mlp_w1[layer].rearrange("(k p) c -> p k c", p=128)
            w2v = mlp_w2[layer].rearrange("(k p) c -> p k c", p=128)
            for tg in range(NTG):
                rmsnorm_tg(None, KC, D, gcol0, lambda k, tg=tg: hnT[:, k, tg * 512:(tg + 1) * 512], tg,
                           [("hT", kc, tg) for kc in range(KC)],
                           [("hnT", k, tg) for k in range(KC)])
            ti = 0
            for fq in range(4):
                for n in range(2):
                    w1b, wreg = W.get(w1v[:, :, fq * 1024 + n * 512:fq * 1024 + (n + 1) * 512], [KC, 512])
                    for f4 in range(4):
                        fc = n * 4 + f4
                        for tg in range(NTG):
                            ps, psreg = ps_next()
                            mm_group(ps, [(w1b[:, k, f4 * 128:(f4 + 1) * 128],
                                           hnT[:, k, tg * 512:(tg + 1) * 512]) for k in range(KC)],
                                     [("hnT", k, tg) for k in range(KC)] + [wreg], psreg)
                            tf = tmpf[ti % 3]
                            treg = ("tmpf", ti % 3)
                            ti += 1
                            P.add("act", lambda e, ps=ps, tf=tf: e.activation(out=tf, in_=ps, func=AF.Relu),
                                  reads=[psreg], writes=[treg, psreg])
                            P.add("dve", lambda e, tf=tf, fc=fc, tg=tg: e.tensor_tensor(
                                out=hidT[:, fc, tg * 512:(tg + 1) * 512], in0=tf, in1=tf, op=ALU.mult),
                                reads=[treg], writes=[("hid", fc, tg)])
                for dh in range(2):
                    w2b, wreg = W.get(w2v[:, fq * 8:(fq + 1) * 8, dh * 512:(dh + 1) * 512], [8, 512])
                    for d4 in range(4):
                        dc = dh * 4 + d4
                        for tg in range(NTG):
                            ps, psreg = ps_next()
                            mm_group(ps, [(w2b[:, fc, d4 * 128:(d4 + 1) * 128],
                                           hidT[:, fc, tg * 512:(tg + 1) * 512]) for fc in range(8)],
                                     [("hid", fc, tg) for fc in range(8)] + [wreg], psreg)
                            P.add("dve", lambda e, ps=ps, dc=dc, tg=tg: e.tensor_tensor(
                                out=hT[:, dc, tg * 512:(tg + 1) * 512], in0=ps,
                                in1=hT[:, dc, tg * 512:(tg + 1) * 512], op=ALU.add),
                                reads=[psreg, ("hT", dc, tg)], writes=[("hT", dc, tg), psreg])
            P.barrier()
            arena.reset(m)

        mlp(0, C_MLP0)
        dump(1)
        stop_here("mlp0")

        ckvT = arena.alloc([2, T], BF16)
        kpeT = arena.alloc([T], BF16)
        cqT = arena.alloc([3, T], BF16)
        m3 = arena.mark()
        cur["sq"] = arena.alloc([KC, 512], BF16)
        hnT = arena.alloc([KC, T], BF16)
        ckv_raw = arena.alloc([3, 512], F32)
        kva_swp = arena.alloc([KC, 64], BF16)

        for tg in range(NTG):
            rmsnorm_tg(None, KC, D, C_KVSRC, lambda k, tg=tg: hnT[:, k, tg * 512:(tg + 1) * 512], tg,
                       [("hT", kc, tg) for kc in range(KC)], [("hnT", k, tg) for k in range(KC)])
        wa, wareg = W.get(kv_w_a.rearrange("(k p) c -> p k c", p=128), [KC, 320])
        P.add("dve", lambda e: e.tensor_copy(out=kva_swp[:, :, 0:32], in_=wa[:, :, 288:320]),
              reads=[wareg], writes=["kva_swp"])
        P.add("dve", lambda e: e.tensor_copy(out=kva_swp[:, :, 32:64], in_=wa[:, :, 256:288]),
              reads=[wareg, "kva_swp"], writes=["kva_swp"])

        def rope(ps_a, pr_a, ps_b, pr_b, dst, dreg, tsl):
            P.add("dve", lambda e: e.tensor_tensor(out=tmpf[0][0:64, :], in0=ps_a[0:64, :],
                                                   in1=cos2[0:64, tsl], op=ALU.mult),
                  reads=[pr_a, "cos2"], writes=[("tmpf", 0), pr_a])
            P.add("dve", lambda e: e.tensor_tensor(out=tmpf[1][0:64, :], in0=ps_b[0:64, :],
                                                   in1=sin2[0:64, tsl], op=ALU.mult),
                  reads=[pr_b, "sin2"], writes=[("tmpf", 1), pr_b])
            P.add("dve", lambda e: e.tensor_tensor(out=dst, in0=tmpf[0][0:64, :],
                                                   in1=tmpf[1][0:64, :], op=ALU.add),
                  reads=[("tmpf", 0), ("tmpf", 1)], writes=[dreg])

        for tg in range(NTG):
            tsl = slice(tg * 512, (tg + 1) * 512)
            hreads = [("hnT", k, tg) for k in range(KC)]
            for c2 in range(2):
                ps, psreg = ps_next()
                mm_group(ps, [(wa[:, k, c2 * 128:(c2 + 1) * 128], hnT[:, k, tsl]) for k in range(KC)],
                         hreads + [wareg], psreg)
                P.add("act", lambda e, ps=ps, c2=c2: e.activation(out=ckv_raw[:, c2, :], in_=ps, func=AF.Copy),
                      reads=[psreg], writes=[("ckv_raw", c2), psreg])
            rmsnorm_tg(lambda k: ckv_raw[:, k, :], 2, KVL, C_KVAG,
                       lambda k, tsl=tsl: ckvT[:, k, tsl], tg,
                       [("ckv_raw", 0), ("ckv_raw", 1)], [("ckvT", k, tg) for k in range(2)])
            ps_a, pr_a = ps_next()
            mm_group(ps_a[0:64, :], [(wa[:, k, 256:320], hnT[:, k, tsl]) for k in range(KC)],
                     hreads + [wareg], pr_a)
            ps_b, pr_b = ps_next()
            mm_group(ps_b[0:64, :], [(kva_swp[:, k, :], hnT[:, k, tsl]) for k in range(KC)],
                     hreads + ["kva_swp"], pr_b)
            rope(ps_a, pr_a, ps_b, pr_b, kpeT[0:64, tsl], ("kpeT", tg), tsl)

        stop_here("kv")
        for tg in range(NTG):
            rmsnorm_tg(None, KC, D, C_MIX1, lambda k, tg=tg: hnT[:, k, tg * 512:(tg + 1) * 512], tg,
                       [("hT", kc, tg) for kc in range(KC)], [("hnT", k, tg) for k in range(KC)])
        wqa, wqareg = W.get(w_q_a.rearrange("(k p) c -> p k c", p=128), [KC, QL])
        for tg in range(NTG):
            tsl = slice(tg * 512, (tg + 1) * 512)
            raws = [ckv_raw[:, 0, :], ckv_raw[:, 1, :], ckv_raw[:, 2, :]]
            rregs = [("ckv_raw", 0), ("ckv_raw", 1), ("ckv_raw", 2)]
            for c3 in range(3):
                ps, psreg = ps_next()
                mm_group(ps, [(wqa[:, k, c3 * 128:(c3 + 1) * 128], hnT[:, k, tsl]) for k in range(KC)],
                         [("hnT", k, tg) for k in range(KC)] + [wqareg], psreg)
                P.add("act", lambda e, ps=ps, c3=c3, raws=raws: e.activation(out=raws[c3], in_=ps, func=AF.Copy),
                      reads=[psreg], writes=[rregs[c3], psreg])
            rmsnorm_tg(lambda k, raws=raws: raws[k], 3, QL, C_QG,
                       lambda k, tsl=tsl: cqT[:, k, tsl], tg, rregs,
                       [("cqT", k, tg) for k in range(3)])
        P.barrier()
        arena.reset(m3)
        oT_all = arena.alloc([NH, T], BF16)
        kT_h = arena.alloc([T], BF16)
        v_h = arena.alloc([16, 128], BF16)
        qT_h = arena.alloc([T], BF16)
        qpe_h = arena.alloc([T], BF16)
        wq_swp = arena.alloc([3, 64], BF16)
        pT = [arena.alloc([512], BF16) for _ in range(4)]
        rcp = [arena.alloc([512], F32) for _ in range(2)]

        wqb_v = w_q_b.rearrange("(k p) c -> p k c", p=128)
        wkvb_v = kv_w_b.rearrange("(k p) c -> p k c", p=128)
        PROJ_BANKS = [7, 0, 1, 2]
        SC_BANKS = [0, 1, 2]
        cp_i = [0]

        def evac_copy(dst, ps, psreg, dreg):
            cp_i[0] += 1
            if cp_i[0] % 2 == 0:
                P.add("act", lambda e: e.activation(out=dst, in_=ps, func=AF.Copy),
                      reads=[psreg], writes=[dreg, psreg])
            else:
                P.add("dve", lambda e: e.tensor_copy(out=dst, in_=ps), reads=[psreg], writes=[dreg, psreg])

        for h in range(NH):
            wkb, wkbreg = W.get(wkvb_v[:, :, h * 256:(h + 1) * 256], [2, 256])
            wqb, wqbreg = W.get(wqb_v[:, :, h * 192:(h + 1) * 192], [3, 192])
            P.add("dve", lambda e, wqb=wqb: e.tensor_copy(out=wq_swp[:, :, 0:32], in_=wqb[:, :, 160:192]),
                  reads=[wqbreg], writes=["wq_swp"])
            P.add("dve", lambda e, wqb=wqb: e.tensor_copy(out=wq_swp[:, :, 32:64], in_=wqb[:, :, 128:160]),
                  reads=[wqbreg, "wq_swp"], writes=["wq_swp"])
            for tg in range(NTG):
                tsl = slice(tg * 512, (tg + 1) * 512)
                ckr = [("ckvT", k, tg) for k in range(2)]
                cqr = [("cqT", k, tg) for k in range(3)]
                ps, psreg = ps_next(PROJ_BANKS)
                mm_group(ps, [(wkb[:, k, 0:128], ckvT[:, k, tsl]) for k in range(2)], ckr + [wkbreg], psreg)
                evac_copy(kT_h[:, tsl], ps, psreg, ("kT_h", tg))
                ps, psreg = ps_next(PROJ_BANKS)

                def fn(e, ps=ps, tg=tg, wkb=wkb):
                    ins = None
                    for j in range(4):
                        kt = tg * 4 + j
                        for k in range(2):
                            ins = e.matmul(ps[:, j * 128:(j + 1) * 128],
                                           lhsT=ckvT[:, k, kt * 128:(kt + 1) * 128],
                                           rhs=wkb[:, k, 128:256], start=(k == 0), stop=(k == 1))
                    return ins
                P.add("pe", fn, reads=ckr + [wkbreg], writes=[psreg])
                evac_copy(v_h[:, tg * 4:(tg + 1) * 4, :], ps.rearrange("p (a b) -> p a b", a=4), psreg,
                          ("v_h", tg))
                ps, psreg = ps_next(PROJ_BANKS)
                mm_group(ps, [(wqb[:, k, 0:128], cqT[:, k, tsl]) for k in range(3)], cqr + [wqbreg], psreg)
                evac_copy(qT_h[:, tsl], ps, psreg, ("qT_h", tg))
                ps_a, pr_a = ps_next(PROJ_BANKS)
                mm_group(ps_a[0:64, :], [(wqb[:, k, 128:192], cqT[:, k, tsl]) for k in range(3)],
                         cqr + [wqbreg], pr_a)
                ps_b, pr_b = ps_next(PROJ_BANKS)
                mm_group(ps_b[0:64, :], [(wq_swp[:, k, :], cqT[:, k, tsl]) for k in range(3)],
                         cqr + ["wq_swp"], pr_b)
                rope(ps_a, pr_a, ps_b, pr_b, qpe_h[0:64, tsl], ("qpe_h", tg), tsl)

            tiles = [(qg, kt) for qg in range(NTG) for kt in range(4 * qg + 4)]
            sc = {}

            def emit_qk(i):
                qg, kt = tiles[i]
                m = max(0, kt - 4 * qg)
                c0 = 128 * m
                b = SC_BANKS[i % 3]
                ps, psreg = psum[b], ("ps", b)
                qsl = slice(qg * 512 + c0, (qg + 1) * 512)
                ksl = slice(kt * 128, (kt + 1) * 128)

                def fn(e, ps=ps, c0=c0, qsl=qsl, ksl=ksl):
                    e.matmul(ps[:, c0:512], lhsT=kT_h[:, ksl], rhs=qT_h[:, qsl], start=True, stop=False)
                    return e.matmul(ps[:, c0:512], lhsT=kpeT[0:64, ksl], rhs=qpe_h[0:64, qsl],
                                    start=False, stop=True)
                P.add("pe", fn, reads=[("kT_h", kt // 4), ("qT_h", qg), ("kpeT", kt // 4), ("qpe_h", qg)],
                      writes=[psreg])
                sc[i] = (ps, psreg, c0)

            def emit_rest(i):
                qg, kt = tiles[i]
                ps, psreg, c0 = sc.pop(i)
                pb = pT[i % 4]
                preg = ("pT", i % 4)
                P.add("act", lambda e, ps=ps, pb=pb, c0=c0: e.activation(
                    out=pb[:, c0:512], in_=ps[:, c0:512], func=AF.Exp, scale=SCALE),
                    reads=[psreg], writes=[preg, psreg])
                if kt >= 4 * qg:
                    P.add("pool", lambda e, pb=pb, c0=c0: e.memset(pb[64:128, c0:c0 + 64], 0.0),
                          reads=[preg], writes=[preg])
                first = (kt == 0)
                last = (kt == 4 * qg + 3)
                po, poreg = psum[3 + qg % 2], ("ps", 3 + qg % 2)
                pd, pdreg = psum[5 + qg % 2], ("ps", 5 + qg % 2)

                def fn(e, pb=pb, c0=c0, kt=kt, po=po, pd=pd, first=first, last=last):
                    e.matmul(po[:, c0:512], lhsT=v_h[:, kt, :], rhs=pb[:, c0:512], start=first, stop=last)
                    return e.matmul(pd[:, c0:512], lhsT=ones_bf, rhs=pb[:, c0:512], start=first, stop=last)
                P.add("pe", fn, reads=[preg, ("v_h", kt // 4), "ones"], writes=[poreg, pdreg])
                if last:
                    r = rcp[qg % 2]
                    rreg = ("rcp", qg % 2)
                    P.add("dve", lambda e, r=r, pd=pd: e.reciprocal(out=r, in_=pd),
                          reads=[pdreg], writes=[rreg, pdreg])
                    P.add("dve", lambda e, r=r, po=po, qg=qg, h=h: e.tensor_tensor(
                        out=oT_all[:, h, qg * 512:(qg + 1) * 512], in0=po, in1=r, op=ALU.mult),
                        reads=[poreg, rreg], writes=[("oT", h, qg), poreg])

            n = len(tiles)
            emit_qk(0)
            emit_qk(1)
            for i in range(n):
                if i + 2 < n:
                    emit_qk(i + 2)
                emit_rest(i)

        wov = w_o.rearrange("(k p) c -> p k c", p=128)
        for dh in range(2):
            wob, wreg = W.get(wov[:, :, dh * 512:(dh + 1) * 512], [NH, 512])
            for d4 in range(4):
                dc = dh * 4 + d4
                for tg in range(NTG):
                    ps, psreg = ps_next()
                    mm_group(ps, [(wob[:, hh, d4 * 128:(d4 + 1) * 128], oT_all[:, hh, tg * 512:(tg + 1) * 512])
                                  for hh in range(NH)],
                             [("oT", hh, tg) for hh in range(NH)] + [wreg], psreg)
                    P.add("dve", lambda e, ps=ps, dc=dc, tg=tg: e.tensor_tensor(
                        out=hT[:, dc, tg * 512:(tg + 1) * 512], in0=ps,
                        in1=hT[:, dc, tg * 512:(tg + 1) * 512], op=ALU.add),
                        reads=[psreg, ("hT", dc, tg)], writes=[("hT", dc, tg), psreg])
        dump(2)
        stop_here("mla")
        P.barrier()
        arena.reset(base_mark)

        mlp(1, C_MLP1)
        dump(3)
        stop_here("mlp1")

        cur["sq"] = arena.alloc([KC, 512], BF16)
        ob = [arena.alloc([KC, 512], F32) for _ in range(2)]
        for tg in range(NTG):
            o = ob[tg % 2]
            rmsnorm_tg(None, KC, D, C_FIN, lambda k, o=o: o[:, k, :], tg,
                       [("hT", kc, tg) for kc in range(KC)], [("ob", tg % 2, k) for k in range(KC)])
            for kc in range(KC):
                P.add("sp", lambda e, o=o, kc=kc, tg=tg: e.dma_start(
                    out=outT[kc * 128:(kc + 1) * 128, tg * 512:(tg + 1) * 512], in_=o[:, kc, :]),
                    reads=[("ob", tg % 2, kc)], dma=("out", tg % 2), dma_val=16 * 8 * (tg // 2 + 1))

    P0 = Prog(nc)
    W0 = emit_all(P0, None)
    plan = list(W0.reqs)
    P = Prog(nc)
    emit_all(P, plan)
    P.emit()
    return nc


_NC_CACHE = {}


def _pack_inputs(inputs):
    f32 = np.float32

    def colv(v):
        v = np.asarray(v, f32)
        return np.ascontiguousarray(v.reshape(-1, 128).T)

    pcol = np.zeros((128, NCOL), f32)
    pcol[:, C_MIX0:C_MIX0 + 8] = colv(inputs["norm_mix_g"][0])
    pcol[:, C_MLP0:C_MLP0 + 8] = colv(inputs["norm_mlp_g"][0])
    pcol[:, C_KVSRC:C_KVSRC + 8] = colv(inputs["kv_src_norm_g"])
    pcol[:, C_MIX1:C_MIX1 + 8] = colv(inputs["norm_mix_g"][1])
    pcol[:, C_MLP1:C_MLP1 + 8] = colv(inputs["norm_mlp_g"][1])
    pcol[:, C_FIN:C_FIN + 8] = colv(inputs["final_norm_g"])
    pcol[:, C_QG:C_QG + 3] = colv(inputs["b_q_norm_g"][0])
    pcol[:, C_KVAG:C_KVAG + 2] = colv(inputs["kv_a_norm_g"])
    pcol[:, C_LNG:C_LNG + 16] = colv(inputs["a_ln_v_g"][0])
    inv_freq = (np.float32(10000.0) ** (-np.arange(0, 64, 2, dtype=np.float32) / np.float32(64))).astype(f32)
    pcol[0:32, C_FREQ] = inv_freq
    pcol[32:64, C_FREQ] = inv_freq
    pcol[:, C_LNB:C_LNB + 16] = colv(inputs["a_ln_v_b"][0])
    bsb = np.ascontiguousarray(np.broadcast_to(np.asarray(inputs["a_b_s"][0], f32)[None], (128, 8, 128)))
    wsT = np.ascontiguousarray(np.transpose(np.asarray(inputs["a_w_s"][0], f32), (2, 0, 1)))
    shared = {
        "pcol": pcol, "bsb": bsb, "wsT": wsT,
        "a_w_in": np.ascontiguousarray(inputs["a_w_in"][0], dtype=f32),
        "a_w_out": np.ascontiguousarray(inputs["a_w_out"][0], dtype=f32),
        "b_w_q_a": np.ascontiguousarray(inputs["b_w_q_a"][0], dtype=f32),
        "b_w_q_b": np.ascontiguousarray(inputs["b_w_q_b"][0], dtype=f32),
        "b_w_o": np.ascontiguousarray(inputs["b_w_o"][0], dtype=f32),
        "kv_w_a": np.ascontiguousarray(inputs["kv_w_a"], dtype=f32),
        "kv_w_b": np.ascontiguousarray(inputs["kv_w_b"], dtype=f32),
        "mlp_w1": np.ascontiguousarray(inputs["mlp_w1"], dtype=f32),
        "mlp_w2": np.ascontiguousarray(inputs["mlp_w2"], dtype=f32),
    }
    return shared


def kernel(**inputs):
    x = np.asarray(inputs["x"], np.float32)
    pos = np.asarray(inputs["positions"], np.int32)
    B = x.shape[0]
    shared = _pack_inputs(inputs)
    in_maps = []
    for b in range(B):
        m = dict(shared)
        m["xT"] = np.ascontiguousarray(x[b].T)
        m["posb"] = np.ascontiguousarray(np.broadcast_to(pos[b][None, :], (64, T)))
        in_maps.append(m)
    if "nc" not in _NC_CACHE:
        _NC_CACHE["nc"] = build_program()
    res = run_bass_kernel_spmd(_NC_CACHE["nc"], in_maps, core_ids=list(range(B)))
    out = np.empty((B, T, D), np.float32)
    for b in range(B):
        out[b] = res.results[b]["outT"].T
    return out
```

```python
import contextlib
import numpy as np
import concourse.bass as bass
import concourse.mybir as mybir
from concourse.bass_utils import run_bass_kernel_spmd

F32 = mybir.dt.float32
BF16 = mybir.dt.bfloat16
I32 = mybir.dt.int32
AF = mybir.ActivationFunctionType
ALU = mybir.AluOpType

D = 1024
T = 2048
KC = 8
NTG = 4
EPS = 1e-6
GATE = 2048
DFF = 4096
QL = 384
KVL = 256
NH = 8
SCALE = 192.0 ** -0.5
PI = float(np.pi)
TWO_PI = float(2 * np.pi)

ENGS = ("pe", "act", "dve", "pool", "sp")

C_MIX0, C_MLP0, C_KVSRC, C_MIX1, C_MLP1, C_FIN = 0, 8, 16, 24, 32, 40
C_QG, C_KVAG, C_LNG, C_FREQ, C_LNB, C_SIGN, NCOL = 48, 51, 53, 69, 70, 86, 87


class Op:
    __slots__ = ("eng", "fn", "deps", "dma_sem", "token", "needed")

    def __init__(self, eng, fn, deps, dma_sem):
        self.eng = eng
        self.fn = fn
        self.deps = deps
        self.dma_sem = dma_sem
        self.token = None
        self.needed = False


class Prog:
    def __init__(self, nc):
        self.nc = nc
        self.ops = {e: [] for e in ENGS}
        self.last_writer = {}
        self.readers = {}
        self.dma_counts = {}
        self.pending_barrier = {}

    def add(self, eng, fn, reads=(), writes=(), dma=None, dma_val=None):
        deps = []
        for r in reads:
            w = self.last_writer.get(r)
            if w is not None:
                deps.append(w)
        for w_ in writes:
            rd = self.readers.get(w_)
            if rd:
                deps.extend(rd.values())
            w = self.last_writer.get(w_)
            if w is not None:
                deps.append(w)
        pb = self.pending_barrier.pop(eng, None)
        if pb:
            deps.extend(pb)
        op = Op(eng, fn, deps, dma)
        if dma is not None:
            c = self.dma_counts.get(dma, 0) + 16
            self.dma_counts[dma] = c
            op.token = (("dma", dma), c if dma_val is None else dma_val)
        self.ops[eng].append(op)
        for r in reads:
            d = self.readers.setdefault(r, {})
            d[eng if dma is None else ("dma", dma)] = op
        for w_ in writes:
            self.last_writer[w_] = op
            self.readers[w_] = {}
        return op

    def barrier(self):
        tails = []
        for e in ENGS:
            last_c = None
            last_d = {}
            for op in self.ops[e]:
                if op.dma_sem is None:
                    last_c = op
                else:
                    last_d[op.dma_sem] = op
            if last_c is not None:
                tails.append(last_c)
            tails.extend(last_d.values())
        for e in ENGS:
            self.pending_barrier[e] = list(tails)
        self.last_writer = {}
        self.readers = {}

    def emit(self):
        nc = self.nc
        for e in ENGS:
            for op in self.ops[e]:
                for d in op.deps:
                    if d.dma_sem is None:
                        if d.eng == "pe" and op.eng == "pe" and op.dma_sem is None:
                            continue
                        d.needed = True
        for e in ENGS:
            c = 0
            for op in self.ops[e]:
                if op.dma_sem is None and op.needed:
                    c += 1
                    op.token = (("eng", e), c)
        with contextlib.ExitStack() as st:
            sems = {}
            for e in ENGS:
                sems[("eng", e)] = st.enter_context(nc.semaphore("s_" + e))
            for name in self.dma_counts:
                sems[("dma", name)] = st.enter_context(nc.semaphore("d_" + str(name)))
            block = st.enter_context(nc.Block())
            engobj = {"pe": block.tensor, "act": block.scalar, "dve": block.vector,
                      "pool": block.gpsimd, "sp": block.sync}

            def make(e):
                def body(eng):
                    seen = {}
                    for op in self.ops[e]:
                        need = {}
                        for d in op.deps:
                            if d.token is None:
                                continue
                            if (d.dma_sem is None and d.eng == "pe" and e == "pe"
                                    and op.dma_sem is None):
                                continue
                            k, v = d.token
                            if v > need.get(k, 0):
                                need[k] = v
                        for k, v in need.items():
                            if seen.get(k, 0) >= v:
                                continue
                            seen[k] = v
                            eng.wait_ge(sems[k], v)
                        ins = op.fn(eng)
                        if op.dma_sem is not None:
                            ins.then_inc(sems[("dma", op.dma_sem)], 16)
                        elif op.needed:
                            ins.then_inc(sems[op.token[0]], 1)
                    fin = {}
                    for op in self.ops[e]:
                        if op.dma_sem is not None:
                            k, v = op.token
                            fin[k] = max(fin.get(k, 0), v)
                    for k, v in fin.items():
                        eng.wait_ge(sems[k], v)
                return body

            for e in ENGS:
                if self.ops[e]:
                    engobj[e](make(e))


class Arena:
    def __init__(self, nc, nbytes):
        self.t = nc.alloc_sbuf_tensor("arena", [128, nbytes // 4], F32).ap()
        self.nbytes = nbytes
        self.off = 0
        self.peak = 0

    def alloc(self, shape, dtype):
        esz = 2 if dtype == BF16 else 4
        n = 1
        for s in shape:
            n *= s
        nb = (n * esz + 31) // 32 * 32
        assert self.off + nb <= self.nbytes, ("SBUF arena overflow", self.off, nb, self.nbytes)
        a = self.t[:, self.off // 4:(self.off + nb) // 4]
        if dtype != F32:
            a = a.bitcast(dtype)
        a = a[:, 0:n]
        if len(shape) == 2:
            a = a.rearrange("p (a b) -> p a b", a=shape[0])
        elif len(shape) == 3:
            a = a.rearrange("p (a b c) -> p a b c", a=shape[0], b=shape[1])
        self.off += nb
        self.peak = max(self.peak, self.off)
        return a

    def mark(self):
        return self.off

    def reset(self, m):
        self.off = m


class WStream:
    def __init__(self, P, bufs, plan=None):
        self.P = P
        self.bufs = bufs
        self.nb = len(bufs)
        self.plan = plan
        self.reqs = []
        self.issued = 0

    def _issue(self, i):
        src, shape = self.plan[i]
        b = i % self.nb
        dst = self.view(b, shape)
        self.P.add("pool", lambda e, dst=dst, src=src: e.dma_start(out=dst, in_=src),
                   writes=[("wbuf", b)], dma=("w", b))

    def view(self, b, shape):
        n = 1
        for s in shape:
            n *= s
        a = self.bufs[b][:, 0:n]
        if len(shape) == 2:
            a = a.rearrange("p (a b) -> p a b", a=shape[0])
        return a

    def get(self, src, shape):
        i = len(self.reqs)
        self.reqs.append((src, shape))
        if self.plan is not None:
            while self.issued < min(len(self.plan), i + self.nb - 1):
                self._issue(self.issued)
                self.issued += 1
        b = i % self.nb
        return self.view(b, shape), ("wbuf", b)


class _Stop(Exception):
    pass


def build_program(debug=False, stop=None):
    nc = bass.Bass("TRN2", target_bir_lowering=False)
    dr = {}

    def din(name, shape, dt=F32):
        dr[name] = nc.dram_tensor(name, list(shape), dt, kind="ExternalInput").ap()
        return dr[name]

    xT = din("xT", [D, T])
    posb = din("posb", [128, T], I32)
    pcol_d = din("pcol", [128, NCOL])
    bsb_d = din("bsb", [128, 8, 128])
    wsT_d = din("wsT", [128, 8, 128])
    a_w_in = din("a_w_in", [D, 4096])
    a_w_out = din("a_w_out", [GATE, D])
    w_q_a = din("b_w_q_a", [D, QL])
    w_q_b = din("b_w_q_b", [QL, 1536])
    w_o = din("b_w_o", [D, D])
    kv_w_a = din("kv_w_a", [D, 320])
    kv_w_b = din("kv_w_b", [KVL, 2048])
    mlp_w1 = din("mlp_w1", [2, D, DFF])
    mlp_w2 = din("mlp_w2", [2, DFF, D])
    outT = nc.dram_tensor("outT", [D, T], F32, kind="ExternalOutput").ap()
    dbg = None
    if debug:
        dbg = nc.dram_tensor("dbg", [6, D, T], F32, kind="ExternalOutput").ap()

    ARENA_BYTES = 206 * 1024
    arena = Arena(nc, ARENA_BYTES)
    psum = [nc.alloc_psum_tensor("ps%d" % i, [128, 512], F32).ap() for i in range(8)]

    def emit_all(P, plan):
        holder = {}
        try:
            _emit_body(P, plan, holder)
        except _Stop:
            pass
        return holder["W"]

    def _emit_body(P, plan, holder):
        arena.off = 0
        hT = arena.alloc([KC, T], F32)
        pcol = arena.alloc([NCOL], F32)
        ones_bf = arena.alloc([128], BF16)
        wbufs = [arena.alloc([4096], BF16) for _ in range(4)]
        W = WStream(P, wbufs, plan)
        holder["W"] = W
        rstd = [arena.alloc([512], F32) for _ in range(2)]
        tmpf = [arena.alloc([512], F32) for _ in range(3)]
        pi_col = arena.alloc([8], F32)
        base_mark = arena.mark()
        TQ = arena.alloc([T], F32)
        TX = arena.alloc([T], F32)
        rope_mark = arena.mark()
        cur = {}

        def col(c0, k):
            return pcol[:, c0 + k:c0 + k + 1]

        state = {"ps": 0, "rstd": 0}

        def ps_next(banks=range(8)):
            banks = list(banks)
            b = banks[state["ps"] % len(banks)]
            state["ps"] += 1
            return psum[b], ("ps", b)

        def mm_group(out_ap, pairs, reads, psreg):
            def fn(e, pairs=pairs, out_ap=out_ap):
                n = len(pairs)
                ins = None
                for i, (l, r) in enumerate(pairs):
                    ins = e.matmul(out_ap, lhsT=l, rhs=r, start=(i == 0), stop=(i == n - 1))
                return ins
            P.add("pe", fn, reads=reads, writes=[psreg])

        P.add("sp", lambda e: e.dma_start(out=pcol, in_=pcol_d), writes=["pcol"], dma="pcol")
        xTv = xT.rearrange("(k p) t -> p k t", p=128)
        for tq in range(NTG):
            P.add("sp", lambda e, tq=tq: e.dma_start(out=hT[:, :, tq * 512:(tq + 1) * 512],
                                                     in_=xTv[:, :, tq * 512:(tq + 1) * 512]),
                  writes=[("hT", kc, tq) for kc in range(KC)], dma=("x", tq))
        P.add("dve", lambda e: e.memset(ones_bf, 1.0), writes=["ones"])
        P.add("dve", lambda e: e.memset(pi_col, PI), writes=["pi_col"])

        def stop_here(name):
            if stop == name:
                for kc in range(KC):
                    P.add("sp", lambda e, kc=kc: e.dma_start(out=outT[kc * 128:(kc + 1) * 128, :],
                                                             in_=hT[:, kc, :]),
                          reads=[("hT", kc, tg) for tg in range(NTG)], dma="outstop", dma_val=16 * 8)
                raise _Stop()

        def dump(slot):
            if not debug:
                return
            for kc in range(KC):
                P.add("sp", lambda e, kc=kc: e.dma_start(out=dbg[slot, kc * 128:(kc + 1) * 128, :],
                                                         in_=hT[:, kc, :]),
                      reads=[("hT", kc, tg) for tg in range(NTG)], dma=("dbg", slot), dma_val=16 * 8)

        def rmsnorm_tg(src_fn, nchunks, dim, gcol0, dst_fn, tg, src_regs, dst_regs,
                       banks=range(8)):
            sq = cur["sq"]
            if nchunks == KC and src_fn is None:
                P.add("act", lambda e, tg=tg, sq=sq: e.activation(
                    out=sq, in_=hT[:, :, tg * 512:(tg + 1) * 512], func=AF.Square),
                    reads=src_regs, writes=[("sq", k) for k in range(KC)])
            else:
                for k in range(nchunks):
                    P.add("act", lambda e, k=k, sq=sq: e.activation(out=sq[:, k, :], in_=src_fn(k),
                                                                    func=AF.Square),
                          reads=src_regs, writes=[("sq", k)])
            ps, psreg = ps_next(banks)
            mm_group(ps, [(ones_bf, sq[:, k, :]) for k in range(nchunks)],
                     ["ones"] + [("sq", k) for k in range(nchunks)], psreg)
            ri = state["rstd"] % 2
            state["rstd"] += 1
            r = rstd[ri]
            rreg = ("rstd", ri)
            P.add("dve", lambda e, r=r, ps=ps: e.tensor_scalar(
                out=r, in0=ps, scalar1=1.0 / dim, scalar2=EPS, op0=ALU.mult, op1=ALU.add),
                reads=[psreg], writes=[rreg, psreg])
            P.add("act", lambda e, r=r: e.activation(out=r, in_=r, func=AF.Sqrt),
                  reads=[rreg], writes=[rreg])
            P.add("dve", lambda e, r=r: e.reciprocal(out=r, in_=r), reads=[rreg], writes=[rreg])
            for k in range(nchunks):
                s_ap = hT[:, k, tg * 512:(tg + 1) * 512] if src_fn is None else src_fn(k)
                P.add("dve", lambda e, k=k, s_ap=s_ap, r=r: e.scalar_tensor_tensor(
                    out=dst_fn(k), in0=s_ap, scalar=col(gcol0, k), in1=r,
                    op0=ALU.mult, op1=ALU.mult),
                    reads=src_regs + [rreg, "pcol"], writes=[dst_regs[k]])

        C1 = 6.28125
        C2 = TWO_PI - 6.28125

        def emit_rope_tables(rp_i, rp_a, rp_t, rp_s):
            def reduce(shift):
                P.add("dve", lambda e: e.tensor_scalar(out=rp_t, in0=rp_a, scalar1=shift,
                                                       scalar2=1.0 / TWO_PI, op0=ALU.add, op1=ALU.mult),
                      reads=["rp_a"], writes=["rp_t"])
                P.add("dve", lambda e: e.tensor_copy(out=rp_i, in_=rp_t), reads=["rp_t"], writes=["rp_i"])
                P.add("dve", lambda e: e.tensor_copy(out=rp_t, in_=rp_i), reads=["rp_i"], writes=["rp_t"])
                P.add("dve", lambda e: e.scalar_tensor_tensor(out=rp_s, in0=rp_t, scalar=-C1, in1=rp_a,
                                                              op0=ALU.mult, op1=ALU.add),
                      reads=["rp_t", "rp_a"], writes=["rp_s"])
                P.add("dve", lambda e: e.scalar_tensor_tensor(out=rp_s, in0=rp_t, scalar=-C2, in1=rp_s,
                                                              op0=ALU.mult, op1=ALU.add),
                      reads=["rp_t", "rp_s"], writes=["rp_s"])
                if shift:
                    P.add("dve", lambda e: e.tensor_scalar(out=rp_s, in0=rp_s, scalar1=shift, scalar2=None,
                                                           op0=ALU.add),
                          reads=["rp_s"], writes=["rp_s"])
                P.add("dve", lambda e: e.tensor_scalar(out=rp_t, in0=rp_s, scalar1=PI, scalar2=TWO_PI,
                                                       op0=ALU.is_gt, op1=ALU.mult),
                      reads=["rp_s"], writes=["rp_t"])
                P.add("dve", lambda e: e.tensor_tensor(out=rp_s, in0=rp_s, in1=rp_t, op=ALU.subtract),
                      reads=["rp_s", "rp_t"], writes=["rp_s"])

            for c in range(NTG):
                csl = slice(c * 512, (c + 1) * 512)
                P.add("sp", lambda e, csl=csl: e.dma_start(out=rp_i, in_=posb[:, csl]),
                      writes=["rp_i"], dma="pos")
                P.add("dve", lambda e: e.tensor_copy(out=rp_a, in_=rp_i), reads=["rp_i"], writes=["rp_a"])
                P.add("dve", lambda e: e.tensor_scalar(out=rp_a, in0=rp_a, scalar1=col(C_FREQ, 0),
                                                       scalar2=None, op0=ALU.mult),
                      reads=["rp_a", "pcol"], writes=["rp_a"])
                reduce(0.0)
                P.add("act", lambda e: e.activation(out=rp_s, in_=rp_s, func=AF.Sin),
                      reads=["rp_s"], writes=["rp_s"])
                P.add("dve", lambda e, csl=csl: e.tensor_scalar(
                    out=TX[0:64, csl], in0=rp_s[0:64, :], scalar1=pcol[0:64, C_SIGN:C_SIGN + 1],
                    scalar2=None, op0=ALU.mult),
                    reads=["rp_s", "pcol"], writes=[("TX", c)])
                P.add("dve", lambda e, csl=csl: e.tensor_scalar(
                    out=TQ[64:128, csl], in0=rp_s[64:128, :], scalar1=pcol[64:128, C_SIGN:C_SIGN + 1],
                    scalar2=None, op0=ALU.mult),
                    reads=["rp_s", "pcol"], writes=[("TQ", c)])
                reduce(PI / 2)
                P.add("act", lambda e, csl=csl: e.activation(out=TQ[0:64, csl], in_=rp_s[0:64, :], func=AF.Sin),
                      reads=["rp_s"], writes=[("TQ", c)])
                P.add("act", lambda e, csl=csl: e.activation(out=TX[64:128, csl], in_=rp_s[64:128, :], func=AF.Sin),
                      reads=["rp_s", ("TQ", c)], writes=[("TX", c)])
            if debug:
                P.add("sp", lambda e: e.dma_start(out=dbg[4, 0:128, :], in_=TQ), reads=[("TQ", c) for c in range(NTG)], dma="dbgc")
                P.add("sp", lambda e: e.dma_start(out=dbg[5, 0:128, :], in_=TX), reads=[("TX", c) for c in range(NTG)], dma="dbgs")

        cur["sq"] = arena.alloc([KC, 512], BF16)
        hn1 = arena.alloc([KC, 512], BF16)
        v_bf = arena.alloc([4, GATE], BF16)
        gateT = arena.alloc([16, 512], BF16)
        uT_sb = [arena.alloc([512], BF16) for _ in range(2)]
        B_T = arena.alloc([16, 128], F32)
        wsT_bf = arena.alloc([8, 128], BF16)
        stats = arena.alloc([4, 6], F32)
        mv = arena.alloc([4, 2], F32)
        nmr = arena.alloc([4, 2], F32)
        m1b = arena.mark()
        rp_i = arena.alloc([512], I32)
        rp_a = arena.alloc([512], F32)
        rp_t = arena.alloc([512], F32)
        rp_s = arena.alloc([512], F32)
        wsT_f = arena.alloc([8, 128], F32)
        rs_bc = arena.alloc([8, 128], F32)
        bs_bc = arena.alloc([8, 128], F32)

        P.add("sp", lambda e: e.dma_start(out=wsT_f, in_=wsT_d), writes=["wsT_f"], dma="wsT")
        P.add("sp", lambda e: e.dma_start(out=bs_bc, in_=bsb_d), writes=["bs_bc"], dma="bsb")
        P.add("dve", lambda e: e.memset(wsT_f[64:128, :, 0:64], 0.0), reads=["wsT_f"], writes=["wsT_f"])
        P.add("dve", lambda e: e.tensor_copy(out=wsT_bf, in_=wsT_f), reads=["wsT_f"], writes=["wsT_bf"])
        for half in range(2):
            ps, psreg = ps_next()
            mm_group(ps, [(ones_bf, wsT_bf[:, half * 4:(half + 1) * 4, :])], ["ones", "wsT_bf"], psreg)
            P.add("dve", lambda e, ps=ps, half=half: e.tensor_copy(
                out=rs_bc[:, half * 4:(half + 1) * 4, :], in_=ps.rearrange("p (a b) -> p a b", a=4)),
                reads=[psreg], writes=["rs_bc", psreg])
        for cc in range(16):
            g = cc // 2
            P.add("dve", lambda e, cc=cc, g=g: e.scalar_tensor_tensor(
                out=B_T[:, cc, :], in0=rs_bc[:, g, :], scalar=col(C_LNB, cc), in1=bs_bc[:, g, :],
                op0=ALU.mult, op1=ALU.add),
                reads=["rs_bc", "bs_bc", "pcol"], writes=["B_T"])
        stop_here("gsetup")

        w_in_v = a_w_in.rearrange("(k p) c -> p k c", p=128)
        w_out_v = a_w_out.rearrange("(k p) c -> p k c", p=128)
        for tp in range(4):
            hregs = [("hT", kc, tp) for kc in range(KC)]
            rmsnorm_tg(None, KC, D, C_MIX0, lambda k: hn1[:, k, :], tp, hregs,
                       [("hn1", k) for k in range(KC)])
            if tp == 0:
                emit_rope_tables(rp_i, rp_a, rp_t, rp_s)
            for n in range(4):
                wv, wreg = W.get(w_in_v[:, :, 2048 + n * 512:2048 + (n + 1) * 512], [KC, 512])
                for tt in range(4):
                    ps, psreg = ps_next()
                    mm_group(ps, [(hn1[:, k, tt * 128:(tt + 1) * 128], wv[:, k, :]) for k in range(KC)],
                             [("hn1", k) for k in range(KC)] + [wreg], psreg)
                    P.add("act", lambda e, ps=ps, tt=tt, n=n: e.activation(
                        out=v_bf[:, tt, n * 512:(n + 1) * 512], in_=ps, func=AF.Gelu),
                        reads=[psreg], writes=[("v", tt), psreg])
            if tp == 0:
                stop_here("gv")
            for tt in range(4):
                for n in range(4):
                    P.add("dve", lambda e, tt=tt, n=n: e.bn_stats(
                        out=stats[:, n, :], in_=v_bf[:, tt, n * 512:(n + 1) * 512]),
                        reads=[("v", tt)], writes=["stats"])
                P.add("dve", lambda e, tt=tt: e.bn_aggr(out=mv[:, tt, :], in_=stats.rearrange("p a b -> p (a b)")),
                      reads=["stats"], writes=[("mv", tt)])
                P.add("dve", lambda e, tt=tt: e.tensor_scalar(
                    out=nmr[:, tt, 1:2], in0=mv[:, tt, 1:2], scalar1=EPS, scalar2=None, op0=ALU.add),
                    reads=[("mv", tt)], writes=[("nmr", tt)])
                P.add("act", lambda e, tt=tt: e.activation(out=nmr[:, tt, 1:2], in_=nmr[:, tt, 1:2],
                                                           func=AF.Sqrt),
                      reads=[("nmr", tt)], writes=[("nmr", tt)])
                P.add("dve", lambda e, tt=tt: e.reciprocal(out=nmr[:, tt, 1:2], in_=nmr[:, tt, 1:2]),
                      reads=[("nmr", tt)], writes=[("nmr", tt)])
                P.add("dve", lambda e, tt=tt: e.tensor_scalar(
                    out=v_bf[:, tt, :], in0=v_bf[:, tt, :], scalar1=mv[:, tt, 0:1],
                    scalar2=nmr[:, tt, 1:2], op0=ALU.subtract, op1=ALU.mult),
                    reads=[("v", tt), ("mv", tt), ("nmr", tt)], writes=[("v", tt)])
            if tp == 0:
                stop_here("gln")
            for n in range(4):
                wu, wreg = W.get(w_in_v[:, :, n * 512:(n + 1) * 512], [KC, 512])
                for c4 in range(4):
                    cc = n * 4 + c4
                    g = cc // 2
                    ps_u, pr_u = ps_next()
                    mm_group(ps_u, [(wu[:, k, c4 * 128:(c4 + 1) * 128], hn1[:, k, :]) for k in range(KC)],
                             [("hn1", k) for k in range(KC)] + [wreg], pr_u)
                    ub = uT_sb[cc % 2]
                    ureg = ("uT", cc % 2)
                    P.add("act", lambda e, ps_u=ps_u, ub=ub: e.activation(out=ub, in_=ps_u, func=AF.Gelu),
                          reads=[pr_u], writes=[ureg, pr_u])
                    ps_s, pr_s = ps_next()

                    def fn(e, ps_s=ps_s, cc=cc, g=g):
                        ins = None
                        for tt in range(4):
                            ins = e.matmul(ps_s[:, tt * 128:(tt + 1) * 128],
                                           lhsT=v_bf[:, tt, cc * 128:(cc + 1) * 128],
                                           rhs=wsT_bf[:, g, :], start=True, stop=True)
                        return ins
                    P.add("pe", fn, reads=[("v", tt) for tt in range(4)] + ["wsT_bf"], writes=[pr_s])
                    tf = tmpf[cc % 2]
                    treg = ("tmpf", cc % 2)
                    P.add("dve", lambda e, ps_s=ps_s, tf=tf, cc=cc: e.scalar_tensor_tensor(
                        out=tf.rearrange("p (a b) -> p a b", a=4),
                        in0=ps_s.rearrange("p (a b) -> p a b", a=4),
                        scalar=col(C_LNG, cc),
                        in1=B_T[:, cc:cc + 1, :].broadcast_to([128, 4, 128]),
                        op0=ALU.mult, op1=ALU.add),
                        reads=[pr_s, "B_T", "pcol"], writes=[treg, pr_s])
                    P.add("dve", lambda e, tf=tf, ub=ub, cc=cc: e.tensor_tensor(
                        out=gateT[:, cc, :], in0=tf, in1=ub, op=ALU.mult),
                        reads=[treg, ureg], writes=[("gate", cc)])
            if tp == 0:
                stop_here("gu")
            for dq in range(4):
                wo, wreg = W.get(w_out_v[:, :, dq * 256:(dq + 1) * 256], [16, 256])
                for d2 in range(2):
                    dc = dq * 2 + d2
                    ps, psreg = ps_next()
                    mm_group(ps, [(wo[:, cc, d2 * 128:(d2 + 1) * 128], gateT[:, cc, :]) for cc in range(16)],
                             [("gate", cc) for cc in range(16)] + [wreg], psreg)
                    P.add("dve", lambda e, ps=ps, dc=dc, tp=tp: e.tensor_tensor(
                        out=hT[:, dc, tp * 512:(tp + 1) * 512], in0=ps,
                        in1=hT[:, dc, tp * 512:(tp + 1) * 512], op=ALU.add),
                        reads=[psreg, ("hT", dc, tp)], writes=[("hT", dc, tp), psreg])
        dump(0)
        stop_here("gmlp")
        P.barrier()
        arena.reset(rope_mark)

        def mlp(layer, gcol0, final=False):
            m = arena.mark()
            cur["sq"] = arena.alloc([KC, 512], BF16)
            hnT = arena.alloc([KC, T], BF16)
            hidT = arena.alloc([KC, T], BF16)
            ob = arena.alloc([KC, 512], F32) if final else None
            outv = outT.rearrange("(k p) t -> p k t", p=128)
            w1v = mlp_w1[layer].rearrange("(k p) c -> p k c", p=128)
            w2v = mlp_w2[layer].rearrange("(k p) c -> p k c", p=128)
            for tg in range(NTG):
                rmsnorm_tg(None, KC, D, gcol0, lambda k, tg=tg: hnT[:, k, tg * 512:(tg + 1) * 512], tg,
                           [("hT", kc, tg) for kc in range(KC)],
                           [("hnT", k, tg) for k in range(KC)])
            ti = 0
            for fq in range(4):
                for n in range(2):
                    w1b, wreg = W.get(w1v[:, :, fq * 1024 + n * 512:fq * 1024 + (n + 1) * 512], [KC, 512])
                    for f4 in range(4):
                        fc = n * 4 + f4
                        for tg in range(NTG):
                            ps, psreg = ps_next()
                            mm_group(ps, [(w1b[:, k, f4 * 128:(f4 + 1) * 128],
                                           hnT[:, k, tg * 512:(tg + 1) * 512]) for k in range(KC)],
                                     [("hnT", k, tg) for k in range(KC)] + [wreg], psreg)
                            tf = tmpf[ti % 3]
                            treg = ("tmpf", ti % 3)
                            ti += 1
                            P.add("act", lambda e, ps=ps, tf=tf: e.activation(out=tf, in_=ps, func=AF.Relu),
                                  reads=[psreg], writes=[treg, psreg])
                            P.add("dve", lambda e, tf=tf, fc=fc, tg=tg: e.tensor_tensor(
                                out=hidT[:, fc, tg * 512:(tg + 1) * 512], in0=tf, in1=tf, op=ALU.mult),
                                reads=[treg], writes=[("hid", fc, tg)])
                def o_group(w2b, wreg, d4, dc, tg):
                    ps, psreg = ps_next()
                    mm_group(ps, [(w2b[:, fc, d4 * 128:(d4 + 1) * 128],
                                   hidT[:, fc, tg * 512:(tg + 1) * 512]) for fc in range(8)],
                             [("hid", fc, tg) for fc in range(8)] + [wreg], psreg)
                    P.add("dve", lambda e, ps=ps, dc=dc, tg=tg: e.tensor_tensor(
                        out=hT[:, dc, tg * 512:(tg + 1) * 512], in0=ps,
                        in1=hT[:, dc, tg * 512:(tg + 1) * 512], op=ALU.add),
                        reads=[psreg, ("hT", dc, tg)], writes=[("hT", dc, tg), psreg])

                if final and fq == 3:
                    blks = [W.get(w2v[:, fq * 8:(fq + 1) * 8, dh * 512:(dh + 1) * 512], [8, 512])
                            for dh in range(2)]
                    for tg in range(NTG):
                        for dc in range(KC):
                            w2b, wreg = blks[dc // 4]
                            o_group(w2b, wreg, dc % 4, dc, tg)
                        if debug:
                            pass
                        rmsnorm_tg(None, KC, D, C_FIN, lambda k: ob[:, k, :], tg,
                                   [("hT", kc, tg) for kc in range(KC)], [("ob", k) for k in range(KC)])
                        P.add("sp", lambda e, tg=tg: e.dma_start(
                            out=outv[:, :, tg * 512:(tg + 1) * 512], in_=ob),
                            reads=[("ob", k) for k in range(KC)], dma="out")
                else:
                    for dh in range(2):
                        w2b, wreg = W.get(w2v[:, fq * 8:(fq + 1) * 8, dh * 512:(dh + 1) * 512], [8, 512])
                        for d4 in range(4):
                            for tg in range(NTG):
                                o_group(w2b, wreg, d4, dh * 4 + d4, tg)
            P.barrier()
            arena.reset(m)

        mlp(0, C_MLP0)
        dump(1)
        stop_here("mlp0")

        ckvT = arena.alloc([2, T], BF16)
        kpeT = arena.alloc([T], BF16)
        cqT = arena.alloc([3, T], BF16)
        m3 = arena.mark()
        cur["sq"] = arena.alloc([KC, 512], BF16)
        hnT = arena.alloc([KC, T], BF16)
        ckv_raw = arena.alloc([3, 512], F32)
        kva_a = arena.alloc([KC, 128], BF16)
        kva_b = arena.alloc([KC, 128], BF16)

        for tg in range(NTG):
            rmsnorm_tg(None, KC, D, C_KVSRC, lambda k, tg=tg: hnT[:, k, tg * 512:(tg + 1) * 512], tg,
                       [("hT", kc, tg) for kc in range(KC)], [("hnT", k, tg) for k in range(KC)])
        wa, wareg = W.get(kv_w_a.rearrange("(k p) c -> p k c", p=128), [KC, 320])
        for j, (dst_t, c_lo, src_lo, w) in enumerate([
                (kva_a, 0, 256, 64), (kva_a, 64, 256, 64),
                (kva_b, 0, 288, 32), (kva_b, 32, 256, 32), (kva_b, 64, 288, 32), (kva_b, 96, 256, 32)]):
            P.add("dve", lambda e, dst_t=dst_t, c_lo=c_lo, src_lo=src_lo, w=w: e.tensor_copy(
                out=dst_t[:, :, c_lo:c_lo + w], in_=wa[:, :, src_lo:src_lo + w]),
                reads=[wareg], writes=[("kva", j)])
        kva_regs = [("kva", j) for j in range(6)]

        for tg in range(NTG):
            tsl = slice(tg * 512, (tg + 1) * 512)
            hreads = [("hnT", k, tg) for k in range(KC)]
            for c2 in range(2):
                ps, psreg = ps_next()
                mm_group(ps, [(wa[:, k, c2 * 128:(c2 + 1) * 128], hnT[:, k, tsl]) for k in range(KC)],
                         hreads + [wareg], psreg)
                P.add("act", lambda e, ps=ps, c2=c2: e.activation(out=ckv_raw[:, c2, :], in_=ps, func=AF.Copy),
                      reads=[psreg], writes=[("ckv_raw", c2), psreg])
            rmsnorm_tg(lambda k: ckv_raw[:, k, :], 2, KVL, C_KVAG,
                       lambda k, tsl=tsl: ckvT[:, k, tsl], tg,
                       [("ckv_raw", 0), ("ckv_raw", 1)], [("ckvT", k, tg) for k in range(2)])
            ps_a, pr_a = ps_next()
            mm_group(ps_a, [(kva_a[:, k, :], hnT[:, k, tsl]) for k in range(KC)], hreads + kva_regs, pr_a)
            ps_b, pr_b = ps_next()
            mm_group(ps_b, [(kva_b[:, k, :], hnT[:, k, tsl]) for k in range(KC)], hreads + kva_regs, pr_b)
            tq, tx = ("TQ", tg), ("TX", tg)
            P.add("dve", lambda e, ps_a=ps_a, tsl=tsl: e.tensor_tensor(
                out=tmpf[0][0:64, :], in0=ps_a[0:64, :], in1=TQ[0:64, tsl], op=ALU.mult),
                reads=[pr_a, tq], writes=[("tmpf", 0), pr_a])
            P.add("dve", lambda e, ps_a=ps_a, tsl=tsl: e.tensor_tensor(
                out=tmpf[0][64:128, :], in0=ps_a[64:128, :], in1=TX[64:128, tsl], op=ALU.mult),
                reads=[pr_a, tx, ("tmpf", 0)], writes=[("tmpf", 0), pr_a])
            P.add("dve", lambda e, ps_b=ps_b, tsl=tsl: e.tensor_tensor(
                out=tmpf[1][0:64, :], in0=ps_b[0:64, :], in1=TX[0:64, tsl], op=ALU.mult),
                reads=[pr_b, tx], writes=[("tmpf", 1), pr_b])
            P.add("dve", lambda e, ps_b=ps_b, tsl=tsl: e.tensor_tensor(
                out=tmpf[1][64:128, :], in0=ps_b[64:128, :], in1=TQ[64:128, tsl], op=ALU.mult),
                reads=[pr_b, tq, ("tmpf", 1)], writes=[("tmpf", 1), pr_b])
            P.add("dve", lambda e, tsl=tsl: e.tensor_tensor(
                out=kpeT[:, tsl], in0=tmpf[0], in1=tmpf[1], op=ALU.add),
                reads=[("tmpf", 0), ("tmpf", 1)], writes=[("kpeT", tg)])

        stop_here("kv")
        for tg in range(NTG):
            rmsnorm_tg(None, KC, D, C_MIX1, lambda k, tg=tg: hnT[:, k, tg * 512:(tg + 1) * 512], tg,
                       [("hT", kc, tg) for kc in range(KC)], [("hnT", k, tg) for k in range(KC)])
        wqa, wqareg = W.get(w_q_a.rearrange("(k p) c -> p k c", p=128), [KC, QL])
        for tg in range(NTG):
            tsl = slice(tg * 512, (tg + 1) * 512)
            raws = [ckv_raw[:, 0, :], ckv_raw[:, 1, :], ckv_raw[:, 2, :]]
            rregs = [("ckv_raw", 0), ("ckv_raw", 1), ("ckv_raw", 2)]
            for c3 in range(3):
                ps, psreg = ps_next()
                mm_group(ps, [(wqa[:, k, c3 * 128:(c3 + 1) * 128], hnT[:, k, tsl]) for k in range(KC)],
                         [("hnT", k, tg) for k in range(KC)] + [wqareg], psreg)
                P.add("act", lambda e, ps=ps, c3=c3, raws=raws: e.activation(out=raws[c3], in_=ps, func=AF.Copy),
                      reads=[psreg], writes=[rregs[c3], psreg])
            rmsnorm_tg(lambda k, raws=raws: raws[k], 3, QL, C_QG,
                       lambda k, tsl=tsl: cqT[:, k, tsl], tg, rregs,
                       [("cqT", k, tg) for k in range(3)])
        P.barrier()
        arena.reset(m3)
        oT_all = arena.alloc([NH, T], BF16)
        kT_h = arena.alloc([T], BF16)
        v_h = arena.alloc([16, 128], BF16)
        qT_h = arena.alloc([T], BF16)
        qpe_h = arena.alloc([T], BF16)
        wq_cat = arena.alloc([3, 128], BF16)
        pT = [arena.alloc([512], BF16) for _ in range(4)]
        rcp = [arena.alloc([512], F32) for _ in range(2)]

        wqb_v = w_q_b.rearrange("(k p) c -> p k c", p=128)
        wkvb_v = kv_w_b.rearrange("(k p) c -> p k c", p=128)
        PROJ_BANKS = [7, 0, 1, 2]
        SC_BANKS = [0, 1, 2]
        cp_i = [0]

        def evac_copy(dst, ps, psreg, dreg):
            cp_i[0] += 1
            if cp_i[0] % 2 == 0:
                P.add("act", lambda e: e.activation(out=dst, in_=ps, func=AF.Copy),
                      reads=[psreg], writes=[dreg, psreg])
            else:
                P.add("dve", lambda e: e.tensor_copy(out=dst, in_=ps), reads=[psreg], writes=[dreg, psreg])

        for h in range(NH):
            wkb, wkbreg = W.get(wkvb_v[:, :, h * 256:(h + 1) * 256], [2, 256])
            wqb, wqbreg = W.get(wqb_v[:, :, h * 192:(h + 1) * 192], [3, 192])
            for j, (c_lo, src_lo, w) in enumerate([(0, 128, 64), (64, 160, 32), (96, 128, 32)]):
                P.add("dve", lambda e, wqb=wqb, c_lo=c_lo, src_lo=src_lo, w=w: e.tensor_copy(
                    out=wq_cat[:, :, c_lo:c_lo + w], in_=wqb[:, :, src_lo:src_lo + w]),
                    reads=[wqbreg], writes=[("wq_cat", j)])
            wqc_regs = [("wq_cat", j) for j in range(3)]
            for tg in range(NTG):
                tsl = slice(tg * 512, (tg + 1) * 512)
                ckr = [("ckvT", k, tg) for k in range(2)]
                cqr = [("cqT", k, tg) for k in range(3)]
                ps, psreg = ps_next(PROJ_BANKS)
                mm_group(ps, [(wkb[:, k, 0:128], ckvT[:, k, tsl]) for k in range(2)], ckr + [wkbreg], psreg)
                evac_copy(kT_h[:, tsl], ps, psreg, ("kT_h", tg))
                ps, psreg = ps_next(PROJ_BANKS)

                def fn(e, ps=ps, tg=tg, wkb=wkb):
                    ins = None
                    for j in range(4):
                        kt = tg * 4 + j
                        for k in range(2):
                            ins = e.matmul(ps[:, j * 128:(j + 1) * 128],
                                           lhsT=ckvT[:, k, kt * 128:(kt + 1) * 128],
                                           rhs=wkb[:, k, 128:256], start=(k == 0), stop=(k == 1))
                    return ins
                P.add("pe", fn, reads=ckr + [wkbreg], writes=[psreg])
                evac_copy(v_h[:, tg * 4:(tg + 1) * 4, :], ps.rearrange("p (a b) -> p a b", a=4), psreg,
                          ("v_h", tg))
                ps, psreg = ps_next(PROJ_BANKS)
                mm_group(ps, [(wqb[:, k, 0:128], cqT[:, k, tsl]) for k in range(3)], cqr + [wqbreg], psreg)
                evac_copy(qT_h[:, tsl], ps, psreg, ("qT_h", tg))
                ps_a, pr_a = ps_next(PROJ_BANKS)
                mm_group(ps_a, [(wq_cat[:, k, :], cqT[:, k, tsl]) for k in range(3)], cqr + wqc_regs, pr_a)
                P.add("dve", lambda e, ps_a=ps_a, tsl=tsl: e.tensor_tensor(
                    out=qpe_h[:, tsl], in0=ps_a, in1=TQ[:, tsl], op=ALU.mult),
                    reads=[pr_a, ("TQ", tg)], writes=[("qpe_h", tg), pr_a])

            tiles = [(qg, kt) for qg in range(NTG) for kt in range(4 * qg + 4)]
            sc = {}

            def emit_qk(i):
                qg, kt = tiles[i]
                m = max(0, kt - 4 * qg)
                c0 = 128 * m
                b = SC_BANKS[i % 3]
                ps, psreg = psum[b], ("ps", b)
                qsl = slice(qg * 512 + c0, (qg + 1) * 512)
                ksl = slice(kt * 128, (kt + 1) * 128)

                def fn(e, ps=ps, c0=c0, qsl=qsl, ksl=ksl):
                    e.matmul(ps[:, c0:512], lhsT=kT_h[:, ksl], rhs=qT_h[:, qsl], start=True, stop=False)
                    return e.matmul(ps[:, c0:512], lhsT=kpeT[:, ksl], rhs=qpe_h[:, qsl],
                                    start=False, stop=True)
                P.add("pe", fn, reads=[("kT_h", kt // 4), ("qT_h", qg), ("kpeT", kt // 4), ("qpe_h", qg)],
                      writes=[psreg])
                sc[i] = (ps, psreg, c0)

            def emit_rest(i):
                qg, kt = tiles[i]
                ps, psreg, c0 = sc.pop(i)
                pb = pT[i % 4]
                preg = ("pT", i % 4)
                P.add("act", lambda e, ps=ps, pb=pb, c0=c0: e.activation(
                    out=pb[:, c0:512], in_=ps[:, c0:512], func=AF.Exp, scale=SCALE),
                    reads=[psreg], writes=[preg, psreg])
                if kt >= 4 * qg:
                    P.add("pool", lambda e, pb=pb, c0=c0: e.memset(pb[64:128, c0:c0 + 64], 0.0),
                          reads=[preg], writes=[preg])
                first = (kt == 0)
                last = (kt == 4 * qg + 3)
                po, poreg = psum[3 + qg % 2], ("ps", 3 + qg % 2)
                pd, pdreg = psum[5 + qg % 2], ("ps", 5 + qg % 2)

                def fn(e, pb=pb, c0=c0, kt=kt, po=po, pd=pd, first=first, last=last):
                    e.matmul(po[:, c0:512], lhsT=v_h[:, kt, :], rhs=pb[:, c0:512], start=first, stop=last)
                    return e.matmul(pd[:, c0:512], lhsT=ones_bf, rhs=pb[:, c0:512], start=first, stop=last)
                P.add("pe", fn, reads=[preg, ("v_h", kt // 4), "ones"], writes=[poreg, pdreg])
                if last:
                    r = rcp[qg % 2]
                    rreg = ("rcp", qg % 2)
                    P.add("dve", lambda e, r=r, pd=pd: e.reciprocal(out=r, in_=pd),
                          reads=[pdreg], writes=[rreg, pdreg])
                    P.add("dve", lambda e, r=r, po=po, qg=qg, h=h: e.tensor_tensor(
                        out=oT_all[:, h, qg * 512:(qg + 1) * 512], in0=po, in1=r, op=ALU.mult),
                        reads=[poreg, rreg], writes=[("oT", h, qg), poreg])

            n = len(tiles)
            emit_qk(0)
            emit_qk(1)
            for i in range(n):
                if i + 2 < n:
                    emit_qk(i + 2)
                emit_rest(i)

        wov = w_o.rearrange("(k p) c -> p k c", p=128)
        for dh in range(2):
            wob, wreg = W.get(wov[:, :, dh * 512:(dh + 1) * 512], [NH, 512])
            for d4 in range(4):
                dc = dh * 4 + d4
                for tg in range(NTG):
                    ps, psreg = ps_next()
                    mm_group(ps, [(wob[:, hh, d4 * 128:(d4 + 1) * 128], oT_all[:, hh, tg * 512:(tg + 1) * 512])
                                  for hh in range(NH)],
                             [("oT", hh, tg) for hh in range(NH)] + [wreg], psreg)
                    P.add("dve", lambda e, ps=ps, dc=dc, tg=tg: e.tensor_tensor(
                        out=hT[:, dc, tg * 512:(tg + 1) * 512], in0=ps,
                        in1=hT[:, dc, tg * 512:(tg + 1) * 512], op=ALU.add),
                        reads=[psreg, ("hT", dc, tg)], writes=[("hT", dc, tg), psreg])
        dump(2)
        stop_here("mla")
        P.barrier()
        arena.reset(base_mark)

        mlp(1, C_MLP1, final=True)
        dump(3)

    P0 = Prog(nc)
    W0 = emit_all(P0, None)
    plan = list(W0.reqs)
    P = Prog(nc)
    emit_all(P, plan)
    P.emit()
    return nc


_NC_CACHE = {}


def _pack_inputs(inputs):
    f32 = np.float32

    def colv(v):
        v = np.asarray(v, f32)
        return np.ascontiguousarray(v.reshape(-1, 128).T)

    pcol = np.zeros((128, NCOL), f32)
    pcol[:, C_MIX0:C_MIX0 + 8] = colv(inputs["norm_mix_g"][0])
    pcol[:, C_MLP0:C_MLP0 + 8] = colv(inputs["norm_mlp_g"][0])
    pcol[:, C_KVSRC:C_KVSRC + 8] = colv(inputs["kv_src_norm_g"])
    pcol[:, C_MIX1:C_MIX1 + 8] = colv(inputs["norm_mix_g"][1])
    pcol[:, C_MLP1:C_MLP1 + 8] = colv(inputs["norm_mlp_g"][1])
    pcol[:, C_FIN:C_FIN + 8] = colv(inputs["final_norm_g"])
    pcol[:, C_QG:C_QG + 3] = colv(inputs["b_q_norm_g"][0])
    pcol[:, C_KVAG:C_KVAG + 2] = colv(inputs["kv_a_norm_g"])
    pcol[:, C_LNG:C_LNG + 16] = colv(inputs["a_ln_v_g"][0])
    inv_freq = (np.float32(10000.0) ** (-np.arange(0, 64, 2, dtype=np.float32) / np.float32(64))).astype(f32)
    for q in range(4):
        pcol[32 * q:32 * (q + 1), C_FREQ] = inv_freq
        pcol[32 * q:32 * (q + 1), C_SIGN] = -1.0 if q % 2 == 0 else 1.0
    pcol[:, C_LNB:C_LNB + 16] = colv(inputs["a_ln_v_b"][0])
    bsb = np.ascontiguousarray(np.broadcast_to(np.asarray(inputs["a_b_s"][0], f32)[None], (128, 8, 128)))
    wsT = np.ascontiguousarray(np.transpose(np.asarray(inputs["a_w_s"][0], f32), (2, 0, 1)))
    shared = {
        "pcol": pcol, "bsb": bsb, "wsT": wsT,
        "a_w_in": np.ascontiguousarray(inputs["a_w_in"][0], dtype=f32),
        "a_w_out": np.ascontiguousarray(inputs["a_w_out"][0], dtype=f32),
        "b_w_q_a": np.ascontiguousarray(inputs["b_w_q_a"][0], dtype=f32),
        "b_w_q_b": np.ascontiguousarray(inputs["b_w_q_b"][0], dtype=f32),
        "b_w_o": np.ascontiguousarray(inputs["b_w_o"][0], dtype=f32),
        "kv_w_a": np.ascontiguousarray(inputs["kv_w_a"], dtype=f32),
        "kv_w_b": np.ascontiguousarray(inputs["kv_w_b"], dtype=f32),
        "mlp_w1": np.ascontiguousarray(inputs["mlp_w1"], dtype=f32),
        "mlp_w2": np.ascontiguousarray(inputs["mlp_w2"], dtype=f32),
    }
    return shared


def kernel(**inputs):
    x = np.asarray(inputs["x"], np.float32)
    pos = np.asarray(inputs["positions"], np.int32)
    B = x.shape[0]
    shared = _pack_inputs(inputs)
    in_maps = []
    for b in range(B):
        m = dict(shared)
        m["xT"] = np.ascontiguousarray(x[b].T)
        m["posb"] = np.ascontiguousarray(np.broadcast_to(pos[b][None, :], (128, T)))
        in_maps.append(m)
    if "nc" not in _NC_CACHE:
        _NC_CACHE["nc"] = build_program()
    res = run_bass_kernel_spmd(_NC_CACHE["nc"], in_maps, core_ids=list(range(B)))
    out = np.empty((B, T, D), np.float32)
    for b in range(B):
        out[b] = res.results[b]["outT"].T
    return out
```

```python
import contextlib
import numpy as np
import concourse.bass as bass
import concourse.mybir as mybir
from concourse.bass_utils import run_bass_kernel_spmd

F32 = mybir.dt.float32
BF16 = mybir.dt.bfloat16
I32 = mybir.dt.int32
AF = mybir.ActivationFunctionType
ALU = mybir.AluOpType

D = 1024
T = 2048
KC = 8
NTG = 4
EPS = 1e-6
GATE = 2048
DFF = 4096
QL = 384
KVL = 256
NH = 8
SCALE = 192.0 ** -0.5
PI = float(np.pi)
TWO_PI = float(2 * np.pi)

ENGS = ("pe", "act", "dve", "pool", "sp")

C_MIX0, C_MLP0, C_KVSRC, C_MIX1, C_MLP1, C_FIN = 0, 8, 16, 24, 32, 40
C_QG, C_KVAG, C_LNG, C_FREQ, C_LNB, C_SIGN, NCOL = 48, 51, 53, 69, 70, 86, 87


class Op:
    __slots__ = ("eng", "fn", "deps", "dma_sem", "token", "needed")

    def __init__(self, eng, fn, deps, dma_sem):
        self.eng = eng
        self.fn = fn
        self.deps = deps
        self.dma_sem = dma_sem
        self.token = None
        self.needed = False


class Prog:
    def __init__(self, nc):
        self.nc = nc
        self.ops = {e: [] for e in ENGS}
        self.last_writer = {}
        self.readers = {}
        self.dma_counts = {}
        self.pending_barrier = {}

    def add(self, eng, fn, reads=(), writes=(), dma=None, dma_val=None):
        deps = []
        for r in reads:
            w = self.last_writer.get(r)
            if w is not None:
                deps.append(w)
        for w_ in writes:
            rd = self.readers.get(w_)
            if rd:
                deps.extend(rd.values())
            w = self.last_writer.get(w_)
            if w is not None:
                deps.append(w)
        pb = self.pending_barrier.pop(eng, None)
        if pb:
            deps.extend(pb)
        op = Op(eng, fn, deps, dma)
        if dma is not None:
            c = self.dma_counts.get(dma, 0) + 16
            self.dma_counts[dma] = c
            op.token = (("dma", dma), c if dma_val is None else dma_val)
        self.ops[eng].append(op)
        for r in reads:
            d = self.readers.setdefault(r, {})
            d[eng if dma is None else ("dma", dma)] = op
        for w_ in writes:
            self.last_writer[w_] = op
            self.readers[w_] = {}
        return op

    def barrier(self):
        tails = []
        for e in ENGS:
            last_c = None
            last_d = {}
            for op in self.ops[e]:
                if op.dma_sem is None:
                    last_c = op
                else:
                    last_d[op.dma_sem] = op
            if last_c is not None:
                tails.append(last_c)
            tails.extend(last_d.values())
        for e in ENGS:
            self.pending_barrier[e] = list(tails)
        self.last_writer = {}
        self.readers = {}

    def emit(self):
        nc = self.nc
        for e in ENGS:
            for op in self.ops[e]:
                for d in op.deps:
                    if d.dma_sem is None:
                        if d.eng == "pe" and op.eng == "pe" and op.dma_sem is None:
                            continue
                        d.needed = True
        for e in ENGS:
            c = 0
            for op in self.ops[e]:
                if op.dma_sem is None and op.needed:
                    c += 1
                    op.token = (("eng", e), c)
        with contextlib.ExitStack() as st:
            sems = {}
            for e in ENGS:
                sems[("eng", e)] = st.enter_context(nc.semaphore("s_" + e))
            for name in self.dma_counts:
                sems[("dma", name)] = st.enter_context(nc.semaphore("d_" + str(name)))
            block = st.enter_context(nc.Block())
            engobj = {"pe": block.tensor, "act": block.scalar, "dve": block.vector,
                      "pool": block.gpsimd, "sp": block.sync}

            def make(e):
                def body(eng):
                    seen = {}
                    for op in self.ops[e]:
                        need = {}
                        for d in op.deps:
                            if d.token is None:
                                continue
                            if (d.dma_sem is None and d.eng == "pe" and e == "pe"
                                    and op.dma_sem is None):
                                continue
                            k, v = d.token
                            if v > need.get(k, 0):
                                need[k] = v
                        for k, v in need.items():
                            if seen.get(k, 0) >= v:
                                continue
                            seen[k] = v
                            eng.wait_ge(sems[k], v)
                        ins = op.fn(eng)
                        if op.dma_sem is not None:
                            ins.then_inc(sems[("dma", op.dma_sem)], 16)
                        elif op.needed:
                            ins.then_inc(sems[op.token[0]], 1)
                    fin = {}
                    for op in self.ops[e]:
                        if op.dma_sem is not None:
                            k, v = op.token
                            fin[k] = max(fin.get(k, 0), v)
                    for k, v in fin.items():
                        eng.wait_ge(sems[k], v)
                return body

            for e in ENGS:
                if self.ops[e]:
                    engobj[e](make(e))


class Arena:
    def __init__(self, nc, nbytes):
        self.t = nc.alloc_sbuf_tensor("arena", [128, nbytes // 4], F32).ap()
        self.nbytes = nbytes
        self.off = 0
        self.peak = 0

    def alloc(self, shape, dtype):
        esz = 2 if dtype == BF16 else 4
        n = 1
        for s in shape:
            n *= s
        nb = (n * esz + 31) // 32 * 32
        assert self.off + nb <= self.nbytes, ("SBUF arena overflow", self.off, nb, self.nbytes)
        a = self.t[:, self.off // 4:(self.off + nb) // 4]
        if dtype != F32:
            a = a.bitcast(dtype)
        a = a[:, 0:n]
        if len(shape) == 2:
            a = a.rearrange("p (a b) -> p a b", a=shape[0])
        elif len(shape) == 3:
            a = a.rearrange("p (a b c) -> p a b c", a=shape[0], b=shape[1])
        self.off += nb
        self.peak = max(self.peak, self.off)
        return a

    def mark(self):
        return self.off

    def reset(self, m):
        self.off = m


class WStream:
    def __init__(self, P, bufs, plan=None):
        self.P = P
        self.bufs = bufs
        self.nb = len(bufs)
        self.plan = plan
        self.reqs = []
        self.issued = 0

    def _issue(self, i):
        src, shape = self.plan[i]
        b = i % self.nb
        dst = self.view(b, shape)
        self.P.add("pool", lambda e, dst=dst, src=src: e.dma_start(out=dst, in_=src),
                   writes=[("wbuf", b)], dma=("w", b))

    def view(self, b, shape):
        n = 1
        for s in shape:
            n *= s
        a = self.bufs[b][:, 0:n]
        if len(shape) == 2:
            a = a.rearrange("p (a b) -> p a b", a=shape[0])
        return a

    def get(self, src, shape):
        i = len(self.reqs)
        self.reqs.append((src, shape))
        if self.plan is not None:
            while self.issued < min(len(self.plan), i + self.nb - 1):
                self._issue(self.issued)
                self.issued += 1
        b = i % self.nb
        return self.view(b, shape), ("wbuf", b)


class _Stop(Exception):
    pass


def build_program(debug=False, stop=None):
    nc = bass.Bass("TRN2", target_bir_lowering=False)
    dr = {}

    def din(name, shape, dt=F32):
        dr[name] = nc.dram_tensor(name, list(shape), dt, kind="ExternalInput").ap()
        return dr[name]

    xT = din("xT", [D, T])
    posb = din("posb", [128, T], I32)
    pcol_d = din("pcol", [128, NCOL])
    bsb_d = din("bsb", [128, 8, 128])
    wsT_d = din("wsT", [128, 8, 128])
    a_w_in = din("a_w_in", [D, 4096])
    a_w_out = din("a_w_out", [GATE, D])
    w_q_a = din("b_w_q_a", [D, QL])
    w_q_b = din("b_w_q_b", [QL, 1536])
    w_o = din("b_w_o", [D, D])
    kv_w_a = din("kv_w_a", [D, 320])
    kv_w_b = din("kv_w_b", [KVL, 2048])
    mlp_w1 = din("mlp_w1", [2, D, DFF])
    mlp_w2 = din("mlp_w2", [2, DFF, D])
    outT = nc.dram_tensor("outT", [D, T], F32, kind="ExternalOutput").ap()
    dbg = None
    if debug:
        dbg = nc.dram_tensor("dbg", [6, D, T], F32, kind="ExternalOutput").ap()

    ARENA_BYTES = 206 * 1024
    arena = Arena(nc, ARENA_BYTES)
    psum = [nc.alloc_psum_tensor("ps%d" % i, [128, 512], F32).ap() for i in range(8)]

    def emit_all(P, plan):
        holder = {}
        try:
            _emit_body(P, plan, holder)
        except _Stop:
            pass
        return holder["W"]

    def _emit_body(P, plan, holder):
        arena.off = 0
        hT = arena.alloc([KC, T], F32)
        pcol = arena.alloc([NCOL], F32)
        ones_bf = arena.alloc([128], BF16)
        wbufs = [arena.alloc([4096], BF16) for _ in range(4)]
        W = WStream(P, wbufs, plan)
        holder["W"] = W
        rstd = [arena.alloc([512], F32) for _ in range(2)]
        tmpf = [arena.alloc([512], F32) for _ in range(3)]
        pi_col = arena.alloc([8], F32)
        base_mark = arena.mark()
        rope_mark = base_mark
        cur = {}
        tabs = {}

        def col(c0, k):
            return pcol[:, c0 + k:c0 + k + 1]

        state = {"ps": 0, "rstd": 0}

        def ps_next(banks=range(8)):
            banks = list(banks)
            b = banks[state["ps"] % len(banks)]
            state["ps"] += 1
            return psum[b], ("ps", b)

        def mm_group(out_ap, pairs, reads, psreg):
            def fn(e, pairs=pairs, out_ap=out_ap):
                n = len(pairs)
                ins = None
                for i, (l, r) in enumerate(pairs):
                    ins = e.matmul(out_ap, lhsT=l, rhs=r, start=(i == 0), stop=(i == n - 1))
                return ins
            P.add("pe", fn, reads=reads, writes=[psreg])

        P.add("sp", lambda e: e.dma_start(out=pcol, in_=pcol_d), writes=["pcol"], dma="pcol")
        xTv = xT.rearrange("(k p) t -> p k t", p=128)
        for tq in range(NTG):
            P.add("sp", lambda e, tq=tq: e.dma_start(out=hT[:, :, tq * 512:(tq + 1) * 512],
                                                     in_=xTv[:, :, tq * 512:(tq + 1) * 512]),
                  writes=[("hT", kc, tq) for kc in range(KC)], dma=("x", tq))
        P.add("dve", lambda e: e.memset(ones_bf, 1.0), writes=["ones"])
        P.add("dve", lambda e: e.memset(pi_col, PI), writes=["pi_col"])

        def stop_here(name):
            if stop == name:
                for kc in range(KC):
                    P.add("sp", lambda e, kc=kc: e.dma_start(out=outT[kc * 128:(kc + 1) * 128, :],
                                                             in_=hT[:, kc, :]),
                          reads=[("hT", kc, tg) for tg in range(NTG)], dma="outstop", dma_val=16 * 8)
                raise _Stop()

        def dump(slot):
            if not debug:
                return
            for kc in range(KC):
                P.add("sp", lambda e, kc=kc: e.dma_start(out=dbg[slot, kc * 128:(kc + 1) * 128, :],
                                                         in_=hT[:, kc, :]),
                      reads=[("hT", kc, tg) for tg in range(NTG)], dma=("dbg", slot), dma_val=16 * 8)

        def rmsnorm_tg(src_fn, nchunks, dim, gcol0, dst_fn, tg, src_regs, dst_regs,
                       banks=range(8)):
            rr = rms_stats(src_fn, nchunks, dim, tg, src_regs, banks)
            rms_apply(rr, src_fn, nchunks, gcol0, dst_fn, tg, src_regs, dst_regs)

        def rms_stats(src_fn, nchunks, dim, tg, src_regs, banks=range(8)):
            sq = cur["sq"]
            if nchunks == KC and src_fn is None:
                P.add("act", lambda e, tg=tg, sq=sq: e.activation(
                    out=sq, in_=hT[:, :, tg * 512:(tg + 1) * 512], func=AF.Square),
                    reads=src_regs, writes=[("sq", k) for k in range(KC)])
            else:
                for k in range(nchunks):
                    P.add("act", lambda e, k=k, sq=sq: e.activation(out=sq[:, k, :], in_=src_fn(k),
                                                                    func=AF.Square),
                          reads=src_regs, writes=[("sq", k)])
            ps, psreg = ps_next(banks)
            mm_group(ps, [(ones_bf, sq[:, k, :]) for k in range(nchunks)],
                     ["ones"] + [("sq", k) for k in range(nchunks)], psreg)
            ri = state["rstd"] % 2
            state["rstd"] += 1
            r = rstd[ri]
            rreg = ("rstd", ri)
            P.add("dve", lambda e, r=r, ps=ps: e.tensor_scalar(
                out=r, in0=ps, scalar1=1.0 / dim, scalar2=EPS, op0=ALU.mult, op1=ALU.add),
                reads=[psreg], writes=[rreg, psreg])
            P.add("act", lambda e, r=r: e.activation(out=r, in_=r, func=AF.Sqrt),
                  reads=[rreg], writes=[rreg])
            P.add("dve", lambda e, r=r: e.reciprocal(out=r, in_=r), reads=[rreg], writes=[rreg])
            return r, rreg

        def rms_apply(rr, src_fn, nchunks, gcol0, dst_fn, tg, src_regs, dst_regs):
            r, rreg = rr
            for k in range(nchunks):
                s_ap = hT[:, k, tg * 512:(tg + 1) * 512] if src_fn is None else src_fn(k)
                P.add("dve", lambda e, k=k, s_ap=s_ap, r=r: e.scalar_tensor_tensor(
                    out=dst_fn(k), in0=s_ap, scalar=col(gcol0, k), in1=r,
                    op0=ALU.mult, op1=ALU.mult),
                    reads=src_regs + [rreg, "pcol"], writes=[dst_regs[k]])

        C1 = 6.28125
        C2 = TWO_PI - 6.28125

        def emit_rope_chunk(c, rp_i, rp_a, rp_t, rp_s):
            def reduce(shift):
                P.add("dve", lambda e: e.tensor_scalar(out=rp_t, in0=rp_a, scalar1=shift,
                                                       scalar2=1.0 / TWO_PI, op0=ALU.add, op1=ALU.mult),
                      reads=["rp_a"], writes=["rp_t"])
                P.add("dve", lambda e: e.tensor_copy(out=rp_i, in_=rp_t), reads=["rp_t"], writes=["rp_i"])
                P.add("dve", lambda e: e.tensor_copy(out=rp_t, in_=rp_i), reads=["rp_i"], writes=["rp_t"])
                P.add("dve", lambda e: e.scalar_tensor_tensor(out=rp_s, in0=rp_t, scalar=-C1, in1=rp_a,
                                                              op0=ALU.mult, op1=ALU.add),
                      reads=["rp_t", "rp_a"], writes=["rp_s"])
                P.add("dve", lambda e: e.scalar_tensor_tensor(out=rp_s, in0=rp_t, scalar=-C2, in1=rp_s,
                                                              op0=ALU.mult, op1=ALU.add),
                      reads=["rp_t", "rp_s"], writes=["rp_s"])
                if shift:
                    P.add("dve", lambda e: e.tensor_scalar(out=rp_s, in0=rp_s, scalar1=shift, scalar2=None,
                                                           op0=ALU.add),
                          reads=["rp_s"], writes=["rp_s"])
                P.add("dve", lambda e: e.tensor_scalar(out=rp_t, in0=rp_s, scalar1=PI, scalar2=TWO_PI,
                                                       op0=ALU.is_gt, op1=ALU.mult),
                      reads=["rp_s"], writes=["rp_t"])
                P.add("dve", lambda e: e.tensor_tensor(out=rp_s, in0=rp_s, in1=rp_t, op=ALU.subtract),
                      reads=["rp_s", "rp_t"], writes=["rp_s"])

            if True:
                csl = slice(c * 512, (c + 1) * 512)
                P.add("sp", lambda e, csl=csl: e.dma_start(out=rp_i, in_=posb[:, csl]),
                      writes=["rp_i"], dma="pos")
                P.add("dve", lambda e: e.tensor_copy(out=rp_a, in_=rp_i), reads=["rp_i"], writes=["rp_a"])
                P.add("dve", lambda e: e.tensor_scalar(out=rp_a, in0=rp_a, scalar1=col(C_FREQ, 0),
                                                       scalar2=None, op0=ALU.mult),
                      reads=["rp_a", "pcol"], writes=["rp_a"])
                reduce(0.0)
                P.add("act", lambda e: e.activation(out=rp_s, in_=rp_s, func=AF.Sin),
                      reads=["rp_s"], writes=["rp_s"])
                P.add("dve", lambda e, csl=csl: e.tensor_scalar(
                    out=tabs['TX'][0:64, csl], in0=rp_s[0:64, :], scalar1=pcol[0:64, C_SIGN:C_SIGN + 1],
                    scalar2=None, op0=ALU.mult),
                    reads=["rp_s", "pcol"], writes=[("TX", c)])
                P.add("dve", lambda e, csl=csl: e.tensor_scalar(
                    out=tabs['TQ'][64:128, csl], in0=rp_s[64:128, :], scalar1=pcol[64:128, C_SIGN:C_SIGN + 1],
                    scalar2=None, op0=ALU.mult),
                    reads=["rp_s", "pcol"], writes=[("TQ", c)])
                reduce(PI / 2)
                P.add("act", lambda e, csl=csl: e.activation(out=tabs['TQ'][0:64, csl], in_=rp_s[0:64, :], func=AF.Sin),
                      reads=["rp_s"], writes=[("TQ", c)])
                P.add("act", lambda e, csl=csl: e.activation(out=tabs['TX'][64:128, csl], in_=rp_s[64:128, :], func=AF.Sin),
                      reads=["rp_s", ("TQ", c)], writes=[("TX", c)])
            if debug and c == NTG - 1:
                P.add("sp", lambda e: e.dma_start(out=dbg[4, 0:128, :], in_=tabs['TQ']), reads=[("TQ", c) for c in range(NTG)], dma="dbgc")
                P.add("sp", lambda e: e.dma_start(out=dbg[5, 0:128, :], in_=tabs['TX']), reads=[("TX", c) for c in range(NTG)], dma="dbgs")

        cur["sq"] = arena.alloc([KC, 512], BF16)
        hn1 = arena.alloc([KC, 512], BF16)
        v_bf = arena.alloc([4, GATE], BF16)
        gateT = arena.alloc([16, 512], BF16)
        uT_all = arena.alloc([16, 512], BF16)
        B_T = arena.alloc([16, 128], F32)
        wsT_bf = arena.alloc([8, 128], BF16)
        stats = arena.alloc([4, 4, 6], F32)
        mv = arena.alloc([4, 2], F32)
        nmr = arena.alloc([4, 2], F32)
        m1b = arena.mark()
        wsT_f = arena.alloc([8, 128], F32)
        rs_bc = arena.alloc([8, 128], F32)
        bs_bc = arena.alloc([8, 128], F32)

        P.add("sp", lambda e: e.dma_start(out=wsT_f, in_=wsT_d), writes=["wsT_f"], dma="wsT")
        P.add("sp", lambda e: e.dma_start(out=bs_bc, in_=bsb_d), writes=["bs_bc"], dma="bsb")
        P.add("dve", lambda e: e.memset(wsT_f[64:128, :, 0:64], 0.0), reads=["wsT_f"], writes=["wsT_f"])
        P.add("dve", lambda e: e.tensor_copy(out=wsT_bf, in_=wsT_f), reads=["wsT_f"], writes=["wsT_bf"])
        for half in range(2):
            ps, psreg = ps_next()
            mm_group(ps, [(ones_bf, wsT_bf[:, half * 4:(half + 1) * 4, :])], ["ones", "wsT_bf"], psreg)
            P.add("dve", lambda e, ps=ps, half=half: e.tensor_copy(
                out=rs_bc[:, half * 4:(half + 1) * 4, :], in_=ps.rearrange("p (a b) -> p a b", a=4)),
                reads=[psreg], writes=["rs_bc", psreg])
        for cc in range(16):
            g = cc // 2
            P.add("dve", lambda e, cc=cc, g=g: e.scalar_tensor_tensor(
                out=B_T[:, cc, :], in0=rs_bc[:, g, :], scalar=col(C_LNB, cc), in1=bs_bc[:, g, :],
                op0=ALU.mult, op1=ALU.add),
                reads=["rs_bc", "bs_bc", "pcol"], writes=["B_T"])
        stop_here("gsetup")

        w_in_v = a_w_in.rearrange("(k p) c -> p k c", p=128)
        w_out_v = a_w_out.rearrange("(k p) c -> p k c", p=128)
        hn_regs = [("hn1", k) for k in range(KC)]
        rr_next = None
        for tp in range(4):
            hregs = [("hT", kc, tp) for kc in range(KC)]
            rr = rr_next if rr_next is not None else rms_stats(None, KC, D, tp, hregs)
            rr_next = None
            rms_apply(rr, None, KC, C_MIX0, lambda k: hn1[:, k, :], tp, hregs, hn_regs)
            for n in range(4):
                wv, wreg = W.get(w_in_v[:, :, 2048 + n * 512:2048 + (n + 1) * 512], [KC, 512])
                for tt in range(4):
                    ps, psreg = ps_next()
                    mm_group(ps, [(hn1[:, k, tt * 128:(tt + 1) * 128], wv[:, k, :]) for k in range(KC)],
                             hn_regs + [wreg], psreg)
                    P.add("act", lambda e, ps=ps, tt=tt, n=n: e.activation(
                        out=v_bf[:, tt, n * 512:(n + 1) * 512], in_=ps, func=AF.Gelu),
                        reads=[psreg], writes=[("v", tt), psreg])
            if tp == 0:
                stop_here("gv")
            for tt in range(4):
                for n in range(4):
                    P.add("dve", lambda e, tt=tt, n=n: e.bn_stats(
                        out=stats[:, tt, n, :], in_=v_bf[:, tt, n * 512:(n + 1) * 512]),
                        reads=[("v", tt)], writes=[("stats", tt)])
                P.add("dve", lambda e, tt=tt: e.bn_aggr(
                    out=mv[:, tt, :], in_=stats[:, tt, :, :].rearrange("p a b -> p (a b)")),
                    reads=[("stats", tt)], writes=[("mv", tt)])
            mvr = [("mv", tt) for tt in range(4)]
            P.add("dve", lambda e: e.tensor_scalar(
                out=nmr[:, :, 1:2], in0=mv[:, :, 1:2], scalar1=EPS, scalar2=None, op0=ALU.add),
                reads=mvr, writes=["nmr"])
            P.add("act", lambda e: e.activation(out=nmr[:, :, 1:2], in_=nmr[:, :, 1:2], func=AF.Sqrt),
                  reads=["nmr"], writes=["nmr"])
            P.add("dve", lambda e: e.reciprocal(out=nmr[:, :, 1:2], in_=nmr[:, :, 1:2]),
                  reads=["nmr"], writes=["nmr"])
            for tt in range(4):
                P.add("dve", lambda e, tt=tt: e.tensor_scalar(
                    out=v_bf[:, tt, :], in0=v_bf[:, tt, :], scalar1=mv[:, tt, 0:1],
                    scalar2=nmr[:, tt, 1:2], op0=ALU.subtract, op1=ALU.mult),
                    reads=[("v", tt), ("mv", tt), "nmr"], writes=[("v", tt)])
            if tp == 0:
                stop_here("gln")
            ublk = {}

            def emit_u(cc):
                n, c4 = cc // 4, cc % 4
                if c4 == 0:
                    ublk[n] = W.get(w_in_v[:, :, n * 512:(n + 1) * 512], [KC, 512])
                wu, wreg = ublk[n]
                ps_u, pr_u = ps_next()
                mm_group(ps_u, [(wu[:, k, c4 * 128:(c4 + 1) * 128], hn1[:, k, :]) for k in range(KC)],
                         hn_regs + [wreg], pr_u)
                P.add("act", lambda e, ps_u=ps_u, cc=cc: e.activation(out=uT_all[:, cc, :], in_=ps_u,
                                                                      func=AF.Gelu),
                      reads=[pr_u], writes=[("uT", cc), pr_u])

            def emit_gate(cc):
                g = cc // 2
                ps_s, pr_s = ps_next()

                def fn(e, ps_s=ps_s, cc=cc, g=g):
                    ins = None
                    for tt in range(4):
                        ins = e.matmul(ps_s[:, tt * 128:(tt + 1) * 128],
                                       lhsT=v_bf[:, tt, cc * 128:(cc + 1) * 128],
                                       rhs=wsT_bf[:, g, :], start=True, stop=True)
                    return ins
                P.add("pe", fn, reads=[("v", tt) for tt in range(4)] + ["wsT_bf"], writes=[pr_s])
                tf = tmpf[cc % 2]
                treg = ("tmpf", cc % 2)
                P.add("dve", lambda e, ps_s=ps_s, tf=tf, cc=cc: e.scalar_tensor_tensor(
                    out=tf.rearrange("p (a b) -> p a b", a=4),
                    in0=ps_s.rearrange("p (a b) -> p a b", a=4),
                    scalar=col(C_LNG, cc),
                    in1=B_T[:, cc:cc + 1, :].broadcast_to([128, 4, 128]),
                    op0=ALU.mult, op1=ALU.add),
                    reads=[pr_s, "B_T", "pcol"], writes=[treg, pr_s])
                P.add("dve", lambda e, tf=tf, cc=cc: e.tensor_tensor(
                    out=gateT[:, cc, :], in0=tf, in1=uT_all[:, cc, :], op=ALU.mult),
                    reads=[treg, ("uT", cc)], writes=[("gate", cc)])

            AHEAD = 10
            for cc in range(AHEAD):
                emit_u(cc)
            if tp < 3:
                rr_next = rms_stats(None, KC, D, tp + 1, [("hT", kc, tp + 1) for kc in range(KC)])
            for cc in range(16):
                emit_gate(cc)
                if cc + AHEAD < 16:
                    emit_u(cc + AHEAD)
            if tp == 0:
                stop_here("gu")
            for dq in range(4):
                wo, wreg = W.get(w_out_v[:, :, dq * 256:(dq + 1) * 256], [16, 256])
                for d2 in range(2):
                    dc = dq * 2 + d2
                    ps, psreg = ps_next()
                    mm_group(ps, [(wo[:, cc, d2 * 128:(d2 + 1) * 128], gateT[:, cc, :]) for cc in range(16)],
                             [("gate", cc) for cc in range(16)] + [wreg], psreg)
                    P.add("dve", lambda e, ps=ps, dc=dc, tp=tp: e.tensor_tensor(
                        out=hT[:, dc, tp * 512:(tp + 1) * 512], in0=ps,
                        in1=hT[:, dc, tp * 512:(tp + 1) * 512], op=ALU.add),
                        reads=[psreg, ("hT", dc, tp)], writes=[("hT", dc, tp), psreg])
        dump(0)
        stop_here("gmlp")
        P.barrier()
        arena.reset(base_mark)
        tabs["TQ"] = arena.alloc([T], F32)
        tabs["TX"] = arena.alloc([T], F32)

        def mlp(layer, gcol0, final=False, rope_filler=False):
            m = arena.mark()
            cur["sq"] = arena.alloc([KC, 512], BF16)
            hnT = arena.alloc([KC, T], BF16)
            hidT = arena.alloc([KC, T], BF16)
            ob = arena.alloc([KC, 512], F32) if final else None
            if rope_filler:
                rp = [arena.alloc([512], I32)] + [arena.alloc([512], F32) for _ in range(3)]
            outv = outT.rearrange("(k p) t -> p k t", p=128)
            w1v = mlp_w1[layer].rearrange("(k p) c -> p k c", p=128)
            w2v = mlp_w2[layer].rearrange("(k p) c -> p k c", p=128)
            tcnt = [0]

            def h_group(w1b, wreg, f4, fc, tg):
                ps, psreg = ps_next()
                mm_group(ps, [(w1b[:, k, f4 * 128:(f4 + 1) * 128],
                               hnT[:, k, tg * 512:(tg + 1) * 512]) for k in range(KC)],
                         [("hnT", k, tg) for k in range(KC)] + [wreg], psreg)
                tf = tmpf[tcnt[0] % 3]
                treg = ("tmpf", tcnt[0] % 3)
                tcnt[0] += 1
                P.add("act", lambda e, ps=ps, tf=tf: e.activation(out=tf, in_=ps, func=AF.Relu),
                      reads=[psreg], writes=[treg, psreg])
                P.add("dve", lambda e, tf=tf, fc=fc, tg=tg: e.tensor_tensor(
                    out=hidT[:, fc, tg * 512:(tg + 1) * 512], in0=tf, in1=tf, op=ALU.mult),
                    reads=[treg], writes=[("hid", fc, tg)])

            def hn_regs_of(tg):
                return [("hT", kc, tg) for kc in range(KC)]

            rr = rms_stats(None, KC, D, 0, hn_regs_of(0))
            for fq in range(4):
                if rope_filler:
                    emit_rope_chunk(fq, *rp)
                for n in range(2):
                    w1b, wreg = W.get(w1v[:, :, fq * 1024 + n * 512:fq * 1024 + (n + 1) * 512], [KC, 512])
                    if fq == 0 and n == 0:
                        for tg in range(NTG):
                            rms_apply(rr, None, KC, gcol0,
                                      lambda k, tg=tg: hnT[:, k, tg * 512:(tg + 1) * 512], tg,
                                      hn_regs_of(tg), [("hnT", k, tg) for k in range(KC)])
                            if tg + 1 < NTG:
                                rr = rms_stats(None, KC, D, tg + 1, hn_regs_of(tg + 1))
                            for f4 in range(4):
                                h_group(w1b, wreg, f4, f4, tg)
                    else:
                        for f4 in range(4):
                            for tg in range(NTG):
                                h_group(w1b, wreg, f4, n * 4 + f4, tg)
                def o_group(w2b, wreg, d4, dc, tg):
                    ps, psreg = ps_next()
                    mm_group(ps, [(w2b[:, fc, d4 * 128:(d4 + 1) * 128],
                                   hidT[:, fc, tg * 512:(tg + 1) * 512]) for fc in range(8)],
                             [("hid", fc, tg) for fc in range(8)] + [wreg], psreg)
                    P.add("dve", lambda e, ps=ps, dc=dc, tg=tg: e.tensor_tensor(
                        out=hT[:, dc, tg * 512:(tg + 1) * 512], in0=ps,
                        in1=hT[:, dc, tg * 512:(tg + 1) * 512], op=ALU.add),
                        reads=[psreg, ("hT", dc, tg)], writes=[("hT", dc, tg), psreg])

                if final and fq == 3:
                    blks = [W.get(w2v[:, fq * 8:(fq + 1) * 8, dh * 512:(dh + 1) * 512], [8, 512])
                            for dh in range(2)]
                    for tg in range(NTG):
                        for dc in range(KC):
                            w2b, wreg = blks[dc // 4]
                            o_group(w2b, wreg, dc % 4, dc, tg)
                        if debug:
                            pass
                        rmsnorm_tg(None, KC, D, C_FIN, lambda k: ob[:, k, :], tg,
                                   [("hT", kc, tg) for kc in range(KC)], [("ob", k) for k in range(KC)])
                        P.add("sp", lambda e, tg=tg: e.dma_start(
                            out=outv[:, :, tg * 512:(tg + 1) * 512], in_=ob),
                            reads=[("ob", k) for k in range(KC)], dma="out")
                else:
                    for dh in range(2):
                        w2b, wreg = W.get(w2v[:, fq * 8:(fq + 1) * 8, dh * 512:(dh + 1) * 512], [8, 512])
                        for d4 in range(4):
                            for tg in range(NTG):
                                o_group(w2b, wreg, d4, dh * 4 + d4, tg)
            P.barrier()
            arena.reset(m)

        mlp(0, C_MLP0, rope_filler=True)
        dump(1)
        stop_here("mlp0")

        ckvT = arena.alloc([2, T], BF16)
        kpeT = arena.alloc([T], BF16)
        cqT = arena.alloc([3, T], BF16)
        m3 = arena.mark()
        cur["sq"] = arena.alloc([KC, 512], BF16)
        hnT = arena.alloc([KC, T], BF16)
        raw2 = [arena.alloc([3, 512], F32) for _ in range(2)]
        kva_a = arena.alloc([KC, 128], BF16)
        kva_b = arena.alloc([KC, 128], BF16)

        for tg in range(NTG):
            rmsnorm_tg(None, KC, D, C_KVSRC, lambda k, tg=tg: hnT[:, k, tg * 512:(tg + 1) * 512], tg,
                       [("hT", kc, tg) for kc in range(KC)], [("hnT", k, tg) for k in range(KC)])
        wa, wareg = W.get(kv_w_a.rearrange("(k p) c -> p k c", p=128), [KC, 320])
        for j, (dst_t, c_lo, src_lo, w) in enumerate([
                (kva_a, 0, 256, 64), (kva_a, 64, 256, 64),
                (kva_b, 0, 288, 32), (kva_b, 32, 256, 32), (kva_b, 64, 288, 32), (kva_b, 96, 256, 32)]):
            P.add("dve", lambda e, dst_t=dst_t, c_lo=c_lo, src_lo=src_lo, w=w: e.tensor_copy(
                out=dst_t[:, :, c_lo:c_lo + w], in_=wa[:, :, src_lo:src_lo + w]),
                reads=[wareg], writes=[("kva", j)])
        kva_regs = [("kva", j) for j in range(6)]

        def kv_norm(tg):
            tsl = slice(tg * 512, (tg + 1) * 512)
            rb = raw2[tg % 2]
            rmsnorm_tg(lambda k: rb[:, k, :], 2, KVL, C_KVAG,
                       lambda k: ckvT[:, k, tsl], tg,
                       [("raw", tg % 2, 0), ("raw", tg % 2, 1)], [("ckvT", k, tg) for k in range(2)])

        def kv_mm(tg):
            tsl = slice(tg * 512, (tg + 1) * 512)
            hreads = [("hnT", k, tg) for k in range(KC)]
            rb = raw2[tg % 2]
            for c2 in range(2):
                ps, psreg = ps_next()
                mm_group(ps, [(wa[:, k, c2 * 128:(c2 + 1) * 128], hnT[:, k, tsl]) for k in range(KC)],
                         hreads + [wareg], psreg)
                P.add("act", lambda e, ps=ps, c2=c2, rb=rb: e.activation(out=rb[:, c2, :], in_=ps, func=AF.Copy),
                      reads=[psreg], writes=[("raw", tg % 2, c2), psreg])
            ps_a, pr_a = ps_next()
            mm_group(ps_a, [(kva_a[:, k, :], hnT[:, k, tsl]) for k in range(KC)], hreads + kva_regs, pr_a)
            ps_b, pr_b = ps_next()
            mm_group(ps_b, [(kva_b[:, k, :], hnT[:, k, tsl]) for k in range(KC)], hreads + kva_regs, pr_b)
            tq, tx = ("TQ", tg), ("TX", tg)
            P.add("dve", lambda e, ps_a=ps_a, tsl=tsl: e.tensor_tensor(
                out=tmpf[0][0:64, :], in0=ps_a[0:64, :], in1=tabs['TQ'][0:64, tsl], op=ALU.mult),
                reads=[pr_a, tq], writes=[("tmpf", 0), pr_a])
            P.add("dve", lambda e, ps_a=ps_a, tsl=tsl: e.tensor_tensor(
                out=tmpf[0][64:128, :], in0=ps_a[64:128, :], in1=tabs['TX'][64:128, tsl], op=ALU.mult),
                reads=[pr_a, tx, ("tmpf", 0)], writes=[("tmpf", 0), pr_a])
            P.add("dve", lambda e, ps_b=ps_b, tsl=tsl: e.tensor_tensor(
                out=tmpf[1][0:64, :], in0=ps_b[0:64, :], in1=tabs['TX'][0:64, tsl], op=ALU.mult),
                reads=[pr_b, tx], writes=[("tmpf", 1), pr_b])
            P.add("dve", lambda e, ps_b=ps_b, tsl=tsl: e.tensor_tensor(
                out=tmpf[1][64:128, :], in0=ps_b[64:128, :], in1=tabs['TQ'][64:128, tsl], op=ALU.mult),
                reads=[pr_b, tq, ("tmpf", 1)], writes=[("tmpf", 1), pr_b])
            P.add("dve", lambda e, tsl=tsl: e.tensor_tensor(
                out=kpeT[:, tsl], in0=tmpf[0], in1=tmpf[1], op=ALU.add),
                reads=[("tmpf", 0), ("tmpf", 1)], writes=[("kpeT", tg)])

        kv_mm(0)
        for tg in range(NTG):
            if tg + 1 < NTG:
                kv_mm(tg + 1)
            kv_norm(tg)

        stop_here("kv")
        for tg in range(NTG):
            rmsnorm_tg(None, KC, D, C_MIX1, lambda k, tg=tg: hnT[:, k, tg * 512:(tg + 1) * 512], tg,
                       [("hT", kc, tg) for kc in range(KC)], [("hnT", k, tg) for k in range(KC)])
        wqa, wqareg = W.get(w_q_a.rearrange("(k p) c -> p k c", p=128), [KC, QL])
        def cq_norm(tg):
            tsl = slice(tg * 512, (tg + 1) * 512)
            rb = raw2[tg % 2]
            rmsnorm_tg(lambda k: rb[:, k, :], 3, QL, C_QG,
                       lambda k: cqT[:, k, tsl], tg, [("raw", tg % 2, c) for c in range(3)],
                       [("cqT", k, tg) for k in range(3)])

        def cq_mm(tg):
            tsl = slice(tg * 512, (tg + 1) * 512)
            rb = raw2[tg % 2]
            raws = [rb[:, 0, :], rb[:, 1, :], rb[:, 2, :]]
            rregs = [("raw", tg % 2, c) for c in range(3)]
            for c3 in range(3):
                ps, psreg = ps_next()
                mm_group(ps, [(wqa[:, k, c3 * 128:(c3 + 1) * 128], hnT[:, k, tsl]) for k in range(KC)],
                         [("hnT", k, tg) for k in range(KC)] + [wqareg], psreg)
                P.add("act", lambda e, ps=ps, c3=c3, raws=raws: e.activation(out=raws[c3], in_=ps, func=AF.Copy),
                      reads=[psreg], writes=[rregs[c3], psreg])

        cq_mm(0)
        for tg in range(NTG):
            if tg + 1 < NTG:
                cq_mm(tg + 1)
            cq_norm(tg)
        P.barrier()
        arena.reset(m3)
        oT_all = arena.alloc([NH, T], BF16)
        kT_h = arena.alloc([T], BF16)
        v_h = arena.alloc([16, 128], BF16)
        qT_h = arena.alloc([T], BF16)
        qpe_h = arena.alloc([T], BF16)
        wq_cat = arena.alloc([3, 128], BF16)
        pT = [arena.alloc([512], BF16) for _ in range(4)]
        rcp = [arena.alloc([512], F32) for _ in range(2)]

        wqb_v = w_q_b.rearrange("(k p) c -> p k c", p=128)
        wkvb_v = kv_w_b.rearrange("(k p) c -> p k c", p=128)
        PROJ_BANKS = [7, 0, 1, 2]
        SC_BANKS = [0, 1, 2]
        cp_i = [0]

        def evac_copy(dst, ps, psreg, dreg):
            cp_i[0] += 1
            if cp_i[0] % 2 == 0:
                P.add("act", lambda e: e.activation(out=dst, in_=ps, func=AF.Copy),
                      reads=[psreg], writes=[dreg, psreg])
            else:
                P.add("dve", lambda e: e.tensor_copy(out=dst, in_=ps), reads=[psreg], writes=[dreg, psreg])

        for h in range(NH):
            wkb, wkbreg = W.get(wkvb_v[:, :, h * 256:(h + 1) * 256], [2, 256])
            wqb, wqbreg = W.get(wqb_v[:, :, h * 192:(h + 1) * 192], [3, 192])
            for j, (c_lo, src_lo, w) in enumerate([(0, 128, 64), (64, 160, 32), (96, 128, 32)]):
                P.add("dve", lambda e, wqb=wqb, c_lo=c_lo, src_lo=src_lo, w=w: e.tensor_copy(
                    out=wq_cat[:, :, c_lo:c_lo + w], in_=wqb[:, :, src_lo:src_lo + w]),
                    reads=[wqbreg], writes=[("wq_cat", j)])
            wqc_regs = [("wq_cat", j) for j in range(3)]
            for tg in range(NTG):
                tsl = slice(tg * 512, (tg + 1) * 512)
                ckr = [("ckvT", k, tg) for k in range(2)]
                cqr = [("cqT", k, tg) for k in range(3)]
                ps, psreg = ps_next(PROJ_BANKS)
                mm_group(ps, [(wkb[:, k, 0:128], ckvT[:, k, tsl]) for k in range(2)], ckr + [wkbreg], psreg)
                evac_copy(kT_h[:, tsl], ps, psreg, ("kT_h", tg))
                ps, psreg = ps_next(PROJ_BANKS)

                def fn(e, ps=ps, tg=tg, wkb=wkb):
                    ins = None
                    for j in range(4):
                        kt = tg * 4 + j
                        for k in range(2):
                            ins = e.matmul(ps[:, j * 128:(j + 1) * 128],
                                           lhsT=ckvT[:, k, kt * 128:(kt + 1) * 128],
                                           rhs=wkb[:, k, 128:256], start=(k == 0), stop=(k == 1))
                    return ins
                P.add("pe", fn, reads=ckr + [wkbreg], writes=[psreg])
                evac_copy(v_h[:, tg * 4:(tg + 1) * 4, :], ps.rearrange("p (a b) -> p a b", a=4), psreg,
                          ("v_h", tg))
                ps, psreg = ps_next(PROJ_BANKS)
                mm_group(ps, [(wqb[:, k, 0:128], cqT[:, k, tsl]) for k in range(3)], cqr + [wqbreg], psreg)
                evac_copy(qT_h[:, tsl], ps, psreg, ("qT_h", tg))
                ps_a, pr_a = ps_next(PROJ_BANKS)
                mm_group(ps_a, [(wq_cat[:, k, :], cqT[:, k, tsl]) for k in range(3)], cqr + wqc_regs, pr_a)
                P.add("dve", lambda e, ps_a=ps_a, tsl=tsl: e.tensor_tensor(
                    out=qpe_h[:, tsl], in0=ps_a, in1=tabs['TQ'][:, tsl], op=ALU.mult),
                    reads=[pr_a, ("TQ", tg)], writes=[("qpe_h", tg), pr_a])

            tiles = [(qg, kt) for qg in range(NTG) for kt in range(4 * qg + 4)]
            sc = {}

            def emit_qk(i):
                qg, kt = tiles[i]
                m = max(0, kt - 4 * qg)
                c0 = 128 * m
                b = SC_BANKS[i % 3]
                ps, psreg = psum[b], ("ps", b)
                qsl = slice(qg * 512 + c0, (qg + 1) * 512)
                ksl = slice(kt * 128, (kt + 1) * 128)

                def fn(e, ps=ps, c0=c0, qsl=qsl, ksl=ksl):
                    e.matmul(ps[:, c0:512], lhsT=kT_h[:, ksl], rhs=qT_h[:, qsl], start=True, stop=False)
                    return e.matmul(ps[:, c0:512], lhsT=kpeT[:, ksl], rhs=qpe_h[:, qsl],
                                    start=False, stop=True)
                P.add("pe", fn, reads=[("kT_h", kt // 4), ("qT_h", qg), ("kpeT", kt // 4), ("qpe_h", qg)],
                      writes=[psreg])
                sc[i] = (ps, psreg, c0)

            def emit_rest(i):
                qg, kt = tiles[i]
                ps, psreg, c0 = sc.pop(i)
                pb = pT[i % 4]
                preg = ("pT", i % 4)
                P.add("act", lambda e, ps=ps, pb=pb, c0=c0: e.activation(
                    out=pb[:, c0:512], in_=ps[:, c0:512], func=AF.Exp, scale=SCALE),
                    reads=[psreg], writes=[preg, psreg])
                if kt >= 4 * qg:
                    P.add("pool", lambda e, pb=pb, c0=c0: e.memset(pb[64:128, c0:c0 + 64], 0.0),
                          reads=[preg], writes=[preg])
                first = (kt == 0)
                last = (kt == 4 * qg + 3)
                po, poreg = psum[3 + qg % 2], ("ps", 3 + qg % 2)
                pd, pdreg = psum[5 + qg % 2], ("ps", 5 + qg % 2)

                def fn(e, pb=pb, c0=c0, kt=kt, po=po, pd=pd, first=first, last=last):
                    e.matmul(po[:, c0:512], lhsT=v_h[:, kt, :], rhs=pb[:, c0:512], start=first, stop=last)
                    return e.matmul(pd[:, c0:512], lhsT=ones_bf, rhs=pb[:, c0:512], start=first, stop=last)
                P.add("pe", fn, reads=[preg, ("v_h", kt // 4), "ones"], writes=[poreg, pdreg])
                if last:
                    r = rcp[qg % 2]
                    rreg = ("rcp", qg % 2)
                    P.add("dve", lambda e, r=r, pd=pd: e.reciprocal(out=r, in_=pd),
                          reads=[pdreg], writes=[rreg, pdreg])
                    P.add("dve", lambda e, r=r, po=po, qg=qg, h=h: e.tensor_tensor(
                        out=oT_all[:, h, qg * 512:(qg + 1) * 512], in0=po, in1=r, op=ALU.mult),
                        reads=[poreg, rreg], writes=[("oT", h, qg), poreg])

            n = len(tiles)
            emit_qk(0)
            emit_qk(1)
            for i in range(n):
                if i + 2 < n:
                    emit_qk(i + 2)
                emit_rest(i)

        wov = w_o.rearrange("(k p) c -> p k c", p=128)
        for dh in range(2):
            wob, wreg = W.get(wov[:, :, dh * 512:(dh + 1) * 512], [NH, 512])
            for d4 in range(4):
                dc = dh * 4 + d4
                for tg in range(NTG):
                    ps, psreg = ps_next()
                    mm_group(ps, [(wob[:, hh, d4 * 128:(d4 + 1) * 128], oT_all[:, hh, tg * 512:(tg + 1) * 512])
                                  for hh in range(NH)],
                             [("oT", hh, tg) for hh in range(NH)] + [wreg], psreg)
                    P.add("dve", lambda e, ps=ps, dc=dc, tg=tg: e.tensor_tensor(
                        out=hT[:, dc, tg * 512:(tg + 1) * 512], in0=ps,
                        in1=hT[:, dc, tg * 512:(tg + 1) * 512], op=ALU.add),
                        reads=[psreg, ("hT", dc, tg)], writes=[("hT", dc, tg), psreg])
        dump(2)
        stop_here("mla")
        P.barrier()
        arena.reset(base_mark)

        mlp(1, C_MLP1, final=True)
        dump(3)

    P0 = Prog(nc)
    W0 = emit_all(P0, None)
    plan = list(W0.reqs)
    P = Prog(nc)
    emit_all(P, plan)
    P.emit()
    return nc


_NC_CACHE = {}


def _pack_inputs(inputs):
    f32 = np.float32

    def colv(v):
        v = np.asarray(v, f32)
        return np.ascontiguousarray(v.reshape(-1, 128).T)

    pcol = np.zeros((128, NCOL), f32)
    pcol[:, C_MIX0:C_MIX0 + 8] = colv(inputs["norm_mix_g"][0])
    pcol[:, C_MLP0:C_MLP0 + 8] = colv(inputs["norm_mlp_g"][0])
    pcol[:, C_KVSRC:C_KVSRC + 8] = colv(inputs["kv_src_norm_g"])
    pcol[:, C_MIX1:C_MIX1 + 8] = colv(inputs["norm_mix_g"][1])
    pcol[:, C_MLP1:C_MLP1 + 8] = colv(inputs["norm_mlp_g"][1])
    pcol[:, C_FIN:C_FIN + 8] = colv(inputs["final_norm_g"])
    pcol[:, C_QG:C_QG + 3] = colv(inputs["b_q_norm_g"][0])
    pcol[:, C_KVAG:C_KVAG + 2] = colv(inputs["kv_a_norm_g"])
    pcol[:, C_LNG:C_LNG + 16] = colv(inputs["a_ln_v_g"][0])
    inv_freq = (np.float32(10000.0) ** (-np.arange(0, 64, 2, dtype=np.float32) / np.float32(64))).astype(f32)
    for q in range(4):
        pcol[32 * q:32 * (q + 1), C_FREQ] = inv_freq
        pcol[32 * q:32 * (q + 1), C_SIGN] = -1.0 if q % 2 == 0 else 1.0
    pcol[:, C_LNB:C_LNB + 16] = colv(inputs["a_ln_v_b"][0])
    bsb = np.ascontiguousarray(np.broadcast_to(np.asarray(inputs["a_b_s"][0], f32)[None], (128, 8, 128)))
    wsT = np.ascontiguousarray(np.transpose(np.asarray(inputs["a_w_s"][0], f32), (2, 0, 1)))
    shared = {
        "pcol": pcol, "bsb": bsb, "wsT": wsT,
        "a_w_in": np.ascontiguousarray(inputs["a_w_in"][0], dtype=f32),
        "a_w_out": np.ascontiguousarray(inputs["a_w_out"][0], dtype=f32),
        "b_w_q_a": np.ascontiguousarray(inputs["b_w_q_a"][0], dtype=f32),
        "b_w_q_b": np.ascontiguousarray(inputs["b_w_q_b"][0], dtype=f32),
        "b_w_o": np.ascontiguousarray(inputs["b_w_o"][0], dtype=f32),
        "kv_w_a": np.ascontiguousarray(inputs["kv_w_a"], dtype=f32),
        "kv_w_b": np.ascontiguousarray(inputs["kv_w_b"], dtype=f32),
        "mlp_w1": np.ascontiguousarray(inputs["mlp_w1"], dtype=f32),
        "mlp_w2": np.ascontiguousarray(inputs["mlp_w2"], dtype=f32),
    }
    return shared


def kernel(**inputs):
    x = np.asarray(inputs["x"], np.float32)
    pos = np.asarray(inputs["positions"], np.int32)
    B = x.shape[0]
    shared = _pack_inputs(inputs)
    in_maps = []
    for b in range(B):
        m = dict(shared)
        m["xT"] = np.ascontiguousarray(x[b].T)
        m["posb"] = np.ascontiguousarray(np.broadcast_to(pos[b][None, :], (128, T)))
        in_maps.append(m)
    if "nc" not in _NC_CACHE:
        _NC_CACHE["nc"] = build_program()
    res = run_bass_kernel_spmd(_NC_CACHE["nc"], in_maps, core_ids=list(range(B)))
    out = np.empty((B, T, D), np.float32)
    for b in range(B):
        out[b] = res.results[b]["outT"].T
    return out
```

```python
import contextlib
import numpy as np
import concourse.bass as bass
import concourse.mybir as mybir
from concourse.bass_utils import run_bass_kernel_spmd

F32 = mybir.dt.float32
BF16 = mybir.dt.bfloat16
I32 = mybir.dt.int32
AF = mybir.ActivationFunctionType
ALU = mybir.AluOpType

D = 1024
T = 2048
KC = 8
NTG = 4
EPS = 1e-6
GATE = 2048
DFF = 4096
QL = 384
KVL = 256
NH = 8
SCALE = 192.0 ** -0.5
PI = float(np.pi)
TWO_PI = float(2 * np.pi)

ENGS = ("pe", "act", "dve", "pool", "sp")

C_MIX0, C_MLP0, C_KVSRC, C_MIX1, C_MLP1, C_FIN = 0, 8, 16, 24, 32, 40
C_QG, C_KVAG, C_LNG, C_FREQ, C_LNB, C_SIGN, NCOL = 48, 51, 53, 69, 70, 86, 87


class Op:
    __slots__ = ("eng", "fn", "deps", "dma_sem", "token", "needed")

    def __init__(self, eng, fn, deps, dma_sem):
        self.eng = eng
        self.fn = fn
        self.deps = deps
        self.dma_sem = dma_sem
        self.token = None
        self.needed = False


class Prog:
    def __init__(self, nc):
        self.nc = nc
        self.ops = {e: [] for e in ENGS}
        self.last_writer = {}
        self.readers = {}
        self.dma_counts = {}
        self.pending_barrier = {}

    def add(self, eng, fn, reads=(), writes=(), dma=None, dma_val=None):
        deps = []
        for r in reads:
            w = self.last_writer.get(r)
            if w is not None:
                deps.append(w)
        for w_ in writes:
            rd = self.readers.get(w_)
            if rd:
                deps.extend(rd.values())
            w = self.last_writer.get(w_)
            if w is not None:
                deps.append(w)
        pb = self.pending_barrier.pop(eng, None)
        if pb:
            deps.extend(pb)
        op = Op(eng, fn, deps, dma)
        if dma is not None:
            c = self.dma_counts.get(dma, 0) + 16
            self.dma_counts[dma] = c
            op.token = (("dma", dma), c if dma_val is None else dma_val)
        self.ops[eng].append(op)
        for r in reads:
            d = self.readers.setdefault(r, {})
            d[eng if dma is None else ("dma", dma)] = op
        for w_ in writes:
            self.last_writer[w_] = op
            self.readers[w_] = {}
        return op

    def barrier(self):
        tails = []
        for e in ENGS:
            last_c = None
            last_d = {}
            for op in self.ops[e]:
                if op.dma_sem is None:
                    last_c = op
                else:
                    last_d[op.dma_sem] = op
            if last_c is not None:
                tails.append(last_c)
            tails.extend(last_d.values())
        for e in ENGS:
            self.pending_barrier[e] = list(tails)
        self.last_writer = {}
        self.readers = {}

    def emit(self):
        nc = self.nc
        for e in ENGS:
            for op in self.ops[e]:
                for d in op.deps:
                    if d.dma_sem is None:
                        if d.eng == "pe" and op.eng == "pe" and op.dma_sem is None:
                            continue
                        d.needed = True
        for e in ENGS:
            c = 0
            for op in self.ops[e]:
                if op.dma_sem is None and op.needed:
                    c += 1
                    op.token = (("eng", e), c)
        with contextlib.ExitStack() as st:
            sems = {}
            for e in ENGS:
                sems[("eng", e)] = st.enter_context(nc.semaphore("s_" + e))
            for name in self.dma_counts:
                sems[("dma", name)] = st.enter_context(nc.semaphore("d_" + str(name)))
            block = st.enter_context(nc.Block())
            engobj = {"pe": block.tensor, "act": block.scalar, "dve": block.vector,
                      "pool": block.gpsimd, "sp": block.sync}

            def make(e):
                def body(eng):
                    seen = {}
                    for op in self.ops[e]:
                        need = {}
                        for d in op.deps:
                            if d.token is None:
                                continue
                            if (d.dma_sem is None and d.eng == "pe" and e == "pe"
                                    and op.dma_sem is None):
                                continue
                            k, v = d.token
                            if v > need.get(k, 0):
                                need[k] = v
                        for k, v in need.items():
                            if seen.get(k, 0) >= v:
                                continue
                            seen[k] = v
                            eng.wait_ge(sems[k], v)
                        ins = op.fn(eng)
                        if op.dma_sem is not None:
                            ins.then_inc(sems[("dma", op.dma_sem)], 16)
                        elif op.needed:
                            ins.then_inc(sems[op.token[0]], 1)
                    fin = {}
                    for op in self.ops[e]:
                        if op.dma_sem is not None:
                            k, v = op.token
                            fin[k] = max(fin.get(k, 0), v)
                    for k, v in fin.items():
                        eng.wait_ge(sems[k], v)
                return body

            for e in ENGS:
                if self.ops[e]:
                    engobj[e](make(e))


class Arena:
    def __init__(self, nc, nbytes):
        self.t = nc.alloc_sbuf_tensor("arena", [128, nbytes // 4], F32).ap()
        self.nbytes = nbytes
        self.off = 0
        self.peak = 0

    def alloc(self, shape, dtype):
        esz = 2 if dtype == BF16 else 4
        n = 1
        for s in shape:
            n *= s
        nb = (n * esz + 31) // 32 * 32
        assert self.off + nb <= self.nbytes, ("SBUF arena overflow", self.off, nb, self.nbytes)
        a = self.t[:, self.off // 4:(self.off + nb) // 4]
        if dtype != F32:
            a = a.bitcast(dtype)
        a = a[:, 0:n]
        if len(shape) == 2:
            a = a.rearrange("p (a b) -> p a b", a=shape[0])
        elif len(shape) == 3:
            a = a.rearrange("p (a b c) -> p a b c", a=shape[0], b=shape[1])
        self.off += nb
        self.peak = max(self.peak, self.off)
        return a

    def mark(self):
        return self.off

    def reset(self, m):
        self.off = m


class WStream:
    def __init__(self, P, bufs, plan=None):
        self.P = P
        self.bufs = bufs
        self.nb = len(bufs)
        self.plan = plan
        self.reqs = []
        self.issued = 0

    def _issue(self, i):
        src, shape = self.plan[i]
        b = i % self.nb
        dst = self.view(b, shape)
        self.P.add("pool", lambda e, dst=dst, src=src: e.dma_start(out=dst, in_=src),
                   writes=[("wbuf", b)], dma=("w", b))

    def view(self, b, shape):
        n = 1
        for s in shape:
            n *= s
        a = self.bufs[b][:, 0:n]
        if len(shape) == 2:
            a = a.rearrange("p (a b) -> p a b", a=shape[0])
        return a

    def get(self, src, shape):
        i = len(self.reqs)
        self.reqs.append((src, shape))
        if self.plan is not None:
            while self.issued < min(len(self.plan), i + self.nb - 1):
                self._issue(self.issued)
                self.issued += 1
        b = i % self.nb
        return self.view(b, shape), ("wbuf", b)


class _Stop(Exception):
    pass


def build_program(debug=False, stop=None):
    nc = bass.Bass("TRN2", target_bir_lowering=False)
    dr = {}

    def din(name, shape, dt=F32):
        dr[name] = nc.dram_tensor(name, list(shape), dt, kind="ExternalInput").ap()
        return dr[name]

    xT = din("xT", [D, T])
    posb = din("posb", [128, T], I32)
    pcol_d = din("pcol", [128, NCOL])
    bsb_d = din("bsb", [128, 8, 128])
    wsT_d = din("wsT", [128, 8, 128])
    a_w_in = din("a_w_in", [D, 4096])
    a_w_out = din("a_w_out", [GATE, D])
    w_q_a = din("b_w_q_a", [D, QL])
    w_q_b = din("b_w_q_b", [QL, 1536])
    w_o = din("b_w_o", [D, D])
    kv_w_a = din("kv_w_a", [D, 320])
    kv_w_b = din("kv_w_b", [KVL, 2048])
    mlp_w1 = din("mlp_w1", [2, D, DFF])
    mlp_w2 = din("mlp_w2", [2, DFF, D])
    outT = nc.dram_tensor("outT", [D, T], F32, kind="ExternalOutput").ap()
    dbg = None
    if debug:
        dbg = nc.dram_tensor("dbg", [6, D, T], F32, kind="ExternalOutput").ap()

    ARENA_BYTES = 206 * 1024
    arena = Arena(nc, ARENA_BYTES)
    psum = [nc.alloc_psum_tensor("ps%d" % i, [128, 512], F32).ap() for i in range(8)]

    def emit_all(P, plan):
        holder = {}
        try:
            _emit_body(P, plan, holder)
        except _Stop:
            pass
        return holder["W"]

    def _emit_body(P, plan, holder):
        arena.off = 0
        hT = arena.alloc([KC, T], F32)
        pcol = arena.alloc([NCOL], F32)
        ones_bf = arena.alloc([128], BF16)
        wbufs = [arena.alloc([4096], BF16) for _ in range(4)]
        W = WStream(P, wbufs, plan)
        holder["W"] = W
        rstd = [arena.alloc([512], F32) for _ in range(2)]
        tmpf = [arena.alloc([512], F32) for _ in range(3)]
        pi_col = arena.alloc([8], F32)
        base_mark = arena.mark()
        rope_mark = base_mark
        cur = {}
        tabs = {}

        def col(c0, k):
            return pcol[:, c0 + k:c0 + k + 1]

        state = {"ps": 0, "rstd": 0}

        def ps_next(banks=range(8)):
            banks = list(banks)
            b = banks[state["ps"] % len(banks)]
            state["ps"] += 1
            return psum[b], ("ps", b)

        def mm_group(out_ap, pairs, reads, psreg):
            def fn(e, pairs=pairs, out_ap=out_ap):
                n = len(pairs)
                ins = None
                for i, (l, r) in enumerate(pairs):
                    ins = e.matmul(out_ap, lhsT=l, rhs=r, start=(i == 0), stop=(i == n - 1))
                return ins
            P.add("pe", fn, reads=reads, writes=[psreg])

        P.add("sp", lambda e: e.dma_start(out=pcol, in_=pcol_d), writes=["pcol"], dma="pcol")
        xTv = xT.rearrange("(k p) t -> p k t", p=128)
        for tq in range(NTG):
            P.add("sp", lambda e, tq=tq: e.dma_start(out=hT[:, :, tq * 512:(tq + 1) * 512],
                                                     in_=xTv[:, :, tq * 512:(tq + 1) * 512]),
                  writes=[("hT", kc, tq) for kc in range(KC)], dma=("x", tq))
        P.add("dve", lambda e: e.memset(ones_bf, 1.0), writes=["ones"])
        P.add("dve", lambda e: e.memset(pi_col, PI), writes=["pi_col"])

        def stop_here(name):
            if stop == name:
                for kc in range(KC):
                    P.add("sp", lambda e, kc=kc: e.dma_start(out=outT[kc * 128:(kc + 1) * 128, :],
                                                             in_=hT[:, kc, :]),
                          reads=[("hT", kc, tg) for tg in range(NTG)], dma="outstop", dma_val=16 * 8)
                raise _Stop()

        def dump(slot):
            if not debug:
                return
            for kc in range(KC):
                P.add("sp", lambda e, kc=kc: e.dma_start(out=dbg[slot, kc * 128:(kc + 1) * 128, :],
                                                         in_=hT[:, kc, :]),
                      reads=[("hT", kc, tg) for tg in range(NTG)], dma=("dbg", slot), dma_val=16 * 8)

        def rmsnorm_tg(src_fn, nchunks, dim, gcol0, dst_fn, tg, src_regs, dst_regs,
                       banks=range(8)):
            rr = rms_stats(src_fn, nchunks, dim, tg, src_regs, banks)
            rms_apply(rr, src_fn, nchunks, gcol0, dst_fn, tg, src_regs, dst_regs)

        def rms_stats(src_fn, nchunks, dim, tg, src_regs, banks=range(8)):
            sq = cur["sq"]
            if nchunks == KC and src_fn is None:
                P.add("act", lambda e, tg=tg, sq=sq: e.activation(
                    out=sq, in_=hT[:, :, tg * 512:(tg + 1) * 512], func=AF.Square),
                    reads=src_regs, writes=[("sq", k) for k in range(KC)])
            else:
                for k in range(nchunks):
                    P.add("act", lambda e, k=k, sq=sq: e.activation(out=sq[:, k, :], in_=src_fn(k),
                                                                    func=AF.Square),
                          reads=src_regs, writes=[("sq", k)])
            ps, psreg = ps_next(banks)
            mm_group(ps, [(ones_bf, sq[:, k, :]) for k in range(nchunks)],
                     ["ones"] + [("sq", k) for k in range(nchunks)], psreg)
            ri = state["rstd"] % 2
            state["rstd"] += 1
            r = rstd[ri]
            rreg = ("rstd", ri)
            P.add("dve", lambda e, r=r, ps=ps: e.tensor_scalar(
                out=r, in0=ps, scalar1=1.0 / dim, scalar2=EPS, op0=ALU.mult, op1=ALU.add),
                reads=[psreg], writes=[rreg, psreg])
            P.add("act", lambda e, r=r: e.activation(out=r, in_=r, func=AF.Sqrt),
                  reads=[rreg], writes=[rreg])
            P.add("dve", lambda e, r=r: e.reciprocal(out=r, in_=r), reads=[rreg], writes=[rreg])
            return r, rreg

        def rms_apply(rr, src_fn, nchunks, gcol0, dst_fn, tg, src_regs, dst_regs):
            r, rreg = rr
            for k in range(nchunks):
                s_ap = hT[:, k, tg * 512:(tg + 1) * 512] if src_fn is None else src_fn(k)
                P.add("dve", lambda e, k=k, s_ap=s_ap, r=r: e.scalar_tensor_tensor(
                    out=dst_fn(k), in0=s_ap, scalar=col(gcol0, k), in1=r,
                    op0=ALU.mult, op1=ALU.mult),
                    reads=src_regs + [rreg, "pcol"], writes=[dst_regs[k]])

        C1 = 6.28125
        C2 = TWO_PI - 6.28125

        def emit_rope_chunk(c, rp_i, rp_a, rp_t, rp_s):
            def reduce(shift):
                P.add("dve", lambda e: e.tensor_scalar(out=rp_t, in0=rp_a, scalar1=shift,
                                                       scalar2=1.0 / TWO_PI, op0=ALU.add, op1=ALU.mult),
                      reads=["rp_a"], writes=["rp_t"])
                P.add("dve", lambda e: e.tensor_copy(out=rp_i, in_=rp_t), reads=["rp_t"], writes=["rp_i"])
                P.add("dve", lambda e: e.tensor_copy(out=rp_t, in_=rp_i), reads=["rp_i"], writes=["rp_t"])
                P.add("dve", lambda e: e.scalar_tensor_tensor(out=rp_s, in0=rp_t, scalar=-C1, in1=rp_a,
                                                              op0=ALU.mult, op1=ALU.add),
                      reads=["rp_t", "rp_a"], writes=["rp_s"])
                P.add("dve", lambda e: e.scalar_tensor_tensor(out=rp_s, in0=rp_t, scalar=-C2, in1=rp_s,
                                                              op0=ALU.mult, op1=ALU.add),
                      reads=["rp_t", "rp_s"], writes=["rp_s"])
                if shift:
                    P.add("dve", lambda e: e.tensor_scalar(out=rp_s, in0=rp_s, scalar1=shift, scalar2=None,
                                                           op0=ALU.add),
                          reads=["rp_s"], writes=["rp_s"])
                P.add("dve", lambda e: e.tensor_scalar(out=rp_t, in0=rp_s, scalar1=PI, scalar2=TWO_PI,
                                                       op0=ALU.is_gt, op1=ALU.mult),
                      reads=["rp_s"], writes=["rp_t"])
                P.add("dve", lambda e: e.tensor_tensor(out=rp_s, in0=rp_s, in1=rp_t, op=ALU.subtract),
                      reads=["rp_s", "rp_t"], writes=["rp_s"])

            if True:
                csl = slice(c * 512, (c + 1) * 512)
                P.add("sp", lambda e, csl=csl: e.dma_start(out=rp_i, in_=posb[:, csl]),
                      writes=["rp_i"], dma="pos")
                P.add("dve", lambda e: e.tensor_copy(out=rp_a, in_=rp_i), reads=["rp_i"], writes=["rp_a"])
                P.add("dve", lambda e: e.tensor_scalar(out=rp_a, in0=rp_a, scalar1=col(C_FREQ, 0),
                                                       scalar2=None, op0=ALU.mult),
                      reads=["rp_a", "pcol"], writes=["rp_a"])
                reduce(0.0)
                P.add("act", lambda e: e.activation(out=rp_s, in_=rp_s, func=AF.Sin),
                      reads=["rp_s"], writes=["rp_s"])
                P.add("dve", lambda e, csl=csl: e.tensor_scalar(
                    out=tabs['TX'][0:64, csl], in0=rp_s[0:64, :], scalar1=pcol[0:64, C_SIGN:C_SIGN + 1],
                    scalar2=None, op0=ALU.mult),
                    reads=["rp_s", "pcol"], writes=[("TX", c)])
                P.add("dve", lambda e, csl=csl: e.tensor_scalar(
                    out=tabs['TQ'][64:128, csl], in0=rp_s[64:128, :], scalar1=pcol[64:128, C_SIGN:C_SIGN + 1],
                    scalar2=None, op0=ALU.mult),
                    reads=["rp_s", "pcol"], writes=[("TQ", c)])
                reduce(PI / 2)
                P.add("act", lambda e, csl=csl: e.activation(out=tabs['TQ'][0:64, csl], in_=rp_s[0:64, :], func=AF.Sin),
                      reads=["rp_s"], writes=[("TQ", c)])
                P.add("act", lambda e, csl=csl: e.activation(out=tabs['TX'][64:128, csl], in_=rp_s[64:128, :], func=AF.Sin),
                      reads=["rp_s", ("TQ", c)], writes=[("TX", c)])
            if debug and c == NTG - 1:
                P.add("sp", lambda e: e.dma_start(out=dbg[4, 0:128, :], in_=tabs['TQ']), reads=[("TQ", c) for c in range(NTG)], dma="dbgc")
                P.add("sp", lambda e: e.dma_start(out=dbg[5, 0:128, :], in_=tabs['TX']), reads=[("TX", c) for c in range(NTG)], dma="dbgs")

        cur["sq"] = arena.alloc([KC, 512], BF16)
        hn1 = arena.alloc([KC, 512], BF16)
        v_bf = arena.alloc([4, GATE], BF16)
        gateT = arena.alloc([16, 512], BF16)
        uT_all = arena.alloc([16, 512], BF16)
        B_T = arena.alloc([16, 128], F32)
        wsT_bf = arena.alloc([8, 128], BF16)
        stats = arena.alloc([4, 4, 6], F32)
        mv = arena.alloc([4, 2], F32)
        nmr = arena.alloc([4, 2], F32)
        m1b = arena.mark()
        wsT_f = arena.alloc([8, 128], F32)
        rs_bc = arena.alloc([8, 128], F32)
        bs_bc = arena.alloc([8, 128], F32)

        P.add("sp", lambda e: e.dma_start(out=wsT_f, in_=wsT_d), writes=["wsT_f"], dma="wsT")
        P.add("sp", lambda e: e.dma_start(out=bs_bc, in_=bsb_d), writes=["bs_bc"], dma="bsb")
        P.add("dve", lambda e: e.memset(wsT_f[64:128, :, 0:64], 0.0), reads=["wsT_f"], writes=["wsT_f"])
        P.add("dve", lambda e: e.tensor_copy(out=wsT_bf, in_=wsT_f), reads=["wsT_f"], writes=["wsT_bf"])
        for half in range(2):
            ps, psreg = ps_next()
            mm_group(ps, [(ones_bf, wsT_bf[:, half * 4:(half + 1) * 4, :])], ["ones", "wsT_bf"], psreg)
            P.add("dve", lambda e, ps=ps, half=half: e.tensor_copy(
                out=rs_bc[:, half * 4:(half + 1) * 4, :], in_=ps.rearrange("p (a b) -> p a b", a=4)),
                reads=[psreg], writes=["rs_bc", psreg])
        for cc in range(16):
            g = cc // 2
            P.add("dve", lambda e, cc=cc, g=g: e.scalar_tensor_tensor(
                out=B_T[:, cc, :], in0=rs_bc[:, g, :], scalar=col(C_LNB, cc), in1=bs_bc[:, g, :],
                op0=ALU.mult, op1=ALU.add),
                reads=["rs_bc", "bs_bc", "pcol"], writes=["B_T"])
        stop_here("gsetup")

        w_in_v = a_w_in.rearrange("(k p) c -> p k c", p=128)
        w_out_v = a_w_out.rearrange("(k p) c -> p k c", p=128)
        hn_regs = [("hn1", k) for k in range(KC)]
        rr_next = None
        for tp in range(4):
            hregs = [("hT", kc, tp) for kc in range(KC)]
            rr = rr_next if rr_next is not None else rms_stats(None, KC, D, tp, hregs)
            rr_next = None
            rms_apply(rr, None, KC, C_MIX0, lambda k: hn1[:, k, :], tp, hregs, hn_regs)
            for n in range(4):
                wv, wreg = W.get(w_in_v[:, :, 2048 + n * 512:2048 + (n + 1) * 512], [KC, 512])
                for tt in range(4):
                    ps, psreg = ps_next()
                    mm_group(ps, [(hn1[:, k, tt * 128:(tt + 1) * 128], wv[:, k, :]) for k in range(KC)],
                             hn_regs + [wreg], psreg)
                    P.add("act", lambda e, ps=ps, tt=tt, n=n: e.activation(
                        out=v_bf[:, tt, n * 512:(n + 1) * 512], in_=ps, func=AF.Gelu),
                        reads=[psreg], writes=[("v", tt), psreg])
            if tp == 0:
                stop_here("gv")
            for tt in range(4):
                for n in range(4):
                    P.add("dve", lambda e, tt=tt, n=n: e.bn_stats(
                        out=stats[:, tt, n, :], in_=v_bf[:, tt, n * 512:(n + 1) * 512]),
                        reads=[("v", tt)], writes=[("stats", tt)])
                P.add("dve", lambda e, tt=tt: e.bn_aggr(
                    out=mv[:, tt, :], in_=stats[:, tt, :, :].rearrange("p a b -> p (a b)")),
                    reads=[("stats", tt)], writes=[("mv", tt)])
            mvr = [("mv", tt) for tt in range(4)]
            P.add("dve", lambda e: e.tensor_scalar(
                out=nmr[:, :, 1:2], in0=mv[:, :, 1:2], scalar1=EPS, scalar2=None, op0=ALU.add),
                reads=mvr, writes=["nmr"])
            P.add("act", lambda e: e.activation(out=nmr[:, :, 1:2], in_=nmr[:, :, 1:2], func=AF.Sqrt),
                  reads=["nmr"], writes=["nmr"])
            P.add("dve", lambda e: e.reciprocal(out=nmr[:, :, 1:2], in_=nmr[:, :, 1:2]),
                  reads=["nmr"], writes=["nmr"])
            for tt in range(4):
                P.add("dve", lambda e, tt=tt: e.tensor_scalar(
                    out=v_bf[:, tt, :], in0=v_bf[:, tt, :], scalar1=mv[:, tt, 0:1],
                    scalar2=nmr[:, tt, 1:2], op0=ALU.subtract, op1=ALU.mult),
                    reads=[("v", tt), ("mv", tt), "nmr"], writes=[("v", tt)])
            if tp == 0:
                stop_here("gln")
            ublk = {}

            def emit_u(cc):
                n, c4 = cc // 4, cc % 4
                if c4 == 0:
                    ublk[n] = W.get(w_in_v[:, :, n * 512:(n + 1) * 512], [KC, 512])
                wu, wreg = ublk[n]
                ps_u, pr_u = ps_next()
                mm_group(ps_u, [(wu[:, k, c4 * 128:(c4 + 1) * 128], hn1[:, k, :]) for k in range(KC)],
                         hn_regs + [wreg], pr_u)
                P.add("act", lambda e, ps_u=ps_u, cc=cc: e.activation(out=uT_all[:, cc, :], in_=ps_u,
                                                                      func=AF.Gelu),
                      reads=[pr_u], writes=[("uT", cc), pr_u])

            def emit_gate(cc):
                g = cc // 2
                ps_s, pr_s = ps_next()

                def fn(e, ps_s=ps_s, cc=cc, g=g):
                    ins = None
                    for tt in range(4):
                        ins = e.matmul(ps_s[:, tt * 128:(tt + 1) * 128],
                                       lhsT=v_bf[:, tt, cc * 128:(cc + 1) * 128],
                                       rhs=wsT_bf[:, g, :], start=True, stop=True)
                    return ins
                P.add("pe", fn, reads=[("v", tt) for tt in range(4)] + ["wsT_bf"], writes=[pr_s])
                tf = tmpf[cc % 2].bitcast(BF16)[:, 0:512]
                treg = ("tmpf", cc % 2)
                P.add("dve", lambda e, ps_s=ps_s, tf=tf, cc=cc: e.scalar_tensor_tensor(
                    out=tf.rearrange("p (a b) -> p a b", a=4),
                    in0=ps_s.rearrange("p (a b) -> p a b", a=4),
                    scalar=col(C_LNG, cc),
                    in1=B_T[:, cc:cc + 1, :].broadcast_to([128, 4, 128]),
                    op0=ALU.mult, op1=ALU.add),
                    reads=[pr_s, "B_T", "pcol"], writes=[treg, pr_s])
                P.add("dve", lambda e, tf=tf, cc=cc: e.tensor_tensor(
                    out=gateT[:, cc, :], in0=tf, in1=uT_all[:, cc, :], op=ALU.mult),
                    reads=[treg, ("uT", cc)], writes=[("gate", cc)])

            AHEAD = 10
            for cc in range(AHEAD):
                emit_u(cc)
            if tp < 3:
                rr_next = rms_stats(None, KC, D, tp + 1, [("hT", kc, tp + 1) for kc in range(KC)])
            for cc in range(16):
                emit_gate(cc)
                if cc + AHEAD < 16:
                    emit_u(cc + AHEAD)
            if tp == 0:
                stop_here("gu")
            for dq in range(4):
                wo, wreg = W.get(w_out_v[:, :, dq * 256:(dq + 1) * 256], [16, 256])
                for d2 in range(2):
                    dc = dq * 2 + d2
                    ps, psreg = ps_next()
                    mm_group(ps, [(wo[:, cc, d2 * 128:(d2 + 1) * 128], gateT[:, cc, :]) for cc in range(16)],
                             [("gate", cc) for cc in range(16)] + [wreg], psreg)
                    P.add("dve", lambda e, ps=ps, dc=dc, tp=tp: e.tensor_tensor(
                        out=hT[:, dc, tp * 512:(tp + 1) * 512], in0=ps,
                        in1=hT[:, dc, tp * 512:(tp + 1) * 512], op=ALU.add),
                        reads=[psreg, ("hT", dc, tp)], writes=[("hT", dc, tp), psreg])
        dump(0)
        stop_here("gmlp")
        P.barrier()
        arena.reset(base_mark)
        tabs["TQ"] = arena.alloc([T], F32)
        tabs["TX"] = arena.alloc([T], F32)

        def mlp(layer, gcol0, final=False, rope_filler=False):
            m = arena.mark()
            cur["sq"] = arena.alloc([KC, 512], BF16)
            hnT = arena.alloc([KC, T], BF16)
            hidT = arena.alloc([KC, T], BF16)
            ob = arena.alloc([KC, 512], F32) if final else None
            if rope_filler:
                rp = [arena.alloc([512], I32)] + [arena.alloc([512], F32) for _ in range(3)]
            outv = outT.rearrange("(k p) t -> p k t", p=128)
            w1v = mlp_w1[layer].rearrange("(k p) c -> p k c", p=128)
            w2v = mlp_w2[layer].rearrange("(k p) c -> p k c", p=128)
            tcnt = [0]

            def h_group(w1b, wreg, f4, fc, tg):
                ps, psreg = ps_next()
                mm_group(ps, [(w1b[:, k, f4 * 128:(f4 + 1) * 128],
                               hnT[:, k, tg * 512:(tg + 1) * 512]) for k in range(KC)],
                         [("hnT", k, tg) for k in range(KC)] + [wreg], psreg)
                tf = tmpf[tcnt[0] % 3]
                treg = ("tmpf", tcnt[0] % 3)
                tcnt[0] += 1
                P.add("act", lambda e, ps=ps, tf=tf: e.activation(out=tf, in_=ps, func=AF.Relu),
                      reads=[psreg], writes=[treg, psreg])
                P.add("dve", lambda e, tf=tf, fc=fc, tg=tg: e.tensor_tensor(
                    out=hidT[:, fc, tg * 512:(tg + 1) * 512], in0=tf, in1=tf, op=ALU.mult),
                    reads=[treg], writes=[("hid", fc, tg)])

            def hn_regs_of(tg):
                return [("hT", kc, tg) for kc in range(KC)]

            rr = rms_stats(None, KC, D, 0, hn_regs_of(0))
            for fq in range(4):
                if rope_filler:
                    emit_rope_chunk(fq, *rp)
                for n in range(2):
                    w1b, wreg = W.get(w1v[:, :, fq * 1024 + n * 512:fq * 1024 + (n + 1) * 512], [KC, 512])
                    if fq == 0 and n == 0:
                        for tg in range(NTG):
                            rms_apply(rr, None, KC, gcol0,
                                      lambda k, tg=tg: hnT[:, k, tg * 512:(tg + 1) * 512], tg,
                                      hn_regs_of(tg), [("hnT", k, tg) for k in range(KC)])
                            if tg + 1 < NTG:
                                rr = rms_stats(None, KC, D, tg + 1, hn_regs_of(tg + 1))
                            for f4 in range(4):
                                h_group(w1b, wreg, f4, f4, tg)
                    else:
                        for f4 in range(4):
                            for tg in range(NTG):
                                h_group(w1b, wreg, f4, n * 4 + f4, tg)
                def o_group(w2b, wreg, d4, dc, tg):
                    ps, psreg = ps_next()
                    mm_group(ps, [(w2b[:, fc, d4 * 128:(d4 + 1) * 128],
                                   hidT[:, fc, tg * 512:(tg + 1) * 512]) for fc in range(8)],
                             [("hid", fc, tg) for fc in range(8)] + [wreg], psreg)
                    P.add("dve", lambda e, ps=ps, dc=dc, tg=tg: e.tensor_tensor(
                        out=hT[:, dc, tg * 512:(tg + 1) * 512], in0=ps,
                        in1=hT[:, dc, tg * 512:(tg + 1) * 512], op=ALU.add),
                        reads=[psreg, ("hT", dc, tg)], writes=[("hT", dc, tg), psreg])

                if final and fq == 3:
                    blks = [W.get(w2v[:, fq * 8:(fq + 1) * 8, dh * 512:(dh + 1) * 512], [8, 512])
                            for dh in range(2)]
                    for tg in range(NTG):
                        for dc in range(KC):
                            w2b, wreg = blks[dc // 4]
                            o_group(w2b, wreg, dc % 4, dc, tg)
                        if debug:
                            pass
                        rmsnorm_tg(None, KC, D, C_FIN, lambda k: ob[:, k, :], tg,
                                   [("hT", kc, tg) for kc in range(KC)], [("ob", k) for k in range(KC)])
                        P.add("sp", lambda e, tg=tg: e.dma_start(
                            out=outv[:, :, tg * 512:(tg + 1) * 512], in_=ob),
                            reads=[("ob", k) for k in range(KC)], dma="out")
                else:
                    for dh in range(2):
                        w2b, wreg = W.get(w2v[:, fq * 8:(fq + 1) * 8, dh * 512:(dh + 1) * 512], [8, 512])
                        for d4 in range(4):
                            for tg in range(NTG):
                                o_group(w2b, wreg, d4, dh * 4 + d4, tg)
            P.barrier()
            arena.reset(m)

        mlp(0, C_MLP0, rope_filler=True)
        dump(1)
        stop_here("mlp0")

        ckvT = arena.alloc([2, T], BF16)
        kpeT = arena.alloc([T], BF16)
        cqT = arena.alloc([3, T], BF16)
        m3 = arena.mark()
        cur["sq"] = arena.alloc([KC, 512], BF16)
        hnT = arena.alloc([KC, T], BF16)
        raw2 = [arena.alloc([3, 512], F32) for _ in range(2)]
        kva_a = arena.alloc([KC, 128], BF16)
        kva_b = arena.alloc([KC, 128], BF16)

        for tg in range(NTG):
            rmsnorm_tg(None, KC, D, C_KVSRC, lambda k, tg=tg: hnT[:, k, tg * 512:(tg + 1) * 512], tg,
                       [("hT", kc, tg) for kc in range(KC)], [("hnT", k, tg) for k in range(KC)])
        wa, wareg = W.get(kv_w_a.rearrange("(k p) c -> p k c", p=128), [KC, 320])
        for j, (dst_t, c_lo, src_lo, w) in enumerate([
                (kva_a, 0, 256, 64), (kva_a, 64, 256, 64),
                (kva_b, 0, 288, 32), (kva_b, 32, 256, 32), (kva_b, 64, 288, 32), (kva_b, 96, 256, 32)]):
            P.add("dve", lambda e, dst_t=dst_t, c_lo=c_lo, src_lo=src_lo, w=w: e.tensor_copy(
                out=dst_t[:, :, c_lo:c_lo + w], in_=wa[:, :, src_lo:src_lo + w]),
                reads=[wareg], writes=[("kva", j)])
        kva_regs = [("kva", j) for j in range(6)]

        def kv_norm(tg):
            tsl = slice(tg * 512, (tg + 1) * 512)
            rb = raw2[tg % 2]
            rmsnorm_tg(lambda k: rb[:, k, :], 2, KVL, C_KVAG,
                       lambda k: ckvT[:, k, tsl], tg,
                       [("raw", tg % 2, 0), ("raw", tg % 2, 1)], [("ckvT", k, tg) for k in range(2)])

        def kv_mm(tg):
            tsl = slice(tg * 512, (tg + 1) * 512)
            hreads = [("hnT", k, tg) for k in range(KC)]
            rb = raw2[tg % 2]
            for c2 in range(2):
                ps, psreg = ps_next()
                mm_group(ps, [(wa[:, k, c2 * 128:(c2 + 1) * 128], hnT[:, k, tsl]) for k in range(KC)],
                         hreads + [wareg], psreg)
                P.add("act", lambda e, ps=ps, c2=c2, rb=rb: e.activation(out=rb[:, c2, :], in_=ps, func=AF.Copy),
                      reads=[psreg], writes=[("raw", tg % 2, c2), psreg])
            ps_a, pr_a = ps_next()
            mm_group(ps_a, [(kva_a[:, k, :], hnT[:, k, tsl]) for k in range(KC)], hreads + kva_regs, pr_a)
            ps_b, pr_b = ps_next()
            mm_group(ps_b, [(kva_b[:, k, :], hnT[:, k, tsl]) for k in range(KC)], hreads + kva_regs, pr_b)
            tq, tx = ("TQ", tg), ("TX", tg)
            P.add("dve", lambda e, ps_a=ps_a, tsl=tsl: e.tensor_tensor(
                out=tmpf[0][0:64, :], in0=ps_a[0:64, :], in1=tabs['TQ'][0:64, tsl], op=ALU.mult),
                reads=[pr_a, tq], writes=[("tmpf", 0), pr_a])
            P.add("dve", lambda e, ps_a=ps_a, tsl=tsl: e.tensor_tensor(
                out=tmpf[0][64:128, :], in0=ps_a[64:128, :], in1=tabs['TX'][64:128, tsl], op=ALU.mult),
                reads=[pr_a, tx, ("tmpf", 0)], writes=[("tmpf", 0), pr_a])
            P.add("dve", lambda e, ps_b=ps_b, tsl=tsl: e.tensor_tensor(
                out=tmpf[1][0:64, :], in0=ps_b[0:64, :], in1=tabs['TX'][0:64, tsl], op=ALU.mult),
                reads=[pr_b, tx], writes=[("tmpf", 1), pr_b])
            P.add("dve", lambda e, ps_b=ps_b, tsl=tsl: e.tensor_tensor(
                out=tmpf[1][64:128, :], in0=ps_b[64:128, :], in1=tabs['TQ'][64:128, tsl], op=ALU.mult),
                reads=[pr_b, tq, ("tmpf", 1)], writes=[("tmpf", 1), pr_b])
            P.add("dve", lambda e, tsl=tsl: e.tensor_tensor(
                out=kpeT[:, tsl], in0=tmpf[0], in1=tmpf[1], op=ALU.add),
                reads=[("tmpf", 0), ("tmpf", 1)], writes=[("kpeT", tg)])

        kv_mm(0)
        for tg in range(NTG):
            if tg + 1 < NTG:
                kv_mm(tg + 1)
            kv_norm(tg)

        stop_here("kv")
        for tg in range(NTG):
            rmsnorm_tg(None, KC, D, C_MIX1, lambda k, tg=tg: hnT[:, k, tg * 512:(tg + 1) * 512], tg,
                       [("hT", kc, tg) for kc in range(KC)], [("hnT", k, tg) for k in range(KC)])
        wqa, wqareg = W.get(w_q_a.rearrange("(k p) c -> p k c", p=128), [KC, QL])
        def cq_norm(tg):
            tsl = slice(tg * 512, (tg + 1) * 512)
            rb = raw2[tg % 2]
            rmsnorm_tg(lambda k: rb[:, k, :], 3, QL, C_QG,
                       lambda k: cqT[:, k, tsl], tg, [("raw", tg % 2, c) for c in range(3)],
                       [("cqT", k, tg) for k in range(3)])

        def cq_mm(tg):
            tsl = slice(tg * 512, (tg + 1) * 512)
            rb = raw2[tg % 2]
            raws = [rb[:, 0, :], rb[:, 1, :], rb[:, 2, :]]
            rregs = [("raw", tg % 2, c) for c in range(3)]
            for c3 in range(3):
                ps, psreg = ps_next()
                mm_group(ps, [(wqa[:, k, c3 * 128:(c3 + 1) * 128], hnT[:, k, tsl]) for k in range(KC)],
                         [("hnT", k, tg) for k in range(KC)] + [wqareg], psreg)
                P.add("act", lambda e, ps=ps, c3=c3, raws=raws: e.activation(out=raws[c3], in_=ps, func=AF.Copy),
                      reads=[psreg], writes=[rregs[c3], psreg])

        cq_mm(0)
        for tg in range(NTG):
            if tg + 1 < NTG:
                cq_mm(tg + 1)
            cq_norm(tg)
        P.barrier()
        arena.reset(m3)
        oT_all = arena.alloc([NH, T], BF16)
        kT_h = arena.alloc([T], BF16)
        v_h = arena.alloc([16, 128], BF16)
        qT_h = arena.alloc([T], BF16)
        qpe_h = arena.alloc([T], BF16)
        wq_cat = arena.alloc([3, 128], BF16)
        pT = [arena.alloc([512], BF16) for _ in range(4)]
        rcp = [arena.alloc([512], F32) for _ in range(2)]

        wqb_v = w_q_b.rearrange("(k p) c -> p k c", p=128)
        wkvb_v = kv_w_b.rearrange("(k p) c -> p k c", p=128)
        PROJ_BANKS = [7, 0, 1, 2]
        SC_BANKS = [0, 1, 2]
        cp_i = [0]

        def evac_copy(dst, ps, psreg, dreg):
            cp_i[0] += 1
            if cp_i[0] % 2 == 0:
                P.add("act", lambda e: e.activation(out=dst, in_=ps, func=AF.Copy),
                      reads=[psreg], writes=[dreg, psreg])
            else:
                P.add("dve", lambda e: e.tensor_copy(out=dst, in_=ps), reads=[psreg], writes=[dreg, psreg])

        for h in range(NH):
            wkb, wkbreg = W.get(wkvb_v[:, :, h * 256:(h + 1) * 256], [2, 256])
            wqb, wqbreg = W.get(wqb_v[:, :, h * 192:(h + 1) * 192], [3, 192])
            for j, (c_lo, src_lo, w) in enumerate([(0, 128, 64), (64, 160, 32), (96, 128, 32)]):
                P.add("dve", lambda e, wqb=wqb, c_lo=c_lo, src_lo=src_lo, w=w: e.tensor_copy(
                    out=wq_cat[:, :, c_lo:c_lo + w], in_=wqb[:, :, src_lo:src_lo + w]),
                    reads=[wqbreg], writes=[("wq_cat", j)])
            wqc_regs = [("wq_cat", j) for j in range(3)]
            for tg in range(NTG):
                tsl = slice(tg * 512, (tg + 1) * 512)
                ckr = [("ckvT", k, tg) for k in range(2)]
                cqr = [("cqT", k, tg) for k in range(3)]
                ps, psreg = ps_next(PROJ_BANKS)
                mm_group(ps, [(wkb[:, k, 0:128], ckvT[:, k, tsl]) for k in range(2)], ckr + [wkbreg], psreg)
                evac_copy(kT_h[:, tsl], ps, psreg, ("kT_h", tg))
                ps, psreg = ps_next(PROJ_BANKS)

                def fn(e, ps=ps, tg=tg, wkb=wkb):
                    ins = None
                    for j in range(4):
                        kt = tg * 4 + j
                        for k in range(2):
                            ins = e.matmul(ps[:, j * 128:(j + 1) * 128],
                                           lhsT=ckvT[:, k, kt * 128:(kt + 1) * 128],
                                           rhs=wkb[:, k, 128:256], start=(k == 0), stop=(k == 1))
                    return ins
                P.add("pe", fn, reads=ckr + [wkbreg], writes=[psreg])
                evac_copy(v_h[:, tg * 4:(tg + 1) * 4, :], ps.rearrange("p (a b) -> p a b", a=4), psreg,
                          ("v_h", tg))
                ps, psreg = ps_next(PROJ_BANKS)
                mm_group(ps, [(wqb[:, k, 0:128], cqT[:, k, tsl]) for k in range(3)], cqr + [wqbreg], psreg)
                evac_copy(qT_h[:, tsl], ps, psreg, ("qT_h", tg))
                ps_a, pr_a = ps_next(PROJ_BANKS)
                mm_group(ps_a, [(wq_cat[:, k, :], cqT[:, k, tsl]) for k in range(3)], cqr + wqc_regs, pr_a)
                P.add("dve", lambda e, ps_a=ps_a, tsl=tsl: e.tensor_tensor(
                    out=qpe_h[:, tsl], in0=ps_a, in1=tabs['TQ'][:, tsl], op=ALU.mult),
                    reads=[pr_a, ("TQ", tg)], writes=[("qpe_h", tg), pr_a])

            tiles = [(qg, kt) for qg in range(NTG) for kt in range(4 * qg + 4)]
            sc = {}

            def emit_qk(i):
                qg, kt = tiles[i]
                m = max(0, kt - 4 * qg)
                c0 = 128 * m
                b = SC_BANKS[i % 3]
                ps, psreg = psum[b], ("ps", b)
                qsl = slice(qg * 512 + c0, (qg + 1) * 512)
                ksl = slice(kt * 128, (kt + 1) * 128)

                def fn(e, ps=ps, c0=c0, qsl=qsl, ksl=ksl):
                    e.matmul(ps[:, c0:512], lhsT=kT_h[:, ksl], rhs=qT_h[:, qsl], start=True, stop=False)
                    return e.matmul(ps[:, c0:512], lhsT=kpeT[:, ksl], rhs=qpe_h[:, qsl],
                                    start=False, stop=True)
                P.add("pe", fn, reads=[("kT_h", kt // 4), ("qT_h", qg), ("kpeT", kt // 4), ("qpe_h", qg)],
                      writes=[psreg])
                sc[i] = (ps, psreg, c0)

            def emit_rest(i):
                qg, kt = tiles[i]
                ps, psreg, c0 = sc.pop(i)
                pb = pT[i % 4]
                preg = ("pT", i % 4)
                P.add("act", lambda e, ps=ps, pb=pb, c0=c0: e.activation(
                    out=pb[:, c0:512], in_=ps[:, c0:512], func=AF.Exp, scale=SCALE),
                    reads=[psreg], writes=[preg, psreg])
                if kt >= 4 * qg:
                    P.add("pool", lambda e, pb=pb, c0=c0: e.memset(pb[64:128, c0:c0 + 64], 0.0),
                          reads=[preg], writes=[preg])
                first = (kt == 0)
                last = (kt == 4 * qg + 3)
                po, poreg = psum[3 + qg % 2], ("ps", 3 + qg % 2)
                pd, pdreg = psum[5 + qg % 2], ("ps", 5 + qg % 2)

                def fn(e, pb=pb, c0=c0, kt=kt, po=po, pd=pd, first=first, last=last):
                    e.matmul(po[:, c0:512], lhsT=v_h[:, kt, :], rhs=pb[:, c0:512], start=first, stop=last)
                    return e.matmul(pd[:, c0:512], lhsT=ones_bf, rhs=pb[:, c0:512], start=first, stop=last)
                P.add("pe", fn, reads=[preg, ("v_h", kt // 4), "ones"], writes=[poreg, pdreg])
                if last:
                    r = rcp[qg % 2]
                    rreg = ("rcp", qg % 2)
                    P.add("dve", lambda e, r=r, pd=pd: e.reciprocal(out=r, in_=pd),
                          reads=[pdreg], writes=[rreg, pdreg])
                    P.add("dve", lambda e, r=r, po=po, qg=qg, h=h: e.tensor_tensor(
                        out=oT_all[:, h, qg * 512:(qg + 1) * 512], in0=po, in1=r, op=ALU.mult),
                        reads=[poreg, rreg], writes=[("oT", h, qg), poreg])

            n = len(tiles)
            emit_qk(0)
            emit_qk(1)
            for i in range(n):
                if i + 2 < n:
                    emit_qk(i + 2)
                emit_rest(i)

        wov = w_o.rearrange("(k p) c -> p k c", p=128)
        for dh in range(2):
            wob, wreg = W.get(wov[:, :, dh * 512:(dh + 1) * 512], [NH, 512])
            for d4 in range(4):
                dc = dh * 4 + d4
                for tg in range(NTG):
                    ps, psreg = ps_next()
                    mm_group(ps, [(wob[:, hh, d4 * 128:(d4 + 1) * 128], oT_all[:, hh, tg * 512:(tg + 1) * 512])
                                  for hh in range(NH)],
                             [("oT", hh, tg) for hh in range(NH)] + [wreg], psreg)
                    P.add("dve", lambda e, ps=ps, dc=dc, tg=tg: e.tensor_tensor(
                        out=hT[:, dc, tg * 512:(tg + 1) * 512], in0=ps,
                        in1=hT[:, dc, tg * 512:(tg + 1) * 512], op=ALU.add),
                        reads=[psreg, ("hT", dc, tg)], writes=[("hT", dc, tg), psreg])
        dump(2)
        stop_here("mla")
        P.barrier()
        arena.reset(base_mark)

        mlp(1, C_MLP1, final=True)
        dump(3)

    P0 = Prog(nc)
    W0 = emit_all(P0, None)
    plan = list(W0.reqs)
    P = Prog(nc)
    emit_all(P, plan)
    P.emit()
    return nc


_NC_CACHE = {}


def _pack_inputs(inputs):
    f32 = np.float32

    def colv(v):
        v = np.asarray(v, f32)
        return np.ascontiguousarray(v.reshape(-1, 128).T)

    pcol = np.zeros((128, NCOL), f32)
    pcol[:, C_MIX0:C_MIX0 + 8] = colv(inputs["norm_mix_g"][0])
    pcol[:, C_MLP0:C_MLP0 + 8] = colv(inputs["norm_mlp_g"][0])
    pcol[:, C_KVSRC:C_KVSRC + 8] = colv(inputs["kv_src_norm_g"])
    pcol[:, C_MIX1:C_MIX1 + 8] = colv(inputs["norm_mix_g"][1])
    pcol[:, C_MLP1:C_MLP1 + 8] = colv(inputs["norm_mlp_g"][1])
    pcol[:, C_FIN:C_FIN + 8] = colv(inputs["final_norm_g"])
    pcol[:, C_QG:C_QG + 3] = colv(inputs["b_q_norm_g"][0])
    pcol[:, C_KVAG:C_KVAG + 2] = colv(inputs["kv_a_norm_g"])
    pcol[:, C_LNG:C_LNG + 16] = colv(inputs["a_ln_v_g"][0])
    inv_freq = (np.float32(10000.0) ** (-np.arange(0, 64, 2, dtype=np.float32) / np.float32(64))).astype(f32)
    for q in range(4):
        pcol[32 * q:32 * (q + 1), C_FREQ] = inv_freq
        pcol[32 * q:32 * (q + 1), C_SIGN] = -1.0 if q % 2 == 0 else 1.0
    pcol[:, C_LNB:C_LNB + 16] = colv(inputs["a_ln_v_b"][0])
    bsb = np.ascontiguousarray(np.broadcast_to(np.asarray(inputs["a_b_s"][0], f32)[None], (128, 8, 128)))
    wsT = np.ascontiguousarray(np.transpose(np.asarray(inputs["a_w_s"][0], f32), (2, 0, 1)))
    shared = {
        "pcol": pcol, "bsb": bsb, "wsT": wsT,
        "a_w_in": np.ascontiguousarray(inputs["a_w_in"][0], dtype=f32),
        "a_w_out": np.ascontiguousarray(inputs["a_w_out"][0], dtype=f32),
        "b_w_q_a": np.ascontiguousarray(inputs["b_w_q_a"][0], dtype=f32),
        "b_w_q_b": np.ascontiguousarray(inputs["b_w_q_b"][0], dtype=f32),
        "b_w_o": np.ascontiguousarray(inputs["b_w_o"][0], dtype=f32),
        "kv_w_a": np.ascontiguousarray(inputs["kv_w_a"], dtype=f32),
        "kv_w_b": np.ascontiguousarray(inputs["kv_w_b"], dtype=f32),
        "mlp_w1": np.ascontiguousarray(inputs["mlp_w1"], dtype=f32),
        "mlp_w2": np.ascontiguousarray(inputs["mlp_w2"], dtype=f32),
    }
    return shared


def kernel(**inputs):
    x = np.asarray(inputs["x"], np.float32)
    pos = np.asarray(inputs["positions"], np.int32)
    B = x.shape[0]
    shared = _pack_inputs(inputs)
    in_maps = []
    for b in range(B):
        m = dict(shared)
        m["xT"] = np.ascontiguousarray(x[b].T)
        m["posb"] = np.ascontiguousarray(np.broadcast_to(pos[b][None, :], (128, T)))
        in_maps.append(m)
    if "nc" not in _NC_CACHE:
        _NC_CACHE["nc"] = build_program()
    res = run_bass_kernel_spmd(_NC_CACHE["nc"], in_maps, core_ids=list(range(B)))
    out = np.empty((B, T, D), np.float32)
    for b in range(B):
        out[b] = res.results[b]["outT"].T
    return out
```
